# Optimizing a Trainium2 kernel written in Bass

```python
import jax, jax.numpy as jnp
from jax import lax
import numpy as np

D_MODEL = 1024
BATCH = 4
SEQ = 8192
DEPTH = 1

RET_HEADS = D_MODEL // 256
RET_QK_DIM = D_MODEL
RET_V_DIM = 2 * D_MODEL
RET_HEAD_QK = RET_QK_DIM // RET_HEADS
RET_HEAD_V = RET_V_DIM // RET_HEADS
RET_CHUNK = 128
ROPE_BASE = 10000.0
GN_EPS = 1e-5
CONV_DIM = D_MODEL
CONV_WIDTH = 31
LN_EPS = 1e-5
RMS_EPS = 1e-6
IN_SPLITS = (RET_QK_DIM, RET_QK_DIM, RET_V_DIM, RET_V_DIM, CONV_DIM, CONV_DIM, CONV_DIM, D_MODEL, D_MODEL)
IN_WIDTH = sum(IN_SPLITS)

kernel_name = "hybrid_retention_conformer_gated_block"


def rmsnorm(x, g):
    xf = x.astype(jnp.float32)
    y = xf * lax.rsqrt(jnp.mean(xf * xf, axis=-1, keepdims=True) + RMS_EPS)
    return (y * g.astype(jnp.float32)).astype(x.dtype)


def layernorm(x, g, b):
    xf = x.astype(jnp.float32)
    mu = jnp.mean(xf, axis=-1, keepdims=True)
    var = jnp.mean(jnp.square(xf - mu), axis=-1, keepdims=True)
    y = (xf - mu) * lax.rsqrt(var + LN_EPS)
    return (y * g.astype(jnp.float32) + b.astype(jnp.float32)).astype(x.dtype)


def head_groupnorm(o):
    of = o.astype(jnp.float32)
    mu = jnp.mean(of, axis=-1, keepdims=True)
    var = jnp.mean(jnp.square(of - mu), axis=-1, keepdims=True)
    return (of - mu) * lax.rsqrt(var + GN_EPS)


def rotary(t, positions):
    dh = t.shape[-1]
    half = dh // 2
    inv_freq = ROPE_BASE ** (-jnp.arange(half, dtype=jnp.float32) / half)
    ang = positions.astype(jnp.float32)[..., None] * inv_freq
    cos = jnp.cos(ang)[:, :, None, :]
    sin = jnp.sin(ang)[:, :, None, :]
    t1, t2 = t[..., :half], t[..., half:]
    return jnp.concatenate([t1 * cos - t2 * sin, t1 * sin + t2 * cos], axis=-1)


def retention_chunkwise(q, k, v):
    B, S, H, dk = q.shape
    dv = v.shape[-1]
    C = RET_CHUNK
    N = S // C
    log_g = jnp.log1p(-jnp.exp2(-5.0 - jnp.arange(H, dtype=jnp.float32)))
    idx = jnp.arange(C, dtype=jnp.float32)
    diff = idx[:, None] - idx[None, :]
    causal = diff >= 0
    decay_mask = jnp.where(causal, jnp.exp(log_g[:, None, None] * jnp.where(causal, diff, 0.0)), 0.0)
    xi = jnp.exp(log_g[:, None] * (idx + 1.0))
    zeta = jnp.exp(log_g[:, None] * (C - 1.0 - idx))
    g_chunk = jnp.exp(log_g * C)

    def to_chunks(t):
        return t.reshape(B, N, C, H, t.shape[-1]).transpose(1, 0, 3, 2, 4)

    def step(state, inp):
        qc, kc, vc = inp
        scores = jnp.einsum('bhid,bhjd->bhij', qc, kc) * decay_mask[None]
        inner = jnp.einsum('bhij,bhje->bhie', scores, vc)
        cross = jnp.einsum('bhid,bhde->bhie', qc, state) * xi[None, :, :, None]
        new_state = state * g_chunk[None, :, None, None] + jnp.einsum(
            'bhjd,bhje->bhde', kc * zeta[None, :, :, None], vc)
        return new_state, inner + cross

    state0 = jnp.zeros((B, H, dk, dv), jnp.float32)
    _, o = lax.scan(step, state0, (to_chunks(q), to_chunks(k), to_chunks(v)))
    return o.transpose(1, 0, 3, 2, 4).reshape(B, S, H, dv)


def causal_depthwise_conv(u, w, b):
    out = lax.conv_general_dilated(
        u, w[:, None, :].astype(u.dtype), window_strides=(1,),
        padding=[(CONV_WIDTH - 1, 0)],
        dimension_numbers=('NWC', 'WIO', 'NWC'),
        feature_group_count=u.shape[-1])
    return out + b.astype(u.dtype)


def setup_inputs(seed: int = 0) -> dict:
    key = jax.random.key(seed)
    ks = jax.random.split(key, 16)
    f32 = jnp.float32
    x = jax.random.normal(ks[0], (BATCH, SEQ, D_MODEL), f32)
    c = jax.random.normal(ks[1], (BATCH, D_MODEL), f32)
    offsets = jax.random.randint(ks[2], (BATCH, 1), 0, 4096, dtype=jnp.int32)
    positions = offsets + jnp.arange(SEQ, dtype=jnp.int32)[None, :]
    w_ada = jax.random.normal(ks[3], (DEPTH, D_MODEL, 3 * D_MODEL), f32) * (0.1 * D_MODEL ** -0.5)
    b_ada = jax.random.normal(ks[4], (DEPTH, 3 * D_MODEL), f32) * 0.02
    pre_norm_g = 1.0 + 0.05 * jax.random.normal(ks[5], (DEPTH, D_MODEL), f32)
    w_in = jax.random.normal(ks[6], (DEPTH, D_MODEL, IN_WIDTH), f32) * D_MODEL ** -0.5
    conv_w = jax.random.normal(ks[7], (DEPTH, CONV_WIDTH, CONV_DIM), f32) * CONV_WIDTH ** -0.5
    conv_b = jax.random.normal(ks[8], (DEPTH, CONV_DIM), f32) * 0.02
    conv_ln_g = 1.0 + 0.05 * jax.random.normal(ks[9], (DEPTH, CONV_DIM), f32)
    conv_ln_b = jax.random.normal(ks[10], (DEPTH, CONV_DIM), f32) * 0.02
    w_ret_out = jax.random.normal(ks[11], (DEPTH, RET_V_DIM, D_MODEL), f32) * RET_V_DIM ** -0.5
    w_conv_out = jax.random.normal(ks[12], (DEPTH, CONV_DIM, D_MODEL), f32) * CONV_DIM ** -0.5
    w_out = jax.random.normal(ks[13], (DEPTH, D_MODEL, D_MODEL), f32) * D_MODEL ** -0.5
    post_norm_g = 1.0 + 0.05 * jax.random.normal(ks[14], (DEPTH, D_MODEL), f32)
    return {"x": x, "c": c, "positions": positions, "w_ada": w_ada, "b_ada": b_ada,
            "pre_norm_g": pre_norm_g, "w_in": w_in, "conv_w": conv_w, "conv_b": conv_b,
            "conv_ln_g": conv_ln_g, "conv_ln_b": conv_ln_b, "w_ret_out": w_ret_out,
            "w_conv_out": w_conv_out, "w_out": w_out, "post_norm_g": post_norm_g}


def reference(x, c, positions, w_ada, b_ada, pre_norm_g, w_in, conv_w, conv_b,
              conv_ln_g, conv_ln_b, w_ret_out, w_conv_out, w_out, post_norm_g):
    B, S, _ = x.shape
    split_at = [int(v) for v in np.cumsum(IN_SPLITS)[:-1]]
    for l in range(DEPTH):
        mod = c @ w_ada[l] + b_ada[l]
        shift, scale, gate = jnp.split(mod, 3, axis=-1)
        h = rmsnorm(x, pre_norm_g[l]) * (1.0 + scale[:, None, :]) + shift[:, None, :]

        proj = h @ w_in[l]
        q, k, v, z_ret, u_val, u_gate, z_conv, g_a, g_b = jnp.split(proj, split_at, axis=-1)

        qh = rotary(q.reshape(B, S, RET_HEADS, RET_HEAD_QK).astype(jnp.float32), positions)
        kh = rotary(k.reshape(B, S, RET_HEADS, RET_HEAD_QK).astype(jnp.float32), positions) * RET_HEAD_QK ** -0.5
        vh = v.reshape(B, S, RET_HEADS, RET_HEAD_V).astype(jnp.float32)
        ret = head_groupnorm(retention_chunkwise(qh, kh, vh)).reshape(B, S, RET_V_DIM).astype(x.dtype)
        y_a = (ret * jax.nn.silu(z_ret)) @ w_ret_out[l]

        a = u_val * jax.nn.sigmoid(u_gate)
        a = causal_depthwise_conv(a, conv_w[l], conv_b[l])
        a = jax.nn.silu(layernorm(a, conv_ln_g[l], conv_ln_b[l]))
        y_b = (a * jax.nn.silu(z_conv)) @ w_conv_out[l]

        merged = jax.nn.sigmoid(g_a) * y_a + jax.nn.sigmoid(g_b) * y_b
        y = merged @ w_out[l]

        x = x + gate[:, None, :] * rmsnorm(y, post_norm_g[l])
    return x
```

```python
import math
from contextlib import ExitStack

import numpy as np
import concourse.bass as bass
import concourse.mybir as mybir
from concourse.bass_utils import run_bass_kernel_spmd

F32 = mybir.dt.float32
BF16 = mybir.dt.bfloat16
I32 = mybir.dt.int32
AF = mybir.ActivationFunctionType
ALU = mybir.AluOpType

D = 1024
T = 512
NT = 4
KC = 8
H = 4
CONVW = 31
HALO = CONVW - 1
IN_W = 11264
RMS_EPS = 1e-6
GN_EPS = 1e-5
LN_EPS = 1e-5
TWO_PI = 2.0 * math.pi
C1 = 6.28125
C2 = TWO_PI - C1
PI_LO = 3.1415925
RSQRT_MAGIC = 0x5F3759DF
NEWTON_ITERS = 3

OFF_Q, OFF_K, OFF_V, OFF_Z, OFF_UV, OFF_UG, OFF_ZC, OFF_GA, OFF_GB = (
    0, 1024, 2048, 4096, 6144, 7168, 8192, 9216, 10240)


class Buf:
    __slots__ = ("name", "w", "r")

    def __init__(self, name=""):
        self.name = name
        self.w = None
        self.r = []


class Op:
    __slots__ = ("eng", "fn", "deps", "dma", "sig", "sem", "val", "idx")


class Sched:
    ENGS = ("pe", "act", "dve", "pool", "sp")

    def __init__(self, nc, n_dma_sems=32):
        self.nc = nc
        self.ops = []
        self.n_dma_sems = n_dma_sems

    dry = False

    def op(self, eng, fn, reads=(), writes=(), dma=False, **kw):
        if self.dry:
            return None
        o = Op()
        o.eng = eng
        o.fn = (fn, kw)
        o.dma = dma
        o.sig = False
        o.sem = None
        o.val = 0
        o.idx = len(self.ops)
        deps = set()
        for b in reads:
            if b.w is not None:
                deps.add(b.w)
        for b in writes:
            if b.w is not None:
                deps.add(b.w)
            for r in b.r:
                deps.add(r)
        deps.discard(o)
        o.deps = deps
        for b in reads:
            b.r.append(o)
        for b in writes:
            b.w = o
            b.r = []
        self.ops.append(o)
        return o

    def emit(self, stack):
        nc = self.nc
        engh = {"pe": nc.tensor, "act": nc.scalar, "dve": nc.vector, "pool": nc.gpsimd, "sp": nc.sync}
        esem = {e: stack.enter_context(nc.semaphore("s_" + e)) for e in self.ENGS}
        dsem = [stack.enter_context(nc.semaphore("d%d" % i)) for i in range(self.n_dma_sems)]
        for o in self.ops:
            for d in o.deps:
                if d.dma:
                    continue
                if d.eng == o.eng and o.eng in ("pe", "sp") and not o.dma:
                    continue
                d.sig = True
        cnt = {e: 0 for e in self.ENGS}
        NSW = 8
        dsem_sw = [stack.enter_context(nc.semaphore("w%d" % i)) for i in range(NSW)]
        pools = {"sp": (dsem, [0] * self.n_dma_sems, [None] * self.n_dma_sems, [0]),
                 "pool": (dsem_sw, [0] * NSW, [None] * NSW, [0])}
        for o in self.ops:
            if o.dma:
                sems, dcnt_, dlast_, nd_ = pools["pool" if o.eng == "pool" else "sp"]
                k = nd_[0] % len(sems)
                nd_[0] += 1
                if dlast_[k] is not None:
                    o.deps.add(dlast_[k])
                dlast_[k] = o
                dcnt_[k] += 16
                o.sem = sems[k]
                o.val = dcnt_[k]
                o.sig = True
            elif o.sig:
                cnt[o.eng] += 1
                o.sem = esem[o.eng]
                o.val = cnt[o.eng]
        waited = {}
        nwait = 0
        for o in self.ops:
            e = engh[o.eng]
            need = {}
            for d in o.deps:
                if not d.dma and d.eng == o.eng and o.eng in ("pe", "sp") and not o.dma:
                    continue
                if not d.dma and d.eng == o.eng and o.eng == "sp":
                    continue
                key = id(d.sem)
                if key not in need or need[key][1] < d.val:
                    need[key] = (d.sem, d.val)
            pend = []
            for key, (sem, val) in need.items():
                wk = (o.eng, key)
                if waited.get(wk, 0) >= val:
                    continue
                waited[wk] = val
                pend.append((sem, val))
            for sem, val in pend[:-1]:
                e.wait_ge(sem, val)
                nwait += 1
            ins = getattr(e, o.fn[0])(**o.fn[1])
            if pend:
                ins._wait_ge(pend[-1][0], pend[-1][1])
            if o.sig:
                ins.then_inc(o.sem, 16 if o.dma else 1)
        for sems, dcnt_, _, _ in pools.values():
            for k in range(len(sems)):
                if dcnt_[k] > 0:
                    nc.sync.wait_ge(sems[k], dcnt_[k])
        return len(self.ops), nwait


def _host_consts():
    hh = np.arange(H, dtype=np.float64)
    g = 1.0 - np.exp2(-5.0 - hh)
    j = np.arange(128, dtype=np.float64)
    ginv = g[None, :] ** (-(j[:, None] + 1.0))
    causal = (j[:, None] <= j[None, :]).astype(np.float64)
    maskT = (ginv[:, :, None] * causal[:, None, :] / 16.0).astype(np.float32)
    zs = (g[None, :] ** (127.0 - j[:, None]) / 16.0).astype(np.float32)
    xi = g[None, :] ** (j[:, None] + 1.0)
    epsp = (GN_EPS / (xi * xi)).astype(np.float32)
    gC = [float(np.float32(gg ** 128.0)) for gg in g]
    half = 128
    inv_freq = (np.float32(10000.0) ** (-(np.arange(half, dtype=np.float32) / np.float32(half)))).astype(np.float32)
    return maskT, zs, epsp, gC, inv_freq.reshape(128, 1)


def build_program(NBLK):
    NTOK = NBLK * T
    nc = bass.Bass("TRN2", target_bir_lowering=False)
    maskT_np, zs_np, epsp_np, gC, _ = _host_consts()

    def din(name, shape, dt=F32):
        return nc.dram_tensor(name, list(shape), dt, kind="ExternalInput").ap()

    x_own = din("x_own", [NTOK, D])
    x_prev = din("x_prev", [NTOK, D])
    pos_own = din("pos_own", [1, NTOK], I32)
    pos_prev = din("pos_prev", [1, NTOK], I32)
    flag_d = din("flag", [128, 1])
    cT_d = din("cT", [128, KC])
    w_ada = din("w_ada", [D, 3 * D])
    b_adaT_d = din("b_adaT", [128, 16])
    b_gate_d = din("b_gate", [1, D])
    pre_gT_d = din("pre_gT", [128, KC])
    w_in = din("w_in", [D, IN_W])
    conv_wT_d = din("conv_wT", [128, KC * CONVW])
    conv_bT_d = din("conv_bT", [128, KC])
    ln_gT_d = din("ln_gT", [128, KC])
    ln_bT_d = din("ln_bT", [128, KC])
    w_ro = din("w_ret_out", [2 * D, D])
    w_co = din("w_conv_out", [D, D])
    w_o = din("w_out", [D, D])
    post_g_d = din("post_g", [1, D])
    ident_d = din("ident", [128, 128])
    maskT_d = din("maskT", [128, H * 128])
    zs_d = din("zs", [128, H])
    epsp_d = din("epsp", [128, H])
    invf_d = din("inv_freq", [128, 1])
    out_d = nc.dram_tensor("out", [NTOK, D], F32, kind="ExternalOutput").ap()

    units = {}
    ulist = []

    def add_unit(name):
        units[name] = len(ulist)
        ulist.append(name)

    for h in range(H):
        for nm in ("Q%d", "K%d", "V%d_0", "V%d_1", "Z%d_0", "Z%d_1"):
            add_unit(nm % h)
    for nm in ("UV", "UG", "ZC", "GA", "GB"):
        for i in range(4):
            add_unit("%s%d" % (nm, i))
    for i in range(8):
        add_unit("RO%d" % i)
    for i in range(4):
        add_unit("CO%d" % i)
    for i in range(4):
        add_unit("WO%d" % i)
    for c in range(KC):
        add_unit("DG%d" % c)
    NU = len(ulist)
    wsc = nc.dram_tensor("wsc", [NU, 128, 2048], BF16).ap()
    B_wsc = [Buf("wsc%d" % i) for i in range(NU)]

    S = Sched(nc)
    sb = nc.alloc_sbuf_tensor
    ps = nc.alloc_psum_tensor

    def V(m, reads, writes, **kw):
        S.op("dve", m, reads, writes, **kw)

    def A(m, reads, writes, **kw):
        S.op("act", m, reads, writes, **kw)

    def G(m, reads, writes, **kw):
        S.op("pool", m, reads, writes, **kw)

    def PE(m, reads, writes, **kw):
        S.op("pe", m, reads, writes, **kw)

    def DMA(reads, writes, **kw):
        S.op("sp", "dma_start", reads, writes, dma=True, **kw)

    def DMA2(reads, writes, **kw):
        S.op("pool", "dma_start", reads, writes, dma=True, **kw)

    ident_f = sb("ident_f", [128, 128], F32)
    ident_b = sb("ident_b", [128, 128], BF16)
    ones_b = sb("ones_b", [128, 128], BF16)
    maskT = sb("maskT_sb", [128, H, 128], F32)
    zs = sb("zs_sb", [128, H], F32)
    epsp = sb("epsp_sb", [128, H], F32)
    invf = sb("invf_sb", [128, 1], F32)
    flag = sb("flagt", [128, 1], F32)
    cT = sb("cTt", [128, KC], F32)
    cT_b = sb("cT_b", [128, KC], BF16)
    cbc = sb("cbc", [128, KC, 128], BF16)
    b_adaT = sb("b_adaTt", [128, 16], F32)
    pre_gT = sb("pre_gTt", [128, KC], F32)
    gs = sb("gs", [128, KC], F32)
    shiftT = sb("shiftT", [128, KC], F32)
    conv_w05 = sb("conv_w05", [128, KC, CONVW], F32)
    conv_bT = sb("conv_bTt", [128, KC], F32)
    ln_gT = sb("ln_gTt", [128, KC], F32)
    ln_bT = sb("ln_bTt", [128, KC], F32)
    pgg = sb("pgg", [128, D], F32)
    xr = [sb("xr%d" % i, [128, D], F32) for i in range(2)]
    xr_b = [Buf("xr%d" % i) for i in range(2)]
    bgate, bgate_b = xr[1], xr_b[1]
    B_const = Buf("const")
    B_c2 = Buf("const2")

    def ld_const(dst_ap, src_ap, buf=B_const):
        DMA([], [buf], out=dst_ap, in_=src_ap)

    ld_const(ident_f[:], ident_d[:, :])
    ld_const(maskT[:].rearrange("p h i -> p (h i)"), maskT_d[:, :])
    ld_const(zs[:], zs_d[:, :])
    ld_const(epsp[:], epsp_d[:, :])
    ld_const(invf[:], invf_d[:, :])
    ld_const(flag[:], flag_d[:, :])
    ld_const(cT[:], cT_d[:, :])
    ld_const(b_adaT[:], b_adaT_d[:, :])
    ld_const(pre_gT[:], pre_gT_d[:, :])
    ld_const(conv_w05[:].rearrange("p c k -> p (c k)"), conv_wT_d[:, :])
    ld_const(conv_bT[:], conv_bT_d[:, :])
    ld_const(ln_gT[:], ln_gT_d[:, :])
    ld_const(ln_bT[:], ln_bT_d[:, :])
    ld_const(pgg[:], post_g_d[0:1, :].to_broadcast([128, D]))
    ld_const(bgate[:], b_gate_d[0:1, :].to_broadcast([128, D]), bgate_b)
    V("tensor_copy", [B_const], [B_c2], out=ident_b[:], in_=ident_f[:])
    V("memset", [], [B_c2], ap=ones_b[:], constant=1.0)
    V("tensor_copy", [B_const], [B_c2], out=cT_b[:], in_=cT[:])
    for kc in range(KC):
        V("tensor_copy", [B_const], [B_c2], out=cbc[:, kc, :], in_=cT[:, kc:kc + 1].to_broadcast([128, 128]))
    V("tensor_scalar", [B_const], [B_const], out=conv_w05[:], in0=conv_w05[:], scalar1=0.5, scalar2=None, op0=ALU.mult)

    pbank = [ps("pb%d" % i, [128, 512], F32) for i in range(6)]
    pbuf = [Buf("pb%d" % i) for i in range(6)]
    ptr = [ps("ptr%d" % i, [128, 1024], BF16) for i in range(2)]
    ptrbuf = [Buf("ptr0"), Buf("ptr1")]
    rr = {}

    def nxt(key, n):
        i = rr.get(key, 0)
        rr[key] = i + 1
        return i % n

    def pring():
        i = nxt("pring", 4)
        return pbank[i], pbuf[i]

    def ptring():
        i = nxt("ptring", 2)
        return ptr[i][:, 0:512], ptrbuf[i]

    stg_f = [sb("stg_f%d" % i, [128, 4096], F32) for i in range(2)]
    stg_fb = [Buf("stg_f%d" % i) for i in range(2)]
    stg_b0 = sb("stg_b0", [128, 4096], BF16)
    rT = sb("rT", [128, 16, T], BF16)
    stg_bap = [stg_b0[:], rT[:, 0:8, :].rearrange("p a b -> p (a b)")]
    stg_bb = [Buf("stg_b%d" % i) for i in range(2)]
    vvbuf = sb("vvbuf", [128, 4096], BF16)

    NSLOT = 5
    wslot = [sb("wslot%d" % i, [128, 2048], BF16) for i in range(NSLOT)]
    wslot_b = [Buf("wslot%d" % i) for i in range(NSLOT)]

    cast_mode = {"pool_only": False}

    def cast_op(dst, src, reads, writes):
        k = 1 if cast_mode["pool_only"] else nxt("cast", 3)
        if k == 0:
            V("tensor_copy", reads, writes, out=dst, in_=src)
        elif k == 1:
            A("activation", reads, writes, out=dst, in_=src, func=AF.Copy)
        else:
            G("tensor_copy", reads, writes, out=dst, in_=src)

    def stage_load_issue(src_ap3, nA):
        i = nxt("stg", 2)
        sf, sfb = stg_f[i], stg_fb[i]
        sf3 = sf[:].rearrange("p (a b) -> p a b", a=nA)
        (DMA2 if cast_mode["pool_only"] else DMA)([], [sfb], out=sf3, in_=src_ap3)
        return i

    def stage_load_cast(i, nA):
        sf, sfb, sbap, sbb = stg_f[i], stg_fb[i], stg_bap[i], stg_bb[i]
        sb3 = sbap.rearrange("p (a b) -> p a b", a=nA)
        cast_op(sbap, sf[:], [sfb], [sbb])
        return sb3, sbb

    def stage_load(src_ap3, nA):
        i = stage_load_issue(src_ap3, nA)
        return stage_load_cast(i, nA)

    def convert_finish(i, nA, nB, unit_names):
        sb3, sbb = stage_load_cast(i, nA)
        hb = nB // 2
        for q, un in enumerate(unit_names):
            dst = wsc[units[un]].rearrange("p (a b) -> p a b", a=nA)
            (DMA2 if cast_mode["pool_only"] else DMA)([sbb], [B_wsc[units[un]]], out=dst, in_=sb3[:, :, q * hb:(q + 1) * hb])

    def convert(src_ap3, nA, nB, unit_names):
        i = stage_load_issue(src_ap3, nA)
        convert_finish(i, nA, nB, unit_names)

    def kview(w, c0, n):
        return w[:, c0:c0 + n].rearrange("(kc p) n -> p kc n", p=128)

    def conv_win(nm, off, i):
        convert(kview(w_in, off + i * 512, 512), KC, 512, ["%s%d" % (nm, 2 * i), "%s%d" % (nm, 2 * i + 1)])

    def gen_convert_first():
        for hp in range(2):
            conv_win("K", OFF_K, hp)
            yield
        for h in range(3):
            convert(kview(w_in, OFF_V + h * 512, 512), KC, 512, ["V%d_0" % h, "V%d_1" % h])
            yield

    def gen_diag_units():
        for c in range(KC):
            i = nxt("stg", 2)
            sbap, sbb = stg_bap[i], stg_bb[i]
            d3 = sbap[:, 0:2048].rearrange("p (a b) -> p a b", a=16)
            for t in range(16):
                A("activation", [B_const], [sbb], out=d3[:, t, :], in_=ident_f[:], func=AF.Copy,
                  scale=conv_w05[:, c, 2 * t:2 * t + 1])
            DMA2([sbb], [B_wsc[units["DG%d" % c]]], out=wsc[units["DG%d" % c]], in_=sbap[:, 0:2048])
            yield

    def rest_specs():
        sp = [(kview(w_in, OFF_V + 3 * 512, 512), KC, 512, ["V3_0", "V3_1"])]
        for i in range(2):
            for nm, off in (("UG", OFF_UG), ("UV", OFF_UV)):
                sp.append((kview(w_in, off + i * 512, 512), KC, 512, ["%s%d" % (nm, 2 * i), "%s%d" % (nm, 2 * i + 1)]))
        for i in range(2):
            sp.append((kview(w_in, OFF_ZC + i * 512, 512), KC, 512, ["ZC%d" % (2 * i), "ZC%d" % (2 * i + 1)]))
        for hp in range(2):
            sp.append((kview(w_in, OFF_Q + hp * 512, 512), KC, 512, ["Q%d" % (2 * hp), "Q%d" % (2 * hp + 1)]))
        for h in range(H):
            sp.append((kview(w_in, OFF_Z + h * 512, 512), KC, 512, ["Z%d_0" % h, "Z%d_1" % h]))
        for nm, off in (("GA", OFF_GA), ("GB", OFF_GB)):
            for i in range(2):
                sp.append((kview(w_in, off + i * 512, 512), KC, 512, ["%s%d" % (nm, 2 * i), "%s%d" % (nm, 2 * i + 1)]))
        for i in range(2):
            sp.append((kview(w_co, i * 512, 512), KC, 512, ["CO%d" % (2 * i), "CO%d" % (2 * i + 1)]))
        for i in range(4):
            sp.append((kview(w_ro, i * 256, 256), 16, 256, ["RO%d" % (2 * i), "RO%d" % (2 * i + 1)]))
        for i in range(2):
            sp.append((kview(w_o, i * 512, 512), KC, 512, ["WO%d" % (2 * i), "WO%d" % (2 * i + 1)]))
        return sp

    def gen_convert_rest():
        sp = rest_specs()
        slots = [stage_load_issue(sp[0][0], sp[0][1])]
        for i in range(len(sp)):
            if i + 1 < len(sp):
                slots.append(stage_load_issue(sp[i + 1][0], sp[i + 1][1]))
            convert_finish(slots[i], sp[i][1], sp[i][2], sp[i][3])
            yield

    pmod, pmodb = pbank[4], pbuf[4]
    for blk in range(4):
        sb3, sbb = stage_load(kview(w_ada, blk * 512, 512), KC)
        for jj in range(4):
            j = blk * 4 + jj
            for kc in range(KC):
                PE("matmul", [sbb, B_c2], [pmodb], out=pmod[:, j:j + 1], lhsT=sb3[:, kc, jj * 128:(jj + 1) * 128],
                   rhs=cT_b[:, kc:kc + 1], start=(kc == 0), stop=(kc == KC - 1))
    V("tensor_tensor", [pmodb, B_const], [B_c2], out=shiftT[:], in0=pmod[:, 0:8], in1=b_adaT[:, 0:8], op=ALU.add)
    V("tensor_tensor", [pmodb, B_const], [B_c2], out=gs[:], in0=pmod[:, 8:16], in1=b_adaT[:, 8:16], op=ALU.add)
    V("scalar_tensor_tensor", [B_c2, B_const], [B_c2], out=gs[:], in0=gs[:], scalar=1.0, in1=pre_gT[:],
      op0=ALU.add, op1=ALU.mult)
    def gen_gate_row():
        for blk in range(2):
            sb3, sbb = stage_load(kview(w_ada, 2048 + blk * 512, 512), KC)
            pg, pgb = pring()
            for kc in range(KC):
                PE("matmul", [sbb, B_c2], [pgb], out=pg[:], lhsT=cbc[:, kc, :], rhs=sb3[:, kc, :],
                   start=(kc == 0), stop=(kc == KC - 1))
            sl = slice(blk * 512, (blk + 1) * 512)
            V("tensor_tensor", [pgb, bgate_b], [bgate_b], out=bgate[:, sl], in0=pg[:], in1=bgate[:, sl], op=ALU.add)
            V("tensor_tensor", [bgate_b, B_const], [B_pgg], out=pgg[:, sl], in0=pgg[:, sl], in1=bgate[:, sl], op=ALU.mult)
            yield

    B_pgg = Buf("pgg")

    ac = stg_f[0]
    ac_cb = [Buf("ac%d" % c) for c in range(KC)]
    lnt = stg_f[1]
    lnt_b = Buf("lnt")
    lns = [lnt[:, (6 + i) * T:(7 + i) * T] for i in range(2)]
    lns_b = [Buf("lns%d" % i) for i in range(2)]
    szT3 = [stg_b0[:, i * 2048:(i + 1) * 2048].rearrange("p (a b) -> p a b", a=4) for i in range(2)]
    szT_b = [Buf("szT0"), Buf("szT1")]
    vv = [vvbuf[:, i * 2048:(i + 1) * 2048].rearrange("p (n e) -> p n e", n=NT) for i in range(2)]
    vv_b = [[Buf("vv%d_%d" % (i, n)) for n in range(NT)] for i in range(2)]
    rT_b = [Buf("rT%d" % i) for i in range(H)]
    bT = sb("bT", [128, KC, T], BF16)
    bT_b = [Buf("bT%d" % c) for c in range(KC)]
    hT = sb("hT", [128, KC, T], BF16)
    hT_b = [Buf("hT%d" % c) for c in range(KC)]
    aT = sb("aT", [128, KC, HALO + T], BF16)
    aT_b = [Buf("aT%d" % c) for c in range(KC)]
    mT3 = aT[:, :, HALO:HALO + T]
    NXS = 2
    xs = [sb("xs%d" % i, [128, D], F32) for i in range(NXS)]
    xs_b = [Buf("xs%d" % i) for i in range(NXS)]
    xn = [sb("xn%d" % i, [128, D], BF16) for i in range(4)]
    xn_b = [Buf("xn%d" % i) for i in range(4)]
    st_f = sb("st_f", [128, H * 2, 512], F32)
    st_bf = sb("st_bf", [128, H * 2, 512], BF16)
    st_fb = [Buf("stf%d" % i) for i in range(H * 2)]
    st_bb = [Buf("stb%d" % i) for i in range(H * 2)]
    cos_t = sb("cos_t", [128, T], F32)
    sin_t = sb("sin_t", [128, T], F32)
    tab_b = Buf("tab")
    rtmp = [sb("rtmp%d" % i, [128, T], F32) for i in range(4)]
    rtmp_b = [Buf("rtmp%d" % i) for i in range(4)]
    qT = [sb("qT%d" % i, [128, 2, T], BF16) for i in range(2)]
    kT = [sb("kT%d" % i, [128, 2, T], BF16) for i in range(2)]
    kz = [sb("kz%d" % i, [128, NT, 256], BF16) for i in range(2)]
    qT_b = [[Buf("qT%d_%d" % (i, d)) for d in range(2)] for i in range(2)]
    kT_b = [[Buf("kT%d_%d" % (i, d)) for d in range(2)] for i in range(2)]
    kz_b = [[Buf("kz%d_%d" % (i, n)) for n in range(NT)] for i in range(2)]
    PT = [sb("PT%d" % i, [128, 128], BF16) for i in range(4)]
    PT_b = [Buf("PT%d" % i) for i in range(4)]
    on = [sb("on%d" % i, [128, 512], BF16) for i in range(2)]
    on_b = [Buf("on%d" % i) for i in range(2)]
    sm = [sb("sm%d" % i, [128, 16], F32) for i in range(4)]
    smi = [sb("smi%d" % i, [128, 2], I32) for i in range(4)]
    sm_b = [Buf("sm%d" % i) for i in range(4)]
    NDIAG = 30
    diag = [sb("diag%d" % i, [128, 128], BF16) for i in range(NDIAG)]
    diag_b = [Buf("diag%d" % i) for i in range(NDIAG)]
    acb16 = [sb("acb%d" % i, [128, T], BF16) for i in range(2)]
    sqb16 = [sb("sqb%d" % i, [128, T], BF16) for i in range(2)]
    acb16_b = [Buf("acb%d" % i) for i in range(2)]
    sqb16_b = [Buf("sqb%d" % i) for i in range(2)]
    szc = [sb("szc%d" % i, [128, T], BF16) for i in range(2)]
    szc_b = [Buf("szc%d" % i) for i in range(2)]
    print("sbuf bytes remaining/partition:", nc.sbuf_bytes_remaining)

    bar_t = sb("bar_t", [128, 1], F32)

    def staging_barrier():
        bar_w = ac_cb + [lnt_b] + szT_b + lns_b + rT_b
        V("memset", [], stg_fb + stg_bb + bar_w, ap=bar_t[:], constant=0.0)

    def passA_units(last):
        s = []
        for h in range(H):
            s += ["K%d" % h, "V%d_0" % h, "V%d_1" % h]
        if last:
            for i in range(4):
                s += ["UG%d" % i, "UV%d" % i]
        return s

    def main_units():
        s = []
        for i in range(4):
            s += ["UG%d" % i, "UV%d" % i]
        s += ["ZC%d" % i for i in range(4)]
        for h in range(H):
            s += ["Q%d" % h, "K%d" % h, "V%d_0" % h, "V%d_1" % h, "Z%d_0" % h, "Z%d_1" % h]
        for i in range(4):
            s += ["GA%d" % i, "GB%d" % i, "CO%d" % i, "RO%d" % (2 * i), "RO%d" % (2 * i + 1)]
        s += ["WO%d" % i for i in range(4)] * 2
        return s

    useq = []
    wst = {"cons": 0, "loaded": 0}
    LOOK = NSLOT - 1
    slot_free = list(range(NSLOT))
    slot_of = {}
    held = {}

    def _load(j):
        sl = slot_free.pop(0)
        slot_of[j] = sl
        DMA([B_wsc[units[useq[j]]]], [wslot_b[sl]], out=wslot[sl][:], in_=wsc[units[useq[j]]])

    def _prefetch():
        while wst["loaded"] < len(useq) and slot_free and wst["loaded"] <= wst["cons"] + LOOK:
            _load(wst["loaded"])
            wst["loaded"] += 1

    def release(stream):
        if S.dry:
            return
        j = held.pop(stream, None)
        if j is not None:
            slot_free.append(slot_of.pop(j))
            _prefetch()

    def acquire(name, stream="M"):
        if S.dry:
            useq.append(name)
            return wslot[0], wslot_b[0]
        release(stream)
        i = wst["cons"]
        assert useq[i] == name, (i, useq[i], name)
        if i >= wst["loaded"]:
            assert wst["loaded"] == i and slot_free
            _load(i)
            wst["loaded"] += 1
        wst["cons"] += 1
        held[stream] = i
        _prefetch()
        sl = slot_of[i]
        return wslot[sl], wslot_b[sl]

    xseq = [("prev", t) for t in range(NBLK * NT)] + [("own", t) for t in range(NBLK * NT)]
    xst = {"cons": 0, "loaded": 0}

    def acquire_x():
        i = xst["cons"]
        xst["cons"] += 1
        while xst["loaded"] < len(xseq) and xst["loaded"] <= i + (NXS - 1):
            j = xst["loaded"]
            which, t = xseq[j]
            src = (x_prev if which == "prev" else x_own)[t * 128:(t + 1) * 128, :]
            DMA([], [xs_b[j % NXS]], out=xs[j % NXS][:], in_=src)
            xst["loaded"] += 1
        return xs[i % NXS], xs_b[i % NXS]

    def rsqrt(a_ap, y_ap, t_ap, i_ap, bufs, scalar=True):
        V("tensor_scalar", bufs, bufs, out=i_ap, in0=a_ap.bitcast(I32), scalar1=1, scalar2=None, op0=ALU.arith_shift_right)
        V("tensor_scalar", bufs, bufs, out=i_ap, in0=i_ap, scalar1=-1, scalar2=RSQRT_MAGIC, op0=ALU.mult, op1=ALU.add)
        cur = i_ap.bitcast(F32)
        for it in range(NEWTON_ITERS):
            if scalar:
                V("scalar_tensor_tensor", bufs, bufs, out=t_ap, in0=cur, scalar=a_ap, in1=cur, op0=ALU.mult, op1=ALU.mult)
            else:
                V("tensor_tensor", bufs, bufs, out=t_ap, in0=cur, in1=cur, op=ALU.mult)
                V("tensor_tensor", bufs, bufs, out=t_ap, in0=t_ap, in1=a_ap, op=ALU.mult)
            V("tensor_scalar", bufs, bufs, out=t_ap, in0=t_ap, scalar1=-0.5, scalar2=1.5, op0=ALU.mult, op1=ALU.add)
            V("tensor_tensor", bufs, bufs, out=y_ap, in0=cur, in1=t_ap, op=ALU.mult)
            cur = y_ap

    def pre_elem():
        for n in range(NT):
            xt, xtb = acquire_x()
            si = nxt("sm", 4)
            smt, smb = sm[si], sm_b[si]
            A("activation", [xtb], [xn_b[n], smb], out=xn[n][:], in_=xt[:], func=AF.Square, accum_out=smt[:, 0:1])
            V("tensor_scalar", [smb], [smb], out=smt[:, 1:2], in0=smt[:, 0:1], scalar1=1.0 / D, scalar2=RMS_EPS,
              op0=ALU.mult, op1=ALU.add)
            rsqrt(smt[:, 1:2], smt[:, 2:3], smt[:, 3:4], smi[si][:, 0:1], [smb])
            A("activation", [xtb, smb], [xn_b[n]], out=xn[n][:], in_=xt[:], func=AF.Copy, scale=smt[:, 2:3])

    def pre_pe():
        for half in range(2):
            tl = [half * 2, half * 2 + 1]
            for kc2 in range(KC // 2):
                pt, ptb = ptring()
                for q in range(2):
                    kc = kc2 * 2 + q
                    for n2 in range(2):
                        PE("transpose", [xn_b[tl[n2]], B_c2], [ptb],
                           out=pt[:, (q * 2 + n2) * 128:(q * 2 + n2 + 1) * 128],
                           in_=xn[tl[n2]][:, kc * 128:(kc + 1) * 128], identity=ident_b[:])
                for q in range(2):
                    kc = kc2 * 2 + q
                    A("activation", [ptb, B_c2], [hT_b[kc]], out=hT[:, kc, half * 256:(half + 1) * 256],
                      in_=pt[:, q * 256:(q + 1) * 256], func=AF.Identity, bias=shiftT[:, kc:kc + 1], scale=gs[:, kc:kc + 1])

    def stage_tables(pos_ap, t0):
        tA, tA_b = rtmp[0], rtmp_b[0]
        tB, tB_b = rtmp[1], rtmp_b[1]
        posi, posi_b = rtmp[2][:].bitcast(I32), rtmp_b[2]
        DMA([], [posi_b], out=posi, in_=pos_ap[0:1, t0:t0 + T].to_broadcast([128, T]))
        G("tensor_copy", [posi_b], [tA_b], out=tA[:], in_=posi)
        G("tensor_scalar", [tA_b, B_const], [tA_b], out=tA[:], in0=tA[:], scalar1=invf[:, 0:1], scalar2=None, op0=ALU.mult)
        G("tensor_scalar", [tA_b], [posi_b], out=posi, in0=tA[:], scalar1=1.0 / TWO_PI, scalar2=None, op0=ALU.mult)
        G("tensor_copy", [posi_b], [tB_b], out=tB[:], in_=posi)
        tC, tC_b = rtmp[3], rtmp_b[3]
        G("tensor_scalar", [tB_b], [tC_b], out=tC[:], in0=tB[:], scalar1=-C1, scalar2=None, op0=ALU.mult)
        G("tensor_tensor", [tA_b, tC_b], [tA_b], out=tA[:], in0=tA[:], in1=tC[:], op=ALU.add)
        G("tensor_scalar", [tB_b], [tC_b], out=tC[:], in0=tB[:], scalar1=-C2, scalar2=None, op0=ALU.mult)
        G("tensor_tensor", [tA_b, tC_b], [tA_b], out=tA[:], in0=tA[:], in1=tC[:], op=ALU.add)
        G("tensor_scalar", [tA_b], [tB_b], out=tB[:], in0=tA[:], scalar1=math.pi, scalar2=-TWO_PI, op0=ALU.is_gt, op1=ALU.mult)
        G("tensor_tensor", [tA_b, tB_b], [tA_b], out=tA[:], in0=tA[:], in1=tB[:], op=ALU.add)
        G("tensor_scalar", [tA_b], [tB_b], out=tB[:], in0=tA[:], scalar1=-math.pi, scalar2=TWO_PI, op0=ALU.is_lt, op1=ALU.mult)
        G("tensor_tensor", [tA_b, tB_b], [tA_b], out=tA[:], in0=tA[:], in1=tB[:], op=ALU.add)
        G("tensor_scalar", [tA_b], [tA_b], out=tA[:], in0=tA[:], scalar1=PI_LO, scalar2=-PI_LO, op0=ALU.min, op1=ALU.max)
        A("activation", [tA_b], [tab_b], out=sin_t[:], in_=tA[:], func=AF.Sin)
        A("activation", [tA_b], [tB_b], out=tB[:], in_=tA[:], func=AF.Sin, scale=0.5)
        G("tensor_tensor", [tB_b], [tB_b], out=tB[:], in0=tB[:], in1=tB[:], op=ALU.mult)
        G("tensor_scalar", [tB_b], [tab_b], out=cos_t[:], in0=tB[:], scalar1=-2.0, scalar2=1.0, op0=ALU.mult, op1=ALU.add)

    def proj_fm(unit_t, col, nkc, rhs_fn, rhs_bufs, unit_buf):
        pb, pbb = pring()
        u3 = unit_t[:].rearrange("p (a b) -> p a b", a=nkc)
        for kc in range(nkc):
            PE("matmul", [unit_buf, rhs_bufs[kc]], [pbb], out=pb[:], lhsT=u3[:, kc, col:col + 128], rhs=rhs_fn(kc),
               start=(kc == 0), stop=(kc == nkc - 1))
        return pb, pbb

    def hT_rhs(kc):
        return hT[:, kc, :]

    def rotary(pa, pab, pb_, pbb_, dst, dstb, si):
        i0, i1 = nxt("rtmp", 4), nxt("rtmp", 4)
        V("tensor_tensor", [pab, tab_b], [rtmp_b[i0]], out=rtmp[i0][:], in0=pa[:], in1=cos_t[:], op=ALU.mult)
        V("tensor_tensor", [pbb_, tab_b], [rtmp_b[i1]], out=rtmp[i1][:], in0=pb_[:], in1=sin_t[:], op=ALU.mult)
        V("tensor_tensor", [rtmp_b[i0], rtmp_b[i1]], [dstb[si][0]], out=dst[si][:, 0, :], in0=rtmp[i0][:], in1=rtmp[i1][:],
          op=ALU.subtract)
        i2, i3 = nxt("rtmp", 4), nxt("rtmp", 4)
        V("tensor_tensor", [pab, tab_b], [rtmp_b[i2]], out=rtmp[i2][:], in0=pa[:], in1=sin_t[:], op=ALU.mult)
        V("tensor_tensor", [pbb_, tab_b], [rtmp_b[i3]], out=rtmp[i3][:], in0=pb_[:], in1=cos_t[:], op=ALU.mult)
        V("tensor_tensor", [rtmp_b[i2], rtmp_b[i3]], [dstb[si][1]], out=dst[si][:, 1, :], in0=rtmp[i2][:], in1=rtmp[i3][:],
          op=ALU.add)

    def head_kv(h, si, with_q, stream="M"):
        if with_q:
            u, ub = acquire("Q%d" % h, stream)
            pa, pab = proj_fm(u, 0, KC, hT_rhs, hT_b, ub)
            pb_, pbb_ = proj_fm(u, 128, KC, hT_rhs, hT_b, ub)
            rotary(pa, pab, pb_, pbb_, qT, qT_b, si)
            yield
        u, ub = acquire("K%d" % h, stream)
        pa, pab = proj_fm(u, 0, KC, hT_rhs, hT_b, ub)
        pb_, pbb_ = proj_fm(u, 128, KC, hT_rhs, hT_b, ub)
        rotary(pa, pab, pb_, pbb_, kT, kT_b, si)
        for n2 in range(2):
            pt, ptb = ptring()
            for nn in range(2):
                n = n2 * 2 + nn
                for dc in range(2):
                    PE("transpose", [kT_b[si][dc], B_c2], [ptb], out=pt[:, (nn * 2 + dc) * 128:(nn * 2 + dc + 1) * 128],
                       in_=kT[si][:, dc, n * 128:(n + 1) * 128], identity=ident_b[:])
            for nn in range(2):
                n = n2 * 2 + nn
                A("activation", [ptb, B_const], [kz_b[si][n]], out=kz[si][:, n, :], in_=pt[:, nn * 256:(nn + 1) * 256],
                  func=AF.Copy, scale=zs[:, h:h + 1])
        yield
        for half in range(2):
            u, ub = acquire("V%d_%d" % (h, half), stream)
            u3 = u[:].rearrange("p (a b) -> p a b", a=KC)
            for n in range(NT):
                pb, pbb = pring()
                for kc in range(KC):
                    PE("matmul", [ub, hT_b[kc]], [pbb], out=pb[:, 0:256], lhsT=hT[:, kc, n * 128:(n + 1) * 128],
                       rhs=u3[:, kc, :], start=(kc == 0), stop=(kc == KC - 1))
                dst = vv[si][:, n, half * 256:(half + 1) * 256]
                A("activation", [pbb], [vv_b[si][n]], out=dst, in_=pb[:, 0:256], func=AF.Copy)
        yield

    def state_update(h, si, n, need_bf):
        for dc in range(2):
            pb, pbb = pring()
            PE("matmul", [kz_b[si][n], vv_b[si][n]], [pbb], out=pb[:], lhsT=kz[si][:, n, dc * 128:(dc + 1) * 128],
               rhs=vv[si][:, n, :], start=True, stop=True)
            j = h * 2 + dc
            V("scalar_tensor_tensor", [pbb, st_fb[j]], [st_fb[j]], out=st_f[:, j, :], in0=st_f[:, j, :], scalar=gC[h],
              in1=pb[:], op0=ALU.mult, op1=ALU.add)
            if need_bf:
                A("activation", [st_fb[j]], [st_bb[j]], out=st_bf[:, j, :], in_=st_f[:, j, :], func=AF.Copy)

    def stage_glu(stream="M"):
        for i in range(4):
            ug, ugb = acquire("UG%d" % i, stream)
            tg = []
            for q in range(2):
                pg, pgb = proj_fm(ug, q * 128, KC, hT_rhs, hT_b, ugb)
                li = nxt("lns", 2)
                A("activation", [pgb], [lns_b[li]], out=lns[li], in_=pg[:], func=AF.Tanh, scale=0.5)
                tg.append(li)
                yield
            uv, uvb = acquire("UV%d" % i, stream)
            for q in range(2):
                c = i * 2 + q
                pv, pvb = proj_fm(uv, q * 128, KC, hT_rhs, hT_b, uvb)
                li = tg[q]
                V("scalar_tensor_tensor", [pvb, lns_b[li]], [aT_b[c]], out=aT[:, c, HALO:HALO + T], in0=lns[li], scalar=1.0,
                  in1=pv[:], op0=ALU.add, op1=ALU.mult)
                yield
        release(stream)

    def halo_shift(use_flag):
        for c in range(KC):
            if use_flag:
                V("tensor_scalar", [aT_b[c], B_const], [aT_b[c]], out=aT[:, c, 0:HALO], in0=aT[:, c, T:T + HALO],
                  scalar1=flag[:, 0:1], scalar2=None, op0=ALU.mult)
            else:
                V("tensor_copy", [aT_b[c]], [aT_b[c]], out=aT[:, c, 0:HALO], in_=aT[:, c, T:T + HALO])

    def stage_conv(stream="M"):
        L = lambda i: lnt[:, i * T:(i + 1) * T]
        def gen_odd(c):
            tl = {}
            for k in range(1, CONVW, 2):
                di = nxt("diag", NDIAG)
                A("activation", [B_const], [diag_b[di]], out=diag[di][:], in_=ident_f[:], func=AF.Copy,
                  scale=conv_w05[:, c, k:k + 1])
                tl[k] = di
            return tl

        odd_next = gen_odd(0)
        pend_stats = None
        for c in range(KC):
            odd = odd_next
            if c + 1 < KC:
                odd_next = gen_odd(c + 1)
            dg, dgb = acquire("DG%d" % c, stream)
            dg3 = dg[:].rearrange("p (a b) -> p a b", a=16)
            pb, pbb = pring()
            for k in range(CONVW):
                if k % 2 == 0:
                    lhs, lb_ = dg3[:, k // 2, :], dgb
                else:
                    lhs, lb_ = diag[odd[k]][:], diag_b[odd[k]]
                PE("matmul", [lb_, aT_b[c]], [pbb], out=pb[:], lhsT=lhs, rhs=aT[:, c, k:k + T],
                   start=(k == 0), stop=(k == CONVW - 1))
            acs = ac[:, c * T:(c + 1) * T]
            A("activation", [pbb, B_const], [ac_cb[c]], out=acs, in_=pb[:], func=AF.Identity, bias=conv_bT[:, c:c + 1])
            ai = nxt("acb", 2)
            A("activation", [ac_cb[c]], [acb16_b[ai]], out=acb16[ai][:], in_=acs, func=AF.Copy)
            A("activation", [ac_cb[c]], [sqb16_b[ai]], out=sqb16[ai][:], in_=acs, func=AF.Square)
            def stats(c=c, ai=ai):
                ps_, psb_ = pring()
                PE("matmul", [acb16_b[ai], B_c2], [psb_], out=ps_[:], lhsT=ones_b[:], rhs=acb16[ai][:], start=True, stop=True)
                pq_, pqb_ = pring()
                PE("matmul", [sqb16_b[ai], B_c2], [pqb_], out=pq_[:], lhsT=ones_b[:], rhs=sqb16[ai][:], start=True, stop=True)
                if c == 0:
                    V("tensor_copy", [psb_], [lnt_b], out=L(0), in_=ps_[:])
                    V("tensor_copy", [pqb_], [lnt_b], out=L(1), in_=pq_[:])
                else:
                    V("tensor_tensor", [psb_, lnt_b], [lnt_b], out=L(0), in0=L(0), in1=ps_[:], op=ALU.add)
                    V("tensor_tensor", [pqb_, lnt_b], [lnt_b], out=L(1), in0=L(1), in1=pq_[:], op=ALU.add)
            if pend_stats is not None:
                pend_stats()
            pend_stats = stats
            yield
        pend_stats()
        mean, var, rstd, tmp, mr = L(0), L(1), L(2), L(3), L(4)
        lni = L(5).bitcast(I32)
        lb = [lnt_b]
        V("tensor_scalar", lb, lb, out=mean, in0=mean, scalar1=1.0 / D, scalar2=None, op0=ALU.mult)
        V("tensor_scalar", lb, lb, out=var, in0=var, scalar1=1.0 / D, scalar2=LN_EPS, op0=ALU.mult, op1=ALU.add)
        V("tensor_tensor", lb, lb, out=tmp, in0=mean, in1=mean, op=ALU.mult)
        V("tensor_tensor", lb, lb, out=var, in0=var, in1=tmp, op=ALU.subtract)
        rsqrt(var, rstd, tmp, lni, lb, scalar=False)
        V("tensor_tensor", lb, lb, out=mr, in0=mean, in1=rstd, op=ALU.mult)
        for i in range(4):
            zc, zcb = acquire("ZC%d" % i, stream)
            for q in range(2):
                c = i * 2 + q
                acs = ac[:, c * T:(c + 1) * T]
                pz, pzb = proj_fm(zc, q * 128, KC, hT_rhs, hT_b, zcb)
                zi = nxt("szc", 2)
                A("activation", [pzb], [szc_b[zi]], out=szc[zi][:], in_=pz[:], func=AF.Silu)
                li = nxt("lns", 2)
                V("tensor_tensor", [ac_cb[c], lnt_b], [lns_b[li]], out=lns[li], in0=acs, in1=rstd, op=ALU.mult)
                V("tensor_tensor", [lns_b[li], lnt_b], [lns_b[li]], out=lns[li], in0=lns[li], in1=mr, op=ALU.subtract)
                A("activation", [lns_b[li], B_const], [lns_b[li]], out=lns[li], in_=lns[li], func=AF.Silu,
                  bias=ln_bT[:, c:c + 1], scale=ln_gT[:, c:c + 1])
                V("tensor_tensor", [lns_b[li], szc_b[zi]], [bT_b[c]], out=bT[:, c, :], in0=lns[li], in1=szc[zi][:], op=ALU.mult)
                yield
        release(stream)

    def head_proj(h, si, stream="M"):
        for _ in head_kv(h, si, True, stream):
            yield
        for half in range(2):
            u, ub = acquire("Z%d_%d" % (h, half), stream)
            for q in range(2):
                ech = half * 2 + q
                pz, pzb = proj_fm(u, q * 128, KC, hT_rhs, hT_b, ub)
                A("activation", [pzb], [szT_b[si]], out=szT3[si][:, ech, :], in_=pz[:], func=AF.Silu)
        yield

    def ret_scores(h, si):
        pis = []
        for n in range(NT):
            tok = slice(n * 128, (n + 1) * 128)
            pS, pSb = pring()
            for dc in range(2):
                PE("matmul", [kT_b[si][dc], qT_b[si][dc]], [pSb], out=pS[:, 0:128], lhsT=kT[si][:, dc, tok],
                   rhs=qT[si][:, dc, tok], start=(dc == 0), stop=(dc == 1))
            pi = nxt("PT", 4)
            V("tensor_tensor", [pSb, B_const], [PT_b[pi]], out=PT[pi][:], in0=pS[:, 0:128], in1=maskT[:, h, :], op=ALU.mult)
            pis.append(pi)
        return pis

    def ret_chunk(h, si, n, pi):
        tok = slice(n * 128, (n + 1) * 128)
        oi_ = nxt("pO", 2)
        pO, pOb = pbank[4 + oi_], pbuf[4 + oi_]
        for dc in range(2):
            j = h * 2 + dc
            PE("matmul", [qT_b[si][dc], st_bb[j]], [pOb], out=pO[:], lhsT=qT[si][:, dc, tok], rhs=st_bf[:, j, :],
               start=(dc == 0), stop=False)
        PE("matmul", [PT_b[pi], vv_b[si][n]], [pOb], out=pO[:], lhsT=PT[pi][:], rhs=vv[si][:, n, :], start=False, stop=True)
        state_update(h, si, n, True)
        mi = nxt("sm", 4)
        smt, smb = sm[mi], sm_b[mi]
        V("bn_stats", [pOb], [smb], out=smt[:, 0:6], in_=pO[:])
        V("bn_aggr", [smb], [smb], out=smt[:, 6:8], in_=smt[:, 0:6])
        V("tensor_tensor", [smb, B_const], [smb], out=smt[:, 8:9], in0=smt[:, 7:8], in1=epsp[:, h:h + 1], op=ALU.add)
        rsqrt(smt[:, 8:9], smt[:, 9:10], smt[:, 10:11], smi[mi][:, 0:1], [smb])
        V("scalar_tensor_tensor", [smb], [smb], out=smt[:, 11:12], in0=smt[:, 6:7], scalar=-1.0, in1=smt[:, 9:10],
          op0=ALU.mult, op1=ALU.mult)
        oi = nxt("on", 2)
        A("activation", [pOb, smb], [on_b[oi]], out=on[oi][:], in_=pO[:], func=AF.Identity, bias=smt[:, 11:12],
          scale=smt[:, 9:10])

        def tail():
            pt, ptb = ptring()
            for ech in range(4):
                PE("transpose", [on_b[oi], B_c2], [ptb], out=pt[:, ech * 128:(ech + 1) * 128],
                   in_=on[oi][:, ech * 128:(ech + 1) * 128], identity=ident_b[:])
            V("tensor_tensor", [ptb, szT_b[si]], [rT_b[h]], out=rT[:, h * 4:(h + 1) * 4, tok],
              in0=pt.rearrange("p (a b) -> p a b", a=4), in1=szT3[si][:, :, tok], op=ALU.mult)
        return tail

    def stage_ret(stream="M"):
        g = head_proj(0, 0, stream)
        for _ in g:
            yield
        pending = None
        for h in range(H):
            gn = head_proj(h + 1, (h + 1) % 2, stream) if h + 1 < H else None
            pis = ret_scores(h, h % 2)
            for n in range(NT):
                if gn is not None:
                    next(gn, None)
                    yield
                t = ret_chunk(h, h % 2, n, pis[n])
                if pending is not None:
                    pending()
                pending = t
                yield
            if gn is not None:
                for _ in gn:
                    yield
        pending()
        release(stream)

    def run(g):
        for _ in g:
            pass

    def interleave(ga, gb_):
        a_live, b_live = True, True
        while a_live or b_live:
            if a_live:
                try:
                    next(ga)
                except StopIteration:
                    a_live = False
            if b_live:
                try:
                    next(gb_)
                except StopIteration:
                    b_live = False

    def conv_stream():
        for _ in stage_glu("X"):
            yield
        for _ in stage_conv("X"):
            yield
        halo_shift(False)

    def stage_merge():
        for i in range(4):
            ga, gab = acquire("GA%d" % i)
            ta = []
            for q in range(2):
                pg, pgb = proj_fm(ga, q * 128, KC, hT_rhs, hT_b, gab)
                li = nxt("rtmp", 4)
                A("activation", [pgb], [rtmp_b[li]], out=rtmp[li][:], in_=pg[:], func=AF.Tanh, scale=0.5)
                ta.append(li)
            gb_, gbb = acquire("GB%d" % i)
            tb = []
            for q in range(2):
                pg, pgb = proj_fm(gb_, q * 128, KC, hT_rhs, hT_b, gbb)
                li = nxt("rtmp", 4)
                A("activation", [pgb], [rtmp_b[li]], out=rtmp[li][:], in_=pg[:], func=AF.Tanh, scale=0.5)
                tb.append(li)
            co, cob = acquire("CO%d" % i)
            for q in range(2):
                py, pyb = proj_fm(co, q * 128, KC, lambda kc: bT[:, kc, :], bT_b, cob)
                li = tb[q]
                V("scalar_tensor_tensor", [pyb, rtmp_b[li]], [rtmp_b[li]], out=rtmp[li][:], in0=rtmp[li][:], scalar=1.0,
                  in1=py[:], op0=ALU.add, op1=ALU.mult)
            for q in range(2):
                dch = i * 2 + q
                ro, rob = acquire("RO%d" % dch)
                py, pyb = proj_fm(ro, 0, 16, lambda ec: rT[:, ec, :], [rT_b[ec // 4] for ec in range(16)], rob)
                la, lb_ = ta[q], tb[q]
                V("scalar_tensor_tensor", [pyb, rtmp_b[la]], [rtmp_b[la]], out=rtmp[la][:], in0=rtmp[la][:], scalar=1.0,
                  in1=py[:], op0=ALU.add, op1=ALU.mult)
                V("tensor_tensor", [rtmp_b[la], rtmp_b[lb_]], [aT_b[dch]], out=mT3[:, dch, :], in0=rtmp[la][:],
                  in1=rtmp[lb_][:], op=ALU.add)

    B_out = Buf("outd")

    def stage_out(blk, between=None):
        for tp in range(2):
            if tp == 1 and between is not None:
                between()
            tiles = (tp * 2, tp * 2 + 1)
            banks = {n: (pring(), pring()) for n in tiles}
            for ui in range(4):
                u, ub = acquire("WO%d" % ui)
                u3 = u[:].rearrange("p (a b) -> p a b", a=KC)
                cs = (ui % 2) * 256
                for n in tiles:
                    pb, pbb = banks[n][ui // 2]
                    for kc in range(KC):
                        PE("matmul", [ub, aT_b[kc]], [pbb], out=pb[:, cs:cs + 256], lhsT=mT3[:, kc, n * 128:(n + 1) * 128],
                           rhs=u3[:, kc, :], start=(kc == 0), stop=(kc == KC - 1))
            for n in tiles:
                (pb0, pbb0), (pb1, pbb1) = banks[n]
                t = blk * NT + n
                xi_ = t % 2
                DMA2([], [xr_b[xi_]], out=xr[xi_][:], in_=x_own[t * 128:(t + 1) * 128, :])
                mi = nxt("sm", 4)
                smt, smb = sm[mi], sm_b[mi]
                j0, j1 = nxt("rtmp", 4), nxt("rtmp", 4)
                A("activation", [pbb0], [rtmp_b[j0], smb], out=rtmp[j0][:], in_=pb0[:], func=AF.Square, accum_out=smt[:, 0:1])
                A("activation", [pbb1], [rtmp_b[j1], smb], out=rtmp[j1][:], in_=pb1[:], func=AF.Square, accum_out=smt[:, 1:2])
                V("tensor_tensor", [smb], [smb], out=smt[:, 2:3], in0=smt[:, 0:1], in1=smt[:, 1:2], op=ALU.add)
                V("tensor_scalar", [smb], [smb], out=smt[:, 3:4], in0=smt[:, 2:3], scalar1=1.0 / D, scalar2=4.0 * RMS_EPS,
                  op0=ALU.mult, op1=ALU.add)
                rsqrt(smt[:, 3:4], smt[:, 4:5], smt[:, 5:6], smi[mi][:, 0:1], [smb])
                for hf, (pb, pbb) in enumerate(((pb0, pbb0), (pb1, pbb1))):
                    cs = slice(hf * 512, (hf + 1) * 512)
                    li = nxt("lns", 2)
                    V("scalar_tensor_tensor", [pbb, smb, B_pgg], [lns_b[li]], out=lns[li], in0=pb[:], scalar=smt[:, 4:5],
                      in1=pgg[:, cs], op0=ALU.mult, op1=ALU.mult)
                    V("tensor_tensor", [lns_b[li], xr_b[xi_]], [xr_b[xi_]], out=xr[xi_][:, cs], in0=xr[xi_][:, cs],
                      in1=lns[li], op=ALU.add)
                DMA2([xr_b[xi_]], [B_out], out=out_d[t * 128:(t + 1) * 128, :], in_=xr[xi_][:])

    def body():
        for j in range(H * 2):
            V("memset", [], [st_fb[j]], ap=st_f[:, j, :], constant=0.0)
        for c in range(KC):
            V("memset", [], [aT_b[c]], ap=aT[:, c, :], constant=0.0)
        cast_mode["pool_only"] = False
        run(gen_convert_first())
        gc = gen_convert_rest()
        cast_mode["pool_only"] = True
        gd = gen_diag_units()
        if NBLK == 1:
            run(gc)
            run(gd)
            run(gen_gate_row())
            staging_barrier()
        pre_elem()
        pre_pe()
        stage_tables(pos_prev, 0)
        for blk in range(NBLK):
            run(head_kv(0, 0, False))
            for h in range(H):
                si = h % 2
                if NBLK > 1 and blk < NBLK - 1:
                    next(gc, None)
                    next(gc, None)
                if h + 1 < H:
                    run(head_kv(h + 1, (h + 1) % 2, False))
                if h == 1:
                    pre_elem()
                if h == 2 and blk + 1 < NBLK:
                    pre_pe()
                    stage_tables(pos_prev, (blk + 1) * T)
                for n in range(NT):
                    state_update(h, si, n, False)
            if NBLK > 1 and blk == NBLK - 2:
                run(gc)
                run(gd)
                run(gen_gate_row())
                staging_barrier()
            if blk == NBLK - 1:
                run(stage_glu())
                halo_shift(True)
        for j in range(H * 2):
            V("tensor_scalar", [st_fb[j], B_const], [st_fb[j]], out=st_f[:, j, :], in0=st_f[:, j, :], scalar1=flag[:, 0:1],
              scalar2=None, op0=ALU.mult)
            A("activation", [st_fb[j]], [st_bb[j]], out=st_bf[:, j, :], in_=st_f[:, j, :], func=AF.Copy)
        pre_pe()
        stage_tables(pos_own, 0)
        for blk in range(NBLK):
            release("M")
            interleave(conv_stream(), stage_ret("Y"))
            if blk + 1 < NBLK:
                pre_elem()
            stage_merge()
            if blk + 1 < NBLK:
                def nxt_blk(b=blk + 1):
                    pre_pe()
                    stage_tables(pos_own, b * T)
                stage_out(blk, nxt_blk)
            else:
                stage_out(blk)

    rr_save = dict(rr)
    S.dry = True
    body()
    S.dry = False
    rr.clear()
    rr.update(rr_save)
    xst["cons"] = 0
    xst["loaded"] = 0
    body()
    assert wst["cons"] == len(useq), (wst["cons"], len(useq))

    with ExitStack() as stack:
        nops, nwait = S.emit(stack)
    print("ops", nops, "waits", nwait)
    return nc, nops, nwait


_PROG_CACHE = {}


def kernel(x, c, positions, w_ada, b_ada, pre_norm_g, w_in, conv_w, conv_b, conv_ln_g, conv_ln_b,
           w_ret_out, w_conv_out, w_out, post_norm_g):
    x = np.asarray(x, dtype=np.float32)
    B, S_, _ = x.shape
    half = S_ // 2
    NBLK = half // T
    assert half % T == 0
    ncores = 2 * B
    if NBLK not in _PROG_CACHE:
        _PROG_CACHE[NBLK] = build_program(NBLK)[0]
    nc = _PROG_CACHE[NBLK]
    maskT, zs, epsp, gC, inv_freq = _host_consts()
    c = np.asarray(c, np.float32)
    positions = np.asarray(positions, np.int32)
    f = lambda a: np.ascontiguousarray(np.asarray(a, np.float32))
    lay = lambda v, n: f(np.asarray(v, np.float32).reshape(n, 128).T)
    b_ada0 = np.asarray(b_ada, np.float32)[0]
    shared = {
        "w_ada": f(w_ada[0]), "b_adaT": lay(b_ada0[:2048], 16), "b_gate": f(b_ada0[2048:].reshape(1, D)),
        "pre_gT": lay(pre_norm_g[0], KC), "w_in": f(w_in[0]),
        "conv_wT": f(np.asarray(conv_w[0], np.float32).T.reshape(KC, 128, CONVW).transpose(1, 0, 2).reshape(128, KC * CONVW)),
        "conv_bT": lay(conv_b[0], KC), "ln_gT": lay(conv_ln_g[0], KC), "ln_bT": lay(conv_ln_b[0], KC),
        "w_ret_out": f(w_ret_out[0]), "w_conv_out": f(w_conv_out[0]), "w_out": f(w_out[0]),
        "post_g": f(np.asarray(post_norm_g[0], np.float32).reshape(1, D)),
        "ident": np.eye(128, dtype=np.float32), "maskT": f(maskT.reshape(128, H * 128)), "zs": f(zs), "epsp": f(epsp),
        "inv_freq": f(inv_freq),
    }
    in_maps = []
    for b in range(B):
        for j in range(2):
            own = x[b, j * half:(j + 1) * half]
            prev = x[b, 0:half]
            m = dict(shared)
            m["x_own"] = np.ascontiguousarray(own)
            m["x_prev"] = np.ascontiguousarray(prev)
            m["pos_own"] = np.ascontiguousarray(positions[b, j * half:(j + 1) * half].reshape(1, half))
            m["pos_prev"] = np.ascontiguousarray(positions[b, 0:half].reshape(1, half))
            m["flag"] = np.full((128, 1), float(j), np.float32)
            m["cT"] = lay(c[b], KC)
            in_maps.append(m)
    res = run_bass_kernel_spmd(nc, in_maps, core_ids=list(range(ncores)))
    out = np.empty((B, S_, D), np.float32)
    for b in range(B):
        for j in range(2):
            out[b, j * half:(j + 1) * half] = res.results[b * 2 + j]["out"]
    return out
```

```python
import math
from contextlib import ExitStack

import numpy as np
import concourse.bass as bass
import concourse.mybir as mybir
from concourse.bass_utils import run_bass_kernel_spmd

F32 = mybir.dt.float32
BF16 = mybir.dt.bfloat16
I32 = mybir.dt.int32
AF = mybir.ActivationFunctionType
ALU = mybir.AluOpType

D = 1024
T = 512
NT = 4
KC = 8
H = 4
CONVW = 31
HALO = CONVW - 1
IN_W = 11264
RMS_EPS = 1e-6
GN_EPS = 1e-5
LN_EPS = 1e-5
TWO_PI = 2.0 * math.pi
C1 = 6.28125
C2 = TWO_PI - C1
PI_LO = 3.1415925
RSQRT_MAGIC = 0x5F3759DF
NEWTON_ITERS = 3

OFF_Q, OFF_K, OFF_V, OFF_Z, OFF_UV, OFF_UG, OFF_ZC, OFF_GA, OFF_GB = (
    0, 1024, 2048, 4096, 6144, 7168, 8192, 9216, 10240)


class Buf:
    __slots__ = ("name", "w", "r")

    def __init__(self, name=""):
        self.name = name
        self.w = None
        self.r = []


class Op:
    __slots__ = ("eng", "fn", "deps", "dma", "sig", "sem", "val", "idx")


class Sched:
    ENGS = ("pe", "act", "dve", "pool", "sp")

    def __init__(self, nc, n_dma_sems=32):
        self.nc = nc
        self.ops = []
        self.n_dma_sems = n_dma_sems

    dry = False

    def op(self, eng, fn, reads=(), writes=(), dma=False, **kw):
        if self.dry:
            return None
        o = Op()
        o.eng = eng
        o.fn = (fn, kw)
        o.dma = dma
        o.sig = False
        o.sem = None
        o.val = 0
        o.idx = len(self.ops)
        deps = set()
        for b in reads:
            if b.w is not None:
                deps.add(b.w)
        for b in writes:
            if b.w is not None:
                deps.add(b.w)
            for r in b.r:
                deps.add(r)
        deps.discard(o)
        o.deps = deps
        for b in reads:
            b.r.append(o)
        for b in writes:
            b.w = o
            b.r = []
        self.ops.append(o)
        return o

    def emit(self, stack):
        nc = self.nc
        engh = {"pe": nc.tensor, "act": nc.scalar, "dve": nc.vector, "pool": nc.gpsimd, "sp": nc.sync}
        esem = {e: stack.enter_context(nc.semaphore("s_" + e)) for e in self.ENGS}
        dsem = [stack.enter_context(nc.semaphore("d%d" % i)) for i in range(self.n_dma_sems)]
        for o in self.ops:
            for d in o.deps:
                if d.dma:
                    continue
                if d.eng == o.eng and o.eng in ("pe", "sp") and not o.dma:
                    continue
                d.sig = True
        cnt = {e: 0 for e in self.ENGS}
        NSW = 8
        dsem_sw = [stack.enter_context(nc.semaphore("w%d" % i)) for i in range(NSW)]
        pools = {"sp": (dsem, [0] * self.n_dma_sems, [None] * self.n_dma_sems, [0]),
                 "pool": (dsem_sw, [0] * NSW, [None] * NSW, [0])}
        for o in self.ops:
            if o.dma:
                sems, dcnt_, dlast_, nd_ = pools["pool" if o.eng == "pool" else "sp"]
                k = nd_[0] % len(sems)
                nd_[0] += 1
                if dlast_[k] is not None:
                    o.deps.add(dlast_[k])
                dlast_[k] = o
                dcnt_[k] += 16
                o.sem = sems[k]
                o.val = dcnt_[k]
                o.sig = True
            elif o.sig:
                cnt[o.eng] += 1
                o.sem = esem[o.eng]
                o.val = cnt[o.eng]
        waited = {}
        nwait = 0
        for o in self.ops:
            e = engh[o.eng]
            need = {}
            for d in o.deps:
                if not d.dma and d.eng == o.eng and o.eng in ("pe", "sp") and not o.dma:
                    continue
                if not d.dma and d.eng == o.eng and o.eng == "sp":
                    continue
                key = id(d.sem)
                if key not in need or need[key][1] < d.val:
                    need[key] = (d.sem, d.val)
            pend = []
            for key, (sem, val) in need.items():
                wk = (o.eng, key)
                if waited.get(wk, 0) >= val:
                    continue
                waited[wk] = val
                pend.append((sem, val))
            for sem, val in pend[:-1]:
                e.wait_ge(sem, val)
                nwait += 1
            ins = getattr(e, o.fn[0])(**o.fn[1])
            if pend:
                ins._wait_ge(pend[-1][0], pend[-1][1])
            if o.sig:
                ins.then_inc(o.sem, 16 if o.dma else 1)
        for sems, dcnt_, _, _ in pools.values():
            for k in range(len(sems)):
                if dcnt_[k] > 0:
                    nc.sync.wait_ge(sems[k], dcnt_[k])
        return len(self.ops), nwait


def _host_consts():
    hh = np.arange(H, dtype=np.float64)
    g = 1.0 - np.exp2(-5.0 - hh)
    j = np.arange(128, dtype=np.float64)
    ginv = g[None, :] ** (-(j[:, None] + 1.0))
    causal = (j[:, None] <= j[None, :]).astype(np.float64)
    maskT = (ginv[:, :, None] * causal[:, None, :] / 16.0).astype(np.float32)
    zs = (g[None, :] ** (127.0 - j[:, None]) / 16.0).astype(np.float32)
    xi = g[None, :] ** (j[:, None] + 1.0)
    epsp = (GN_EPS / (xi * xi)).astype(np.float32)
    gC = [float(np.float32(gg ** 128.0)) for gg in g]
    half = 128
    inv_freq = (np.float32(10000.0) ** (-(np.arange(half, dtype=np.float32) / np.float32(half)))).astype(np.float32)
    return maskT, zs, epsp, gC, inv_freq.reshape(128, 1)


def build_program(NBLK):
    NTOK = NBLK * T
    nc = bass.Bass("TRN2", target_bir_lowering=False)
    maskT_np, zs_np, epsp_np, gC, _ = _host_consts()

    def din(name, shape, dt=F32):
        return nc.dram_tensor(name, list(shape), dt, kind="ExternalInput").ap()

    x_own = din("x_own", [NTOK, D])
    x_prev = din("x_prev", [NTOK, D])
    pos_own = din("pos_own", [1, NTOK], I32)
    pos_prev = din("pos_prev", [1, NTOK], I32)
    flag_d = din("flag", [128, 1])
    cT_d = din("cT", [128, KC])
    w_ada = din("w_ada", [D, 3 * D])
    b_adaT_d = din("b_adaT", [128, 16])
    b_gate_d = din("b_gate", [1, D])
    pre_gT_d = din("pre_gT", [128, KC])
    w_in = din("w_in", [D, IN_W])
    conv_wT_d = din("conv_wT", [128, KC * CONVW])
    conv_bT_d = din("conv_bT", [128, KC])
    ln_gT_d = din("ln_gT", [128, KC])
    ln_bT_d = din("ln_bT", [128, KC])
    w_ro = din("w_ret_out", [2 * D, D])
    w_co = din("w_conv_out", [D, D])
    w_o = din("w_out", [D, D])
    post_g_d = din("post_g", [1, D])
    ident_d = din("ident", [128, 128])
    maskT_d = din("maskT", [128, H * 128])
    zs_d = din("zs", [128, H])
    epsp_d = din("epsp", [128, H])
    invf_d = din("inv_freq", [128, 1])
    out_d = nc.dram_tensor("out", [NTOK, D], F32, kind="ExternalOutput").ap()

    units = {}
    ulist = []

    def add_unit(name):
        units[name] = len(ulist)
        ulist.append(name)

    for h in range(H):
        for nm in ("Q%d", "K%d", "V%d_0", "V%d_1", "Z%d_0", "Z%d_1"):
            add_unit(nm % h)
    for nm in ("UV", "UG", "ZC", "GA", "GB"):
        for i in range(4):
            add_unit("%s%d" % (nm, i))
    for i in range(8):
        add_unit("RO%d" % i)
    for i in range(4):
        add_unit("CO%d" % i)
    for i in range(4):
        add_unit("WO%d" % i)
    for c in range(KC):
        add_unit("DG%d" % c)
    NU = len(ulist)
    wsc = nc.dram_tensor("wsc", [NU, 128, 2048], BF16).ap()
    B_wsc = [Buf("wsc%d" % i) for i in range(NU)]

    S = Sched(nc)
    sb = nc.alloc_sbuf_tensor
    ps = nc.alloc_psum_tensor

    def V(m, reads, writes, **kw):
        S.op("dve", m, reads, writes, **kw)

    def A(m, reads, writes, **kw):
        S.op("act", m, reads, writes, **kw)

    def G(m, reads, writes, **kw):
        S.op("pool", m, reads, writes, **kw)

    def PE(m, reads, writes, **kw):
        S.op("pe", m, reads, writes, **kw)

    def DMA(reads, writes, **kw):
        S.op("sp", "dma_start", reads, writes, dma=True, **kw)

    def DMA2(reads, writes, **kw):
        S.op("pool", "dma_start", reads, writes, dma=True, **kw)

    ident_f = sb("ident_f", [128, 128], F32)
    ident_b = sb("ident_b", [128, 128], BF16)
    ones_b = sb("ones_b", [128, 128], BF16)
    maskT = sb("maskT_sb", [128, H, 128], F32)
    zs = sb("zs_sb", [128, H], F32)
    epsp = sb("epsp_sb", [128, H], F32)
    invf = sb("invf_sb", [128, 1], F32)
    flag = sb("flagt", [128, 1], F32)
    cT = sb("cTt", [128, KC], F32)
    cT_b = sb("cT_b", [128, KC], BF16)
    cbc = sb("cbc", [128, KC, 128], BF16)
    b_adaT = sb("b_adaTt", [128, 16], F32)
    pre_gT = sb("pre_gTt", [128, KC], F32)
    gs = sb("gs", [128, KC], F32)
    shiftT = sb("shiftT", [128, KC], F32)
    conv_w05 = sb("conv_w05", [128, KC, CONVW], F32)
    conv_bT = sb("conv_bTt", [128, KC], F32)
    ln_gT = sb("ln_gTt", [128, KC], F32)
    ln_bT = sb("ln_bTt", [128, KC], F32)
    pgg = sb("pgg", [128, D], F32)
    xr = [sb("xr%d" % i, [128, D], F32) for i in range(2)]
    xr_b = [Buf("xr%d" % i) for i in range(2)]
    bgate, bgate_b = xr[1], xr_b[1]
    B_const = Buf("const")
    B_c2 = Buf("const2")

    def ld_const(dst_ap, src_ap, buf=B_const):
        DMA([], [buf], out=dst_ap, in_=src_ap)

    ld_const(ident_f[:], ident_d[:, :])
    ld_const(maskT[:].rearrange("p h i -> p (h i)"), maskT_d[:, :])
    ld_const(zs[:], zs_d[:, :])
    ld_const(epsp[:], epsp_d[:, :])
    ld_const(invf[:], invf_d[:, :])
    ld_const(flag[:], flag_d[:, :])
    ld_const(cT[:], cT_d[:, :])
    ld_const(b_adaT[:], b_adaT_d[:, :])
    ld_const(pre_gT[:], pre_gT_d[:, :])
    ld_const(conv_w05[:].rearrange("p c k -> p (c k)"), conv_wT_d[:, :])
    ld_const(conv_bT[:], conv_bT_d[:, :])
    ld_const(ln_gT[:], ln_gT_d[:, :])
    ld_const(ln_bT[:], ln_bT_d[:, :])
    ld_const(pgg[:], post_g_d[0:1, :].to_broadcast([128, D]))
    ld_const(bgate[:], b_gate_d[0:1, :].to_broadcast([128, D]), bgate_b)
    V("tensor_copy", [B_const], [B_c2], out=ident_b[:], in_=ident_f[:])
    V("memset", [], [B_c2], ap=ones_b[:], constant=1.0)
    V("tensor_copy", [B_const], [B_c2], out=cT_b[:], in_=cT[:])
    for kc in range(KC):
        V("tensor_copy", [B_const], [B_c2], out=cbc[:, kc, :], in_=cT[:, kc:kc + 1].to_broadcast([128, 128]))
    V("tensor_scalar", [B_const], [B_const], out=conv_w05[:], in0=conv_w05[:], scalar1=0.5, scalar2=None, op0=ALU.mult)

    pbank = [ps("pb%d" % i, [128, 512], F32) for i in range(6)]
    pbuf = [Buf("pb%d" % i) for i in range(6)]
    ptr = [ps("ptr%d" % i, [128, 1024], BF16) for i in range(2)]
    ptrbuf = [Buf("ptr0"), Buf("ptr1")]
    rr = {}

    def nxt(key, n):
        i = rr.get(key, 0)
        rr[key] = i + 1
        return i % n

    def pring():
        i = nxt("pring", 4)
        return pbank[i], pbuf[i]

    def ptring():
        i = nxt("ptring", 2)
        return ptr[i][:, 0:512], ptrbuf[i]

    stg_f = [sb("stg_f%d" % i, [128, 4096], F32) for i in range(2)]
    stg_fb = [Buf("stg_f%d" % i) for i in range(2)]
    stg_b0 = sb("stg_b0", [128, 4096], BF16)
    rT = sb("rT", [128, 16, T], BF16)
    stg_bap = [stg_b0[:], rT[:, 0:8, :].rearrange("p a b -> p (a b)")]
    stg_bb = [Buf("stg_b%d" % i) for i in range(2)]
    vvbuf = sb("vvbuf", [128, 4096], BF16)

    NSLOT = 5
    wslot = [sb("wslot%d" % i, [128, 2048], BF16) for i in range(NSLOT)]
    wslot_b = [Buf("wslot%d" % i) for i in range(NSLOT)]

    cast_mode = {"pool_only": False}

    def cast_op(dst, src, reads, writes):
        k = 1 if cast_mode["pool_only"] else nxt("cast", 3)
        if k == 0:
            V("tensor_copy", reads, writes, out=dst, in_=src)
        elif k == 1:
            A("activation", reads, writes, out=dst, in_=src, func=AF.Copy)
        else:
            G("tensor_copy", reads, writes, out=dst, in_=src)

    def stage_load_issue(src_ap3, nA):
        i = nxt("stg", 2)
        sf, sfb = stg_f[i], stg_fb[i]
        sf3 = sf[:].rearrange("p (a b) -> p a b", a=nA)
        (DMA2 if cast_mode["pool_only"] else DMA)([], [sfb], out=sf3, in_=src_ap3)
        return i

    def stage_load_cast(i, nA):
        sf, sfb, sbap, sbb = stg_f[i], stg_fb[i], stg_bap[i], stg_bb[i]
        sb3 = sbap.rearrange("p (a b) -> p a b", a=nA)
        cast_op(sbap, sf[:], [sfb], [sbb])
        return sb3, sbb

    def stage_load(src_ap3, nA):
        i = stage_load_issue(src_ap3, nA)
        return stage_load_cast(i, nA)

    def convert_finish(i, nA, nB, unit_names):
        sb3, sbb = stage_load_cast(i, nA)
        hb = nB // 2
        for q, un in enumerate(unit_names):
            dst = wsc[units[un]].rearrange("p (a b) -> p a b", a=nA)
            (DMA2 if cast_mode["pool_only"] else DMA)([sbb], [B_wsc[units[un]]], out=dst, in_=sb3[:, :, q * hb:(q + 1) * hb])

    def convert(src_ap3, nA, nB, unit_names):
        i = stage_load_issue(src_ap3, nA)
        convert_finish(i, nA, nB, unit_names)

    def kview(w, c0, n):
        return w[:, c0:c0 + n].rearrange("(kc p) n -> p kc n", p=128)

    def conv_win(nm, off, i):
        convert(kview(w_in, off + i * 512, 512), KC, 512, ["%s%d" % (nm, 2 * i), "%s%d" % (nm, 2 * i + 1)])

    def gen_convert_first():
        for hp in range(2):
            conv_win("K", OFF_K, hp)
            yield
        for h in range(3):
            convert(kview(w_in, OFF_V + h * 512, 512), KC, 512, ["V%d_0" % h, "V%d_1" % h])
            yield

    def gen_diag_units():
        for c in range(KC):
            i = nxt("stg", 2)
            sbap, sbb = stg_bap[i], stg_bb[i]
            d3 = sbap[:, 0:2048].rearrange("p (a b) -> p a b", a=16)
            for t in range(16):
                A("activation", [B_const], [sbb], out=d3[:, t, :], in_=ident_f[:], func=AF.Copy,
                  scale=conv_w05[:, c, 2 * t:2 * t + 1])
            DMA2([sbb], [B_wsc[units["DG%d" % c]]], out=wsc[units["DG%d" % c]], in_=sbap[:, 0:2048])
            yield

    def rest_specs():
        sp = [(kview(w_in, OFF_V + 3 * 512, 512), KC, 512, ["V3_0", "V3_1"])]
        for i in range(2):
            for nm, off in (("UG", OFF_UG), ("UV", OFF_UV)):
                sp.append((kview(w_in, off + i * 512, 512), KC, 512, ["%s%d" % (nm, 2 * i), "%s%d" % (nm, 2 * i + 1)]))
        for i in range(2):
            sp.append((kview(w_in, OFF_ZC + i * 512, 512), KC, 512, ["ZC%d" % (2 * i), "ZC%d" % (2 * i + 1)]))
        for hp in range(2):
            sp.append((kview(w_in, OFF_Q + hp * 512, 512), KC, 512, ["Q%d" % (2 * hp), "Q%d" % (2 * hp + 1)]))
        for h in range(H):
            sp.append((kview(w_in, OFF_Z + h * 512, 512), KC, 512, ["Z%d_0" % h, "Z%d_1" % h]))
        for nm, off in (("GA", OFF_GA), ("GB", OFF_GB)):
            for i in range(2):
                sp.append((kview(w_in, off + i * 512, 512), KC, 512, ["%s%d" % (nm, 2 * i), "%s%d" % (nm, 2 * i + 1)]))
        for i in range(2):
            sp.append((kview(w_co, i * 512, 512), KC, 512, ["CO%d" % (2 * i), "CO%d" % (2 * i + 1)]))
        for i in range(4):
            sp.append((kview(w_ro, i * 256, 256), 16, 256, ["RO%d" % (2 * i), "RO%d" % (2 * i + 1)]))
        for i in range(2):
            sp.append((kview(w_o, i * 512, 512), KC, 512, ["WO%d" % (2 * i), "WO%d" % (2 * i + 1)]))
        return sp

    def gen_convert_rest():
        sp = rest_specs()
        slots = [stage_load_issue(sp[0][0], sp[0][1])]
        for i in range(len(sp)):
            if i + 1 < len(sp):
                slots.append(stage_load_issue(sp[i + 1][0], sp[i + 1][1]))
            convert_finish(slots[i], sp[i][1], sp[i][2], sp[i][3])
            yield

    pmod, pmodb = pbank[4], pbuf[4]
    for blk in range(4):
        sb3, sbb = stage_load(kview(w_ada, blk * 512, 512), KC)
        for jj in range(4):
            j = blk * 4 + jj
            for kc in range(KC):
                PE("matmul", [sbb, B_c2], [pmodb], out=pmod[:, j:j + 1], lhsT=sb3[:, kc, jj * 128:(jj + 1) * 128],
                   rhs=cT_b[:, kc:kc + 1], start=(kc == 0), stop=(kc == KC - 1))
    V("tensor_tensor", [pmodb, B_const], [B_c2], out=shiftT[:], in0=pmod[:, 0:8], in1=b_adaT[:, 0:8], op=ALU.add)
    V("tensor_tensor", [pmodb, B_const], [B_c2], out=gs[:], in0=pmod[:, 8:16], in1=b_adaT[:, 8:16], op=ALU.add)
    V("scalar_tensor_tensor", [B_c2, B_const], [B_c2], out=gs[:], in0=gs[:], scalar=1.0, in1=pre_gT[:],
      op0=ALU.add, op1=ALU.mult)
    def gen_gate_row():
        for blk in range(2):
            sb3, sbb = stage_load(kview(w_ada, 2048 + blk * 512, 512), KC)
            pg, pgb = pring()
            for kc in range(KC):
                PE("matmul", [sbb, B_c2], [pgb], out=pg[:], lhsT=cbc[:, kc, :], rhs=sb3[:, kc, :],
                   start=(kc == 0), stop=(kc == KC - 1))
            sl = slice(blk * 512, (blk + 1) * 512)
            V("tensor_tensor", [pgb, bgate_b], [bgate_b], out=bgate[:, sl], in0=pg[:], in1=bgate[:, sl], op=ALU.add)
            V("tensor_tensor", [bgate_b, B_const], [B_pgg], out=pgg[:, sl], in0=pgg[:, sl], in1=bgate[:, sl], op=ALU.mult)
            yield

    B_pgg = Buf("pgg")

    ac = stg_f[0]
    ac_cb = [Buf("ac%d" % c) for c in range(KC)]
    lnt = stg_f[1]
    lnt_b = Buf("lnt")
    lns = [lnt[:, (6 + i) * T:(7 + i) * T] for i in range(2)]
    lns_b = [Buf("lns%d" % i) for i in range(2)]
    szT3 = [stg_b0[:, i * 2048:(i + 1) * 2048].rearrange("p (a b) -> p a b", a=4) for i in range(2)]
    szT_b = [Buf("szT0"), Buf("szT1")]
    vv = [vvbuf[:, i * 2048:(i + 1) * 2048].rearrange("p (n e) -> p n e", n=NT) for i in range(2)]
    vv_b = [[Buf("vv%d_%d" % (i, n)) for n in range(NT)] for i in range(2)]
    rT_b = [Buf("rT%d" % i) for i in range(H)]
    bT = sb("bT", [128, KC, T], BF16)
    bT_b = [Buf("bT%d" % c) for c in range(KC)]
    hT = sb("hT", [128, KC, T], BF16)
    hT_b = [Buf("hT%d" % c) for c in range(KC)]
    aT = sb("aT", [128, KC, HALO + T], BF16)
    aT_b = [Buf("aT%d" % c) for c in range(KC)]
    mT3 = aT[:, :, HALO:HALO + T]
    NXS = 2
    xs = [sb("xs%d" % i, [128, D], F32) for i in range(NXS)]
    xs_b = [Buf("xs%d" % i) for i in range(NXS)]
    xn = [sb("xn%d" % i, [128, D], BF16) for i in range(4)]
    xn_b = [Buf("xn%d" % i) for i in range(4)]
    st_f = sb("st_f", [128, H * 2, 512], F32)
    st_bf = sb("st_bf", [128, H * 2, 512], BF16)
    st_fb = [Buf("stf%d" % i) for i in range(H * 2)]
    st_bb = [Buf("stb%d" % i) for i in range(H * 2)]
    cos_t = sb("cos_t", [128, T], F32)
    sin_t = sb("sin_t", [128, T], F32)
    tab_b = Buf("tab")
    rtmp = [sb("rtmp%d" % i, [128, T], F32) for i in range(4)]
    rtmp_b = [Buf("rtmp%d" % i) for i in range(4)]
    qT = [sb("qT%d" % i, [128, 2, T], BF16) for i in range(2)]
    kT = [sb("kT%d" % i, [128, 2, T], BF16) for i in range(2)]
    kz = [sb("kz%d" % i, [128, NT, 256], BF16) for i in range(2)]
    qT_b = [[Buf("qT%d_%d" % (i, d)) for d in range(2)] for i in range(2)]
    kT_b = [[Buf("kT%d_%d" % (i, d)) for d in range(2)] for i in range(2)]
    kz_b = [[Buf("kz%d_%d" % (i, n)) for n in range(NT)] for i in range(2)]
    PT = [sb("PT%d" % i, [128, 128], BF16) for i in range(4)]
    PT_b = [Buf("PT%d" % i) for i in range(4)]
    on = [sb("on%d" % i, [128, 512], BF16) for i in range(2)]
    on_b = [Buf("on%d" % i) for i in range(2)]
    sm = [sb("sm%d" % i, [128, 16], F32) for i in range(4)]
    smi = [sb("smi%d" % i, [128, 2], I32) for i in range(4)]
    sm_b = [Buf("sm%d" % i) for i in range(4)]
    NDIAG = 30
    diag = [sb("diag%d" % i, [128, 128], BF16) for i in range(NDIAG)]
    diag_b = [Buf("diag%d" % i) for i in range(NDIAG)]
    acb16 = [sb("acb%d" % i, [128, T], BF16) for i in range(2)]
    sqb16 = [sb("sqb%d" % i, [128, T], BF16) for i in range(2)]
    acb16_b = [Buf("acb%d" % i) for i in range(2)]
    sqb16_b = [Buf("sqb%d" % i) for i in range(2)]
    szc = [sb("szc%d" % i, [128, T], BF16) for i in range(2)]
    szc_b = [Buf("szc%d" % i) for i in range(2)]
    print("sbuf bytes remaining/partition:", nc.sbuf_bytes_remaining)

    bar_t = sb("bar_t", [128, 1], F32)

    def staging_barrier():
        bar_w = ac_cb + [lnt_b] + szT_b + lns_b + rT_b
        V("memset", [], stg_fb + stg_bb + bar_w, ap=bar_t[:], constant=0.0)

    def passA_units(last):
        s = []
        for h in range(H):
            s += ["K%d" % h, "V%d_0" % h, "V%d_1" % h]
        if last:
            for i in range(4):
                s += ["UG%d" % i, "UV%d" % i]
        return s

    def main_units():
        s = []
        for i in range(4):
            s += ["UG%d" % i, "UV%d" % i]
        s += ["ZC%d" % i for i in range(4)]
        for h in range(H):
            s += ["Q%d" % h, "K%d" % h, "V%d_0" % h, "V%d_1" % h, "Z%d_0" % h, "Z%d_1" % h]
        for i in range(4):
            s += ["GA%d" % i, "GB%d" % i, "CO%d" % i, "RO%d" % (2 * i), "RO%d" % (2 * i + 1)]
        s += ["WO%d" % i for i in range(4)] * 2
        return s

    useq = []
    wst = {"cons": 0, "loaded": 0}
    LOOK = NSLOT - 1
    slot_free = list(range(NSLOT))
    slot_of = {}
    held = {}

    def _load(j):
        sl = slot_free.pop(0)
        slot_of[j] = sl
        DMA([B_wsc[units[useq[j]]]], [wslot_b[sl]], out=wslot[sl][:], in_=wsc[units[useq[j]]])

    def _prefetch():
        while wst["loaded"] < len(useq) and slot_free and wst["loaded"] <= wst["cons"] + LOOK:
            _load(wst["loaded"])
            wst["loaded"] += 1

    def release(stream):
        if S.dry:
            return
        j = held.pop(stream, None)
        if j is not None:
            slot_free.append(slot_of.pop(j))
            _prefetch()

    def acquire(name, stream="M"):
        if S.dry:
            useq.append(name)
            return wslot[0], wslot_b[0]
        release(stream)
        i = wst["cons"]
        assert useq[i] == name, (i, useq[i], name)
        if i >= wst["loaded"]:
            assert wst["loaded"] == i and slot_free
            _load(i)
            wst["loaded"] += 1
        wst["cons"] += 1
        held[stream] = i
        _prefetch()
        sl = slot_of[i]
        return wslot[sl], wslot_b[sl]

    xseq = [("prev", t) for t in range(NBLK * NT)] + [("own", t) for t in range(NBLK * NT)]
    xst = {"cons": 0, "loaded": 0}

    def acquire_x():
        i = xst["cons"]
        xst["cons"] += 1
        while xst["loaded"] < len(xseq) and xst["loaded"] <= i + (NXS - 1):
            j = xst["loaded"]
            which, t = xseq[j]
            src = (x_prev if which == "prev" else x_own)[t * 128:(t + 1) * 128, :]
            DMA([], [xs_b[j % NXS]], out=xs[j % NXS][:], in_=src)
            xst["loaded"] += 1
        return xs[i % NXS], xs_b[i % NXS]

    def rsqrt(a_ap, y_ap, t_ap, i_ap, bufs, scalar=True):
        V("tensor_scalar", bufs, bufs, out=i_ap, in0=a_ap.bitcast(I32), scalar1=1, scalar2=None, op0=ALU.arith_shift_right)
        V("tensor_scalar", bufs, bufs, out=i_ap, in0=i_ap, scalar1=-1, scalar2=RSQRT_MAGIC, op0=ALU.mult, op1=ALU.add)
        cur = i_ap.bitcast(F32)
        for it in range(NEWTON_ITERS):
            if scalar:
                V("scalar_tensor_tensor", bufs, bufs, out=t_ap, in0=cur, scalar=a_ap, in1=cur, op0=ALU.mult, op1=ALU.mult)
            else:
                V("tensor_tensor", bufs, bufs, out=t_ap, in0=cur, in1=cur, op=ALU.mult)
                V("tensor_tensor", bufs, bufs, out=t_ap, in0=t_ap, in1=a_ap, op=ALU.mult)
            V("tensor_scalar", bufs, bufs, out=t_ap, in0=t_ap, scalar1=-0.5, scalar2=1.5, op0=ALU.mult, op1=ALU.add)
            V("tensor_tensor", bufs, bufs, out=y_ap, in0=cur, in1=t_ap, op=ALU.mult)
            cur = y_ap

    def pre_elem():
        for n in range(NT):
            xt, xtb = acquire_x()
            si = nxt("sm", 4)
            smt, smb = sm[si], sm_b[si]
            A("activation", [xtb], [xn_b[n], smb], out=xn[n][:], in_=xt[:], func=AF.Square, accum_out=smt[:, 0:1])
            V("tensor_scalar", [smb], [smb], out=smt[:, 1:2], in0=smt[:, 0:1], scalar1=1.0 / D, scalar2=RMS_EPS,
              op0=ALU.mult, op1=ALU.add)
            rsqrt(smt[:, 1:2], smt[:, 2:3], smt[:, 3:4], smi[si][:, 0:1], [smb])
            A("activation", [xtb, smb], [xn_b[n]], out=xn[n][:], in_=xt[:], func=AF.Copy, scale=smt[:, 2:3])

    def pre_pe():
        for half in range(2):
            tl = [half * 2, half * 2 + 1]
            for kc2 in range(KC // 2):
                pt, ptb = ptring()
                for q in range(2):
                    kc = kc2 * 2 + q
                    for n2 in range(2):
                        PE("transpose", [xn_b[tl[n2]], B_c2], [ptb],
                           out=pt[:, (q * 2 + n2) * 128:(q * 2 + n2 + 1) * 128],
                           in_=xn[tl[n2]][:, kc * 128:(kc + 1) * 128], identity=ident_b[:])
                for q in range(2):
                    kc = kc2 * 2 + q
                    A("activation", [ptb, B_c2], [hT_b[kc]], out=hT[:, kc, half * 256:(half + 1) * 256],
                      in_=pt[:, q * 256:(q + 1) * 256], func=AF.Identity, bias=shiftT[:, kc:kc + 1], scale=gs[:, kc:kc + 1])

    def stage_tables(pos_ap, t0):
        tA, tA_b = rtmp[0], rtmp_b[0]
        tB, tB_b = rtmp[1], rtmp_b[1]
        posi, posi_b = rtmp[2][:].bitcast(I32), rtmp_b[2]
        DMA([], [posi_b], out=posi, in_=pos_ap[0:1, t0:t0 + T].to_broadcast([128, T]))
        V("tensor_copy", [posi_b], [tA_b], out=tA[:], in_=posi)
        V("tensor_scalar", [tA_b, B_const], [tA_b], out=tA[:], in0=tA[:], scalar1=invf[:, 0:1], scalar2=None, op0=ALU.mult)
        V("tensor_scalar", [tA_b], [posi_b], out=posi, in0=tA[:], scalar1=1.0 / TWO_PI, scalar2=None, op0=ALU.mult)
        V("tensor_copy", [posi_b], [tB_b], out=tB[:], in_=posi)
        V("scalar_tensor_tensor", [tA_b, tB_b], [tA_b], out=tA[:], in0=tB[:], scalar=-C1, in1=tA[:], op0=ALU.mult, op1=ALU.add)
        V("scalar_tensor_tensor", [tA_b, tB_b], [tA_b], out=tA[:], in0=tB[:], scalar=-C2, in1=tA[:], op0=ALU.mult, op1=ALU.add)
        V("tensor_scalar", [tA_b], [tA_b], out=tA[:], in0=tA[:], scalar1=PI_LO, scalar2=-PI_LO, op0=ALU.min, op1=ALU.max)
        A("activation", [tA_b], [tab_b], out=sin_t[:], in_=tA[:], func=AF.Sin)
        A("activation", [tA_b], [tB_b], out=tB[:], in_=tA[:], func=AF.Sin, scale=0.5)
        V("tensor_tensor", [tB_b], [tB_b], out=tB[:], in0=tB[:], in1=tB[:], op=ALU.mult)
        V("tensor_scalar", [tB_b], [tab_b], out=cos_t[:], in0=tB[:], scalar1=-2.0, scalar2=1.0, op0=ALU.mult, op1=ALU.add)

    def proj_fm(unit_t, col, nkc, rhs_fn, rhs_bufs, unit_buf):
        pb, pbb = pring()
        u3 = unit_t[:].rearrange("p (a b) -> p a b", a=nkc)
        for kc in range(nkc):
            PE("matmul", [unit_buf, rhs_bufs[kc]], [pbb], out=pb[:], lhsT=u3[:, kc, col:col + 128], rhs=rhs_fn(kc),
               start=(kc == 0), stop=(kc == nkc - 1))
        return pb, pbb

    def hT_rhs(kc):
        return hT[:, kc, :]

    def rotary(pa, pab, pb_, pbb_, dst, dstb, si):
        i0, i1 = nxt("rtmp", 4), nxt("rtmp", 4)
        V("tensor_tensor", [pab, tab_b], [rtmp_b[i0]], out=rtmp[i0][:], in0=pa[:], in1=cos_t[:], op=ALU.mult)
        V("tensor_tensor", [pbb_, tab_b], [rtmp_b[i1]], out=rtmp[i1][:], in0=pb_[:], in1=sin_t[:], op=ALU.mult)
        V("tensor_tensor", [rtmp_b[i0], rtmp_b[i1]], [dstb[si][0]], out=dst[si][:, 0, :], in0=rtmp[i0][:], in1=rtmp[i1][:],
          op=ALU.subtract)
        i2, i3 = nxt("rtmp", 4), nxt("rtmp", 4)
        V("tensor_tensor", [pab, tab_b], [rtmp_b[i2]], out=rtmp[i2][:], in0=pa[:], in1=sin_t[:], op=ALU.mult)
        V("tensor_tensor", [pbb_, tab_b], [rtmp_b[i3]], out=rtmp[i3][:], in0=pb_[:], in1=cos_t[:], op=ALU.mult)
        V("tensor_tensor", [rtmp_b[i2], rtmp_b[i3]], [dstb[si][1]], out=dst[si][:, 1, :], in0=rtmp[i2][:], in1=rtmp[i3][:],
          op=ALU.add)

    def head_kv(h, si, with_q, stream="M"):
        if with_q:
            u, ub = acquire("Q%d" % h, stream)
            pa, pab = proj_fm(u, 0, KC, hT_rhs, hT_b, ub)
            pb_, pbb_ = proj_fm(u, 128, KC, hT_rhs, hT_b, ub)
            rotary(pa, pab, pb_, pbb_, qT, qT_b, si)
            yield
        u, ub = acquire("K%d" % h, stream)
        pa, pab = proj_fm(u, 0, KC, hT_rhs, hT_b, ub)
        pb_, pbb_ = proj_fm(u, 128, KC, hT_rhs, hT_b, ub)
        rotary(pa, pab, pb_, pbb_, kT, kT_b, si)
        for n2 in range(2):
            pt, ptb = ptring()
            for nn in range(2):
                n = n2 * 2 + nn
                for dc in range(2):
                    PE("transpose", [kT_b[si][dc], B_c2], [ptb], out=pt[:, (nn * 2 + dc) * 128:(nn * 2 + dc + 1) * 128],
                       in_=kT[si][:, dc, n * 128:(n + 1) * 128], identity=ident_b[:])
            for nn in range(2):
                n = n2 * 2 + nn
                A("activation", [ptb, B_const], [kz_b[si][n]], out=kz[si][:, n, :], in_=pt[:, nn * 256:(nn + 1) * 256],
                  func=AF.Copy, scale=zs[:, h:h + 1])
        yield
        for half in range(2):
            u, ub = acquire("V%d_%d" % (h, half), stream)
            u3 = u[:].rearrange("p (a b) -> p a b", a=KC)
            for n in range(NT):
                pb, pbb = pring()
                for kc in range(KC):
                    PE("matmul", [ub, hT_b[kc]], [pbb], out=pb[:, 0:256], lhsT=hT[:, kc, n * 128:(n + 1) * 128],
                       rhs=u3[:, kc, :], start=(kc == 0), stop=(kc == KC - 1))
                dst = vv[si][:, n, half * 256:(half + 1) * 256]
                A("activation", [pbb], [vv_b[si][n]], out=dst, in_=pb[:, 0:256], func=AF.Copy)
        yield

    def state_update(h, si, n, need_bf):
        for dc in range(2):
            pb, pbb = pring()
            PE("matmul", [kz_b[si][n], vv_b[si][n]], [pbb], out=pb[:], lhsT=kz[si][:, n, dc * 128:(dc + 1) * 128],
               rhs=vv[si][:, n, :], start=True, stop=True)
            j = h * 2 + dc
            V("scalar_tensor_tensor", [pbb, st_fb[j]], [st_fb[j]], out=st_f[:, j, :], in0=st_f[:, j, :], scalar=gC[h],
              in1=pb[:], op0=ALU.mult, op1=ALU.add)
            if need_bf:
                A("activation", [st_fb[j]], [st_bb[j]], out=st_bf[:, j, :], in_=st_f[:, j, :], func=AF.Copy)

    def stage_glu(stream="M"):
        for i in range(4):
            ug, ugb = acquire("UG%d" % i, stream)
            tg = []
            for q in range(2):
                pg, pgb = proj_fm(ug, q * 128, KC, hT_rhs, hT_b, ugb)
                li = nxt("lns", 2)
                A("activation", [pgb], [lns_b[li]], out=lns[li], in_=pg[:], func=AF.Tanh, scale=0.5)
                tg.append(li)
                yield
            uv, uvb = acquire("UV%d" % i, stream)
            for q in range(2):
                c = i * 2 + q
                pv, pvb = proj_fm(uv, q * 128, KC, hT_rhs, hT_b, uvb)
                li = tg[q]
                V("scalar_tensor_tensor", [pvb, lns_b[li]], [aT_b[c]], out=aT[:, c, HALO:HALO + T], in0=lns[li], scalar=1.0,
                  in1=pv[:], op0=ALU.add, op1=ALU.mult)
                yield
        release(stream)

    def halo_shift(use_flag):
        for c in range(KC):
            if use_flag:
                V("tensor_scalar", [aT_b[c], B_const], [aT_b[c]], out=aT[:, c, 0:HALO], in0=aT[:, c, T:T + HALO],
                  scalar1=flag[:, 0:1], scalar2=None, op0=ALU.mult)
            else:
                V("tensor_copy", [aT_b[c]], [aT_b[c]], out=aT[:, c, 0:HALO], in_=aT[:, c, T:T + HALO])

    def stage_conv(stream="M"):
        L = lambda i: lnt[:, i * T:(i + 1) * T]
        def gen_odd(c):
            tl = {}
            for k in range(1, CONVW, 2):
                di = nxt("diag", NDIAG)
                A("activation", [B_const], [diag_b[di]], out=diag[di][:], in_=ident_f[:], func=AF.Copy,
                  scale=conv_w05[:, c, k:k + 1])
                tl[k] = di
            return tl

        odd_next = gen_odd(0)
        pend_stats = None
        for c in range(KC):
            odd = odd_next
            if c + 1 < KC:
                odd_next = gen_odd(c + 1)
            dg, dgb = acquire("DG%d" % c, stream)
            dg3 = dg[:].rearrange("p (a b) -> p a b", a=16)
            pb, pbb = pring()
            for k in range(CONVW):
                if k % 2 == 0:
                    lhs, lb_ = dg3[:, k // 2, :], dgb
                else:
                    lhs, lb_ = diag[odd[k]][:], diag_b[odd[k]]
                PE("matmul", [lb_, aT_b[c]], [pbb], out=pb[:], lhsT=lhs, rhs=aT[:, c, k:k + T],
                   start=(k == 0), stop=(k == CONVW - 1))
            acs = ac[:, c * T:(c + 1) * T]
            A("activation", [pbb, B_const], [ac_cb[c]], out=acs, in_=pb[:], func=AF.Identity, bias=conv_bT[:, c:c + 1])
            ai = nxt("acb", 2)
            A("activation", [ac_cb[c]], [acb16_b[ai]], out=acb16[ai][:], in_=acs, func=AF.Copy)
            A("activation", [ac_cb[c]], [sqb16_b[ai]], out=sqb16[ai][:], in_=acs, func=AF.Square)
            def stats(c=c, ai=ai):
                ps_, psb_ = pring()
                PE("matmul", [acb16_b[ai], B_c2], [psb_], out=ps_[:], lhsT=ones_b[:], rhs=acb16[ai][:], start=True, stop=True)
                pq_, pqb_ = pring()
                PE("matmul", [sqb16_b[ai], B_c2], [pqb_], out=pq_[:], lhsT=ones_b[:], rhs=sqb16[ai][:], start=True, stop=True)
                if c == 0:
                    V("tensor_copy", [psb_], [lnt_b], out=L(0), in_=ps_[:])
                    V("tensor_copy", [pqb_], [lnt_b], out=L(1), in_=pq_[:])
                else:
                    V("tensor_tensor", [psb_, lnt_b], [lnt_b], out=L(0), in0=L(0), in1=ps_[:], op=ALU.add)
                    V("tensor_tensor", [pqb_, lnt_b], [lnt_b], out=L(1), in0=L(1), in1=pq_[:], op=ALU.add)
            if pend_stats is not None:
                pend_stats()
            pend_stats = stats
            yield
        pend_stats()
        mean, var, rstd, tmp, mr = L(0), L(1), L(2), L(3), L(4)
        lni = L(5).bitcast(I32)
        lb = [lnt_b]
        V("tensor_scalar", lb, lb, out=mean, in0=mean, scalar1=1.0 / D, scalar2=None, op0=ALU.mult)
        V("tensor_scalar", lb, lb, out=var, in0=var, scalar1=1.0 / D, scalar2=LN_EPS, op0=ALU.mult, op1=ALU.add)
        V("tensor_tensor", lb, lb, out=tmp, in0=mean, in1=mean, op=ALU.mult)
        V("tensor_tensor", lb, lb, out=var, in0=var, in1=tmp, op=ALU.subtract)
        rsqrt(var, rstd, tmp, lni, lb, scalar=False)
        V("tensor_tensor", lb, lb, out=mr, in0=mean, in1=rstd, op=ALU.mult)
        for i in range(4):
            zc, zcb = acquire("ZC%d" % i, stream)
            for q in range(2):
                c = i * 2 + q
                acs = ac[:, c * T:(c + 1) * T]
                pz, pzb = proj_fm(zc, q * 128, KC, hT_rhs, hT_b, zcb)
                zi = nxt("szc", 2)
                A("activation", [pzb], [szc_b[zi]], out=szc[zi][:], in_=pz[:], func=AF.Silu)
                li = nxt("lns", 2)
                V("tensor_tensor", [ac_cb[c], lnt_b], [lns_b[li]], out=lns[li], in0=acs, in1=rstd, op=ALU.mult)
                V("tensor_tensor", [lns_b[li], lnt_b], [lns_b[li]], out=lns[li], in0=lns[li], in1=mr, op=ALU.subtract)
                A("activation", [lns_b[li], B_const], [lns_b[li]], out=lns[li], in_=lns[li], func=AF.Silu,
                  bias=ln_bT[:, c:c + 1], scale=ln_gT[:, c:c + 1])
                V("tensor_tensor", [lns_b[li], szc_b[zi]], [bT_b[c]], out=bT[:, c, :], in0=lns[li], in1=szc[zi][:], op=ALU.mult)
                yield
        release(stream)

    def head_proj(h, si, stream="M"):
        for _ in head_kv(h, si, True, stream):
            yield
        for half in range(2):
            u, ub = acquire("Z%d_%d" % (h, half), stream)
            for q in range(2):
                ech = half * 2 + q
                pz, pzb = proj_fm(u, q * 128, KC, hT_rhs, hT_b, ub)
                A("activation", [pzb], [szT_b[si]], out=szT3[si][:, ech, :], in_=pz[:], func=AF.Silu)
        yield

    def ret_scores(h, si):
        pis = []
        for n in range(NT):
            tok = slice(n * 128, (n + 1) * 128)
            pS, pSb = pring()
            for dc in range(2):
                PE("matmul", [kT_b[si][dc], qT_b[si][dc]], [pSb], out=pS[:, 0:128], lhsT=kT[si][:, dc, tok],
                   rhs=qT[si][:, dc, tok], start=(dc == 0), stop=(dc == 1))
            pi = nxt("PT", 4)
            V("tensor_tensor", [pSb, B_const], [PT_b[pi]], out=PT[pi][:], in0=pS[:, 0:128], in1=maskT[:, h, :], op=ALU.mult)
            pis.append(pi)
        return pis

    def ret_chunk(h, si, n, pi):
        tok = slice(n * 128, (n + 1) * 128)
        oi_ = nxt("pO", 2)
        pO, pOb = pbank[4 + oi_], pbuf[4 + oi_]
        for dc in range(2):
            j = h * 2 + dc
            PE("matmul", [qT_b[si][dc], st_bb[j]], [pOb], out=pO[:], lhsT=qT[si][:, dc, tok], rhs=st_bf[:, j, :],
               start=(dc == 0), stop=False)
        PE("matmul", [PT_b[pi], vv_b[si][n]], [pOb], out=pO[:], lhsT=PT[pi][:], rhs=vv[si][:, n, :], start=False, stop=True)
        state_update(h, si, n, True)
        mi = nxt("sm", 4)
        smt, smb = sm[mi], sm_b[mi]
        V("bn_stats", [pOb], [smb], out=smt[:, 0:6], in_=pO[:])
        V("bn_aggr", [smb], [smb], out=smt[:, 6:8], in_=smt[:, 0:6])
        V("tensor_tensor", [smb, B_const], [smb], out=smt[:, 8:9], in0=smt[:, 7:8], in1=epsp[:, h:h + 1], op=ALU.add)
        rsqrt(smt[:, 8:9], smt[:, 9:10], smt[:, 10:11], smi[mi][:, 0:1], [smb])
        V("scalar_tensor_tensor", [smb], [smb], out=smt[:, 11:12], in0=smt[:, 6:7], scalar=-1.0, in1=smt[:, 9:10],
          op0=ALU.mult, op1=ALU.mult)
        oi = nxt("on", 2)
        A("activation", [pOb, smb], [on_b[oi]], out=on[oi][:], in_=pO[:], func=AF.Identity, bias=smt[:, 11:12],
          scale=smt[:, 9:10])

        def tail():
            pt, ptb = ptring()
            for ech in range(4):
                PE("transpose", [on_b[oi], B_c2], [ptb], out=pt[:, ech * 128:(ech + 1) * 128],
                   in_=on[oi][:, ech * 128:(ech + 1) * 128], identity=ident_b[:])
            V("tensor_tensor", [ptb, szT_b[si]], [rT_b[h]], out=rT[:, h * 4:(h + 1) * 4, tok],
              in0=pt.rearrange("p (a b) -> p a b", a=4), in1=szT3[si][:, :, tok], op=ALU.mult)
        return tail

    def stage_ret(stream="M"):
        g = head_proj(0, 0, stream)
        for _ in g:
            yield
        pending = None
        for h in range(H):
            gn = head_proj(h + 1, (h + 1) % 2, stream) if h + 1 < H else None
            pis = ret_scores(h, h % 2)
            for n in range(NT):
                if gn is not None:
                    next(gn, None)
                    yield
                t = ret_chunk(h, h % 2, n, pis[n])
                if pending is not None:
                    pending()
                pending = t
                yield
            if gn is not None:
                for _ in gn:
                    yield
        pending()
        release(stream)

    def run(g):
        for _ in g:
            pass

    def interleave(ga, gb_):
        a_live, b_live = True, True
        while a_live or b_live:
            if a_live:
                try:
                    next(ga)
                except StopIteration:
                    a_live = False
            if b_live:
                try:
                    next(gb_)
                except StopIteration:
                    b_live = False

    def conv_stream():
        for _ in stage_glu("X"):
            yield
        for _ in stage_conv("X"):
            yield
        halo_shift(False)

    def stage_merge():
        for i in range(4):
            ga, gab = acquire("GA%d" % i)
            ta = []
            for q in range(2):
                pg, pgb = proj_fm(ga, q * 128, KC, hT_rhs, hT_b, gab)
                li = nxt("rtmp", 4)
                A("activation", [pgb], [rtmp_b[li]], out=rtmp[li][:], in_=pg[:], func=AF.Tanh, scale=0.5)
                ta.append(li)
            gb_, gbb = acquire("GB%d" % i)
            tb = []
            for q in range(2):
                pg, pgb = proj_fm(gb_, q * 128, KC, hT_rhs, hT_b, gbb)
                li = nxt("rtmp", 4)
                A("activation", [pgb], [rtmp_b[li]], out=rtmp[li][:], in_=pg[:], func=AF.Tanh, scale=0.5)
                tb.append(li)
            co, cob = acquire("CO%d" % i)
            for q in range(2):
                py, pyb = proj_fm(co, q * 128, KC, lambda kc: bT[:, kc, :], bT_b, cob)
                li = tb[q]
                V("scalar_tensor_tensor", [pyb, rtmp_b[li]], [rtmp_b[li]], out=rtmp[li][:], in0=rtmp[li][:], scalar=1.0,
                  in1=py[:], op0=ALU.add, op1=ALU.mult)
            for q in range(2):
                dch = i * 2 + q
                ro, rob = acquire("RO%d" % dch)
                py, pyb = proj_fm(ro, 0, 16, lambda ec: rT[:, ec, :], [rT_b[ec // 4] for ec in range(16)], rob)
                la, lb_ = ta[q], tb[q]
                V("scalar_tensor_tensor", [pyb, rtmp_b[la]], [rtmp_b[la]], out=rtmp[la][:], in0=rtmp[la][:], scalar=1.0,
                  in1=py[:], op0=ALU.add, op1=ALU.mult)
                V("tensor_tensor", [rtmp_b[la], rtmp_b[lb_]], [aT_b[dch]], out=mT3[:, dch, :], in0=rtmp[la][:],
                  in1=rtmp[lb_][:], op=ALU.add)

    B_out = Buf("outd")

    def stage_out(blk, between=None):
        for tp in range(2):
            if tp == 1 and between is not None:
                between()
            tiles = (tp * 2, tp * 2 + 1)
            banks = {n: (pring(), pring()) for n in tiles}
            for ui in range(4):
                u, ub = acquire("WO%d" % ui)
                u3 = u[:].rearrange("p (a b) -> p a b", a=KC)
                cs = (ui % 2) * 256
                for n in tiles:
                    pb, pbb = banks[n][ui // 2]
                    for kc in range(KC):
                        PE("matmul", [ub, aT_b[kc]], [pbb], out=pb[:, cs:cs + 256], lhsT=mT3[:, kc, n * 128:(n + 1) * 128],
                           rhs=u3[:, kc, :], start=(kc == 0), stop=(kc == KC - 1))
            for n in tiles:
                (pb0, pbb0), (pb1, pbb1) = banks[n]
                t = blk * NT + n
                xi_ = t % 2
                DMA2([], [xr_b[xi_]], out=xr[xi_][:], in_=x_own[t * 128:(t + 1) * 128, :])
                mi = nxt("sm", 4)
                smt, smb = sm[mi], sm_b[mi]
                j0, j1 = nxt("rtmp", 4), nxt("rtmp", 4)
                A("activation", [pbb0], [rtmp_b[j0], smb], out=rtmp[j0][:], in_=pb0[:], func=AF.Square, accum_out=smt[:, 0:1])
                A("activation", [pbb1], [rtmp_b[j1], smb], out=rtmp[j1][:], in_=pb1[:], func=AF.Square, accum_out=smt[:, 1:2])
                V("tensor_tensor", [smb], [smb], out=smt[:, 2:3], in0=smt[:, 0:1], in1=smt[:, 1:2], op=ALU.add)
                V("tensor_scalar", [smb], [smb], out=smt[:, 3:4], in0=smt[:, 2:3], scalar1=1.0 / D, scalar2=4.0 * RMS_EPS,
                  op0=ALU.mult, op1=ALU.add)
                rsqrt(smt[:, 3:4], smt[:, 4:5], smt[:, 5:6], smi[mi][:, 0:1], [smb])
                for hf, (pb, pbb) in enumerate(((pb0, pbb0), (pb1, pbb1))):
                    cs = slice(hf * 512, (hf + 1) * 512)
                    li = nxt("lns", 2)
                    V("scalar_tensor_tensor", [pbb, smb, B_pgg], [lns_b[li]], out=lns[li], in0=pb[:], scalar=smt[:, 4:5],
                      in1=pgg[:, cs], op0=ALU.mult, op1=ALU.mult)
                    V("tensor_tensor", [lns_b[li], xr_b[xi_]], [xr_b[xi_]], out=xr[xi_][:, cs], in0=xr[xi_][:, cs],
                      in1=lns[li], op=ALU.add)
                DMA2([xr_b[xi_]], [B_out], out=out_d[t * 128:(t + 1) * 128, :], in_=xr[xi_][:])

    def body():
        for j in range(H * 2):
            V("memset", [], [st_fb[j]], ap=st_f[:, j, :], constant=0.0)
        for c in range(KC):
            V("memset", [], [aT_b[c]], ap=aT[:, c, :], constant=0.0)
        cast_mode["pool_only"] = False
        run(gen_convert_first())
        gc = gen_convert_rest()
        cast_mode["pool_only"] = True
        gd = gen_diag_units()
        if NBLK == 1:
            run(gc)
            run(gd)
            run(gen_gate_row())
            staging_barrier()
        pre_elem()
        pre_pe()
        stage_tables(pos_prev, 0)
        for blk in range(NBLK):
            run(head_kv(0, 0, False))
            for h in range(H):
                si = h % 2
                if NBLK > 1 and blk < NBLK - 1:
                    next(gc, None)
                    next(gc, None)
                if h + 1 < H:
                    run(head_kv(h + 1, (h + 1) % 2, False))
                if h == 1:
                    pre_elem()
                if h == 2 and blk + 1 < NBLK:
                    pre_pe()
                    stage_tables(pos_prev, (blk + 1) * T)
                for n in range(NT):
                    state_update(h, si, n, False)
            if NBLK > 1 and blk == NBLK - 2:
                run(gc)
                run(gd)
                run(gen_gate_row())
                staging_barrier()
            if blk == NBLK - 1:
                run(stage_glu())
                halo_shift(True)
        for j in range(H * 2):
            V("tensor_scalar", [st_fb[j], B_const], [st_fb[j]], out=st_f[:, j, :], in0=st_f[:, j, :], scalar1=flag[:, 0:1],
              scalar2=None, op0=ALU.mult)
            A("activation", [st_fb[j]], [st_bb[j]], out=st_bf[:, j, :], in_=st_f[:, j, :], func=AF.Copy)
        pre_pe()
        stage_tables(pos_own, 0)
        for blk in range(NBLK):
            release("M")
            interleave(conv_stream(), stage_ret("Y"))
            if blk + 1 < NBLK:
                pre_elem()
            stage_merge()
            if blk + 1 < NBLK:
                def nxt_blk(b=blk + 1):
                    pre_pe()
                    stage_tables(pos_own, b * T)
                stage_out(blk, nxt_blk)
            else:
                stage_out(blk)

    rr_save = dict(rr)
    S.dry = True
    body()
    S.dry = False
    rr.clear()
    rr.update(rr_save)
    xst["cons"] = 0
    xst["loaded"] = 0
    body()
    assert wst["cons"] == len(useq), (wst["cons"], len(useq))

    with ExitStack() as stack:
        nops, nwait = S.emit(stack)
    print("ops", nops, "waits", nwait)
    return nc, nops, nwait


_PROG_CACHE = {}


def kernel(x, c, positions, w_ada, b_ada, pre_norm_g, w_in, conv_w, conv_b, conv_ln_g, conv_ln_b,
           w_ret_out, w_conv_out, w_out, post_norm_g):
    x = np.asarray(x, dtype=np.float32)
    B, S_, _ = x.shape
    half = S_ // 2
    NBLK = half // T
    assert half % T == 0
    ncores = 2 * B
    if NBLK not in _PROG_CACHE:
        _PROG_CACHE[NBLK] = build_program(NBLK)[0]
    nc = _PROG_CACHE[NBLK]
    maskT, zs, epsp, gC, inv_freq = _host_consts()
    c = np.asarray(c, np.float32)
    positions = np.asarray(positions, np.int32)
    f = lambda a: np.ascontiguousarray(np.asarray(a, np.float32))
    lay = lambda v, n: f(np.asarray(v, np.float32).reshape(n, 128).T)
    b_ada0 = np.asarray(b_ada, np.float32)[0]
    shared = {
        "w_ada": f(w_ada[0]), "b_adaT": lay(b_ada0[:2048], 16), "b_gate": f(b_ada0[2048:].reshape(1, D)),
        "pre_gT": lay(pre_norm_g[0], KC), "w_in": f(w_in[0]),
        "conv_wT": f(np.asarray(conv_w[0], np.float32).T.reshape(KC, 128, CONVW).transpose(1, 0, 2).reshape(128, KC * CONVW)),
        "conv_bT": lay(conv_b[0], KC), "ln_gT": lay(conv_ln_g[0], KC), "ln_bT": lay(conv_ln_b[0], KC),
        "w_ret_out": f(w_ret_out[0]), "w_conv_out": f(w_conv_out[0]), "w_out": f(w_out[0]),
        "post_g": f(np.asarray(post_norm_g[0], np.float32).reshape(1, D)),
        "ident": np.eye(128, dtype=np.float32), "maskT": f(maskT.reshape(128, H * 128)), "zs": f(zs), "epsp": f(epsp),
        "inv_freq": f(inv_freq),
    }
    in_maps = []
    for b in range(B):
        for j in range(2):
            own = x[b, j * half:(j + 1) * half]
            prev = x[b, 0:half]
            m = dict(shared)
            m["x_own"] = np.ascontiguousarray(own)
            m["x_prev"] = np.ascontiguousarray(prev)
            m["pos_own"] = np.ascontiguousarray(positions[b, j * half:(j + 1) * half].reshape(1, half))
            m["pos_prev"] = np.ascontiguousarray(positions[b, 0:half].reshape(1, half))
            m["flag"] = np.full((128, 1), float(j), np.float32)
            m["cT"] = lay(c[b], KC)
            in_maps.append(m)
    res = run_bass_kernel_spmd(nc, in_maps, core_ids=list(range(ncores)))
    out = np.empty((B, S_, D), np.float32)
    for b in range(B):
        for j in range(2):
            out[b, j * half:(j + 1) * half] = res.results[b * 2 + j]["out"]
    return out
```

```python
import math
from contextlib import ExitStack

import numpy as np
import concourse.bass as bass
import concourse.mybir as mybir
from concourse.bass_utils import run_bass_kernel_spmd

F32 = mybir.dt.float32
BF16 = mybir.dt.bfloat16
I32 = mybir.dt.int32
AF = mybir.ActivationFunctionType
ALU = mybir.AluOpType

D = 1024
T = 512
NT = 4
KC = 8
H = 4
CONVW = 31
HALO = CONVW - 1
IN_W = 11264
RMS_EPS = 1e-6
GN_EPS = 1e-5
LN_EPS = 1e-5
TWO_PI = 2.0 * math.pi
C1 = 6.28125
C2 = TWO_PI - C1
PI_LO = 3.1415925
RSQRT_MAGIC = 0x5F3759DF
NEWTON_ITERS = 3

OFF_Q, OFF_K, OFF_V, OFF_Z, OFF_UV, OFF_UG, OFF_ZC, OFF_GA, OFF_GB = (
    0, 1024, 2048, 4096, 6144, 7168, 8192, 9216, 10240)


class Buf:
    __slots__ = ("name", "w", "r")

    def __init__(self, name=""):
        self.name = name
        self.w = None
        self.r = []


class Op:
    __slots__ = ("eng", "fn", "deps", "dma", "sig", "sem", "val", "idx")


class Sched:
    ENGS = ("pe", "act", "dve", "pool", "sp")

    def __init__(self, nc, n_dma_sems=32):
        self.nc = nc
        self.ops = []
        self.n_dma_sems = n_dma_sems

    dry = False

    def op(self, eng, fn, reads=(), writes=(), dma=False, **kw):
        if self.dry:
            return None
        o = Op()
        o.eng = eng
        o.fn = (fn, kw)
        o.dma = dma
        o.sig = False
        o.sem = None
        o.val = 0
        o.idx = len(self.ops)
        deps = set()
        for b in reads:
            if b.w is not None:
                deps.add(b.w)
        for b in writes:
            if b.w is not None:
                deps.add(b.w)
            for r in b.r:
                deps.add(r)
        deps.discard(o)
        o.deps = deps
        for b in reads:
            b.r.append(o)
        for b in writes:
            b.w = o
            b.r = []
        self.ops.append(o)
        return o

    def emit(self, stack):
        nc = self.nc
        engh = {"pe": nc.tensor, "act": nc.scalar, "dve": nc.vector, "pool": nc.gpsimd, "sp": nc.sync}
        esem = {e: stack.enter_context(nc.semaphore("s_" + e)) for e in self.ENGS}
        dsem = [stack.enter_context(nc.semaphore("d%d" % i)) for i in range(self.n_dma_sems)]
        for o in self.ops:
            for d in o.deps:
                if d.dma:
                    continue
                if d.eng == o.eng and o.eng in ("pe", "sp") and not o.dma:
                    continue
                d.sig = True
        cnt = {e: 0 for e in self.ENGS}
        NSW = 8
        dsem_sw = [stack.enter_context(nc.semaphore("w%d" % i)) for i in range(NSW)]
        pools = {"sp": (dsem, [0] * self.n_dma_sems, [None] * self.n_dma_sems, [0]),
                 "pool": (dsem_sw, [0] * NSW, [None] * NSW, [0])}
        for o in self.ops:
            if o.dma:
                sems, dcnt_, dlast_, nd_ = pools["pool" if o.eng == "pool" else "sp"]
                k = nd_[0] % len(sems)
                nd_[0] += 1
                if dlast_[k] is not None:
                    o.deps.add(dlast_[k])
                dlast_[k] = o
                dcnt_[k] += 16
                o.sem = sems[k]
                o.val = dcnt_[k]
                o.sig = True
            elif o.sig:
                cnt[o.eng] += 1
                o.sem = esem[o.eng]
                o.val = cnt[o.eng]
        waited = {}
        nwait = 0
        for o in self.ops:
            e = engh[o.eng]
            need = {}
            for d in o.deps:
                if not d.dma and d.eng == o.eng and o.eng in ("pe", "sp") and not o.dma:
                    continue
                if not d.dma and d.eng == o.eng and o.eng == "sp":
                    continue
                key = id(d.sem)
                if key not in need or need[key][1] < d.val:
                    need[key] = (d.sem, d.val)
            pend = []
            for key, (sem, val) in need.items():
                wk = (o.eng, key)
                if waited.get(wk, 0) >= val:
                    continue
                waited[wk] = val
                pend.append((sem, val))
            for sem, val in pend[:-1]:
                e.wait_ge(sem, val)
                nwait += 1
            ins = getattr(e, o.fn[0])(**o.fn[1])
            if pend:
                ins._wait_ge(pend[-1][0], pend[-1][1])
            if o.sig:
                ins.then_inc(o.sem, 16 if o.dma else 1)
        for sems, dcnt_, _, _ in pools.values():
            for k in range(len(sems)):
                if dcnt_[k] > 0:
                    nc.sync.wait_ge(sems[k], dcnt_[k])
        return len(self.ops), nwait


def _host_consts():
    hh = np.arange(H, dtype=np.float64)
    g = 1.0 - np.exp2(-5.0 - hh)
    j = np.arange(128, dtype=np.float64)
    ginv = g[None, :] ** (-(j[:, None] + 1.0))
    causal = (j[:, None] <= j[None, :]).astype(np.float64)
    maskT = (ginv[:, :, None] * causal[:, None, :] / 16.0).astype(np.float32)
    zs = (g[None, :] ** (127.0 - j[:, None]) / 16.0).astype(np.float32)
    xi = g[None, :] ** (j[:, None] + 1.0)
    epsp = (GN_EPS / (xi * xi)).astype(np.float32)
    gC = [float(np.float32(gg ** 128.0)) for gg in g]
    half = 128
    inv_freq = (np.float32(10000.0) ** (-(np.arange(half, dtype=np.float32) / np.float32(half)))).astype(np.float32)
    ext = np.concatenate([zs.astype(np.float64) * (g[None, :] ** 128.0), ginv / 16.0,
                          epsp.astype(np.float64) * (g[None, :] ** -256.0)], axis=1).astype(np.float32)
    return maskT, zs, epsp, gC, inv_freq.reshape(128, 1), ext


def build_program(NBLK):
    NTOK = NBLK * T
    nc = bass.Bass("TRN2", target_bir_lowering=False)
    maskT_np, zs_np, epsp_np, gC, _, _ext = _host_consts()
    g_np = [1.0 - 2.0 ** (-5.0 - hh) for hh in range(H)]
    gC2 = [float(np.float32(gg ** 256.0)) for gg in g_np]
    gm128 = [float(np.float32(gg ** -128.0)) for gg in g_np]

    def din(name, shape, dt=F32):
        return nc.dram_tensor(name, list(shape), dt, kind="ExternalInput").ap()

    x_own = din("x_own", [NTOK, D])
    x_prev = din("x_prev", [NTOK, D])
    pos_own = din("pos_own", [1, NTOK], I32)
    pos_prev = din("pos_prev", [1, NTOK], I32)
    flag_d = din("flag", [128, 1])
    cT_d = din("cT", [128, KC])
    w_ada = din("w_ada", [D, 3 * D])
    b_adaT_d = din("b_adaT", [128, 16])
    b_gate_d = din("b_gate", [1, D])
    pre_gT_d = din("pre_gT", [128, KC])
    w_in = din("w_in", [D, IN_W])
    conv_wT_d = din("conv_wT", [128, KC * CONVW])
    conv_bT_d = din("conv_bT", [128, KC])
    ln_gT_d = din("ln_gT", [128, KC])
    ln_bT_d = din("ln_bT", [128, KC])
    w_ro = din("w_ret_out", [2 * D, D])
    w_co = din("w_conv_out", [D, D])
    w_o = din("w_out", [D, D])
    post_g_d = din("post_g", [1, D])
    ident_d = din("ident", [128, 128])
    maskT_d = din("maskT", [128, H * 128])
    zs_d = din("zs", [128, H])
    epsp_d = din("epsp", [128, H])
    invf_d = din("inv_freq", [128, 1])
    ext_d = din("ext", [128, 12])
    out_d = nc.dram_tensor("out", [NTOK, D], F32, kind="ExternalOutput").ap()

    units = {}
    ulist = []

    def add_unit(name):
        units[name] = len(ulist)
        ulist.append(name)

    for h in range(H):
        for nm in ("Q%d", "K%d", "V%d_0", "V%d_1", "Z%d_0", "Z%d_1"):
            add_unit(nm % h)
    for nm in ("UV", "UG", "ZC", "GA", "GB"):
        for i in range(4):
            add_unit("%s%d" % (nm, i))
    for i in range(8):
        add_unit("RO%d" % i)
    for i in range(4):
        add_unit("CO%d" % i)
    for i in range(4):
        add_unit("WO%d" % i)
    for c in range(KC):
        add_unit("DG%d" % c)
    NU = len(ulist)
    wsc = nc.dram_tensor("wsc", [NU, 128, 2048], BF16).ap()
    B_wsc = [Buf("wsc%d" % i) for i in range(NU)]

    S = Sched(nc)
    sb = nc.alloc_sbuf_tensor
    ps = nc.alloc_psum_tensor

    def V(m, reads, writes, **kw):
        S.op("dve", m, reads, writes, **kw)

    def A(m, reads, writes, **kw):
        S.op("act", m, reads, writes, **kw)

    def G(m, reads, writes, **kw):
        S.op("pool", m, reads, writes, **kw)

    def PE(m, reads, writes, **kw):
        S.op("pe", m, reads, writes, **kw)

    def DMA(reads, writes, **kw):
        S.op("sp", "dma_start", reads, writes, dma=True, **kw)

    def DMA2(reads, writes, **kw):
        S.op("pool", "dma_start", reads, writes, dma=True, **kw)

    ident_f = sb("ident_f", [128, 128], F32)
    ident_b = sb("ident_b", [128, 128], BF16)
    ones_b = sb("ones_b", [128, 128], BF16)
    maskT = sb("maskT_sb", [128, H, 128], F32)
    zs = sb("zs_sb", [128, H], F32)
    epsp = sb("epsp_sb", [128, H], F32)
    invf = sb("invf_sb", [128, 1], F32)
    ext = sb("ext_sb", [128, 12], F32)
    flag = sb("flagt", [128, 1], F32)
    cT = sb("cTt", [128, KC], F32)
    cT_b = sb("cT_b", [128, KC], BF16)
    bT = sb("bT", [128, KC, T], BF16)
    bT_b = [Buf("bT%d" % c) for c in range(KC)]
    cbc = bT[:, 0:2, :].rearrange("p a (b c) -> p (a b) c", c=128)
    b_adaT = sb("b_adaTt", [128, 16], F32)
    pre_gT = sb("pre_gTt", [128, KC], F32)
    gs = sb("gs", [128, KC], F32)
    shiftT = sb("shiftT", [128, KC], F32)
    conv_w05 = sb("conv_w05", [128, KC, CONVW], F32)
    conv_bT = sb("conv_bTt", [128, KC], F32)
    ln_gT = sb("ln_gTt", [128, KC], F32)
    ln_bT = sb("ln_bTt", [128, KC], F32)
    pgg = sb("pgg", [128, D], F32)
    xr = [sb("xr%d" % i, [128, D], F32) for i in range(2)]
    xr_b = [Buf("xr%d" % i) for i in range(2)]
    bgate, bgate_b = xr[1], xr_b[1]
    B_const = Buf("const")
    B_c2 = Buf("const2")

    def ld_const(dst_ap, src_ap, buf=B_const):
        DMA([], [buf], out=dst_ap, in_=src_ap)

    ld_const(ident_f[:], ident_d[:, :])
    ld_const(maskT[:].rearrange("p h i -> p (h i)"), maskT_d[:, :])
    ld_const(zs[:], zs_d[:, :])
    ld_const(epsp[:], epsp_d[:, :])
    ld_const(invf[:], invf_d[:, :])
    ld_const(ext[:], ext_d[:, :])
    ld_const(flag[:], flag_d[:, :])
    ld_const(cT[:], cT_d[:, :])
    ld_const(b_adaT[:], b_adaT_d[:, :])
    ld_const(pre_gT[:], pre_gT_d[:, :])
    ld_const(conv_w05[:].rearrange("p c k -> p (c k)"), conv_wT_d[:, :])
    ld_const(conv_bT[:], conv_bT_d[:, :])
    ld_const(ln_gT[:], ln_gT_d[:, :])
    ld_const(ln_bT[:], ln_bT_d[:, :])
    ld_const(pgg[:], post_g_d[0:1, :].to_broadcast([128, D]))
    ld_const(bgate[:], b_gate_d[0:1, :].to_broadcast([128, D]), bgate_b)
    V("tensor_copy", [B_const], [B_c2], out=ident_b[:], in_=ident_f[:])
    V("memset", [], [B_c2], ap=ones_b[:], constant=1.0)
    V("tensor_copy", [B_const], [B_c2], out=cT_b[:], in_=cT[:])
    for kc in range(KC):
        V("tensor_copy", [B_const], [B_c2, bT_b[0], bT_b[1]], out=cbc[:, kc, :], in_=cT[:, kc:kc + 1].to_broadcast([128, 128]))
    V("tensor_scalar", [B_const], [B_const], out=conv_w05[:], in0=conv_w05[:], scalar1=0.5, scalar2=None, op0=ALU.mult)

    pbank = [ps("pb%d" % i, [128, 512], F32) for i in range(6)]
    pbuf = [Buf("pb%d" % i) for i in range(6)]
    ptr = [ps("ptr%d" % i, [128, 1024], BF16) for i in range(2)]
    ptrbuf = [Buf("ptr0"), Buf("ptr1")]
    rr = {}

    def nxt(key, n):
        i = rr.get(key, 0)
        rr[key] = i + 1
        return i % n

    def pring():
        i = nxt("pring", 4)
        return pbank[i], pbuf[i]

    def ptring():
        i = nxt("ptring", 2)
        return ptr[i][:, 0:512], ptrbuf[i]

    stg_f = [sb("stg_f%d" % i, [128, 4096], F32) for i in range(2)]
    stg_fb = [Buf("stg_f%d" % i) for i in range(2)]
    stg_b0 = sb("stg_b0", [128, 4096], BF16)
    rT = sb("rT", [128, 16, T], BF16)
    stg_bap = [stg_b0[:], rT[:, 0:8, :].rearrange("p a b -> p (a b)")]
    stg_bb = [Buf("stg_b%d" % i) for i in range(2)]
    vvbuf = sb("vvbuf", [128, 4096], BF16)

    NSLOT = 5
    wslot = [sb("wslot%d" % i, [128, 2048], BF16) for i in range(NSLOT)]
    wslot_b = [Buf("wslot%d" % i) for i in range(NSLOT)]

    cast_mode = {"pool_only": False}

    def cast_op(dst, src, reads, writes):
        k = 1 if cast_mode["pool_only"] else nxt("cast", 3)
        if k == 0:
            V("tensor_copy", reads, writes, out=dst, in_=src)
        elif k == 1:
            A("activation", reads, writes, out=dst, in_=src, func=AF.Copy)
        else:
            G("tensor_copy", reads, writes, out=dst, in_=src)

    def stage_load_issue(src_ap3, nA):
        i = nxt("stg", 2)
        sf, sfb = stg_f[i], stg_fb[i]
        sf3 = sf[:].rearrange("p (a b) -> p a b", a=nA)
        (DMA2 if cast_mode["pool_only"] else DMA)([], [sfb], out=sf3, in_=src_ap3)
        return i

    def stage_load_cast(i, nA):
        sf, sfb, sbap, sbb = stg_f[i], stg_fb[i], stg_bap[i], stg_bb[i]
        sb3 = sbap.rearrange("p (a b) -> p a b", a=nA)
        cast_op(sbap, sf[:], [sfb], [sbb])
        return sb3, sbb

    def stage_load(src_ap3, nA):
        i = stage_load_issue(src_ap3, nA)
        return stage_load_cast(i, nA)

    def convert_finish(i, nA, nB, unit_names):
        sb3, sbb = stage_load_cast(i, nA)
        hb = nB // 2
        for q, un in enumerate(unit_names):
            dst = wsc[units[un]].rearrange("p (a b) -> p a b", a=nA)
            (DMA2 if cast_mode["pool_only"] else DMA)([sbb], [B_wsc[units[un]]], out=dst, in_=sb3[:, :, q * hb:(q + 1) * hb])

    def convert(src_ap3, nA, nB, unit_names):
        i = stage_load_issue(src_ap3, nA)
        convert_finish(i, nA, nB, unit_names)

    def kview(w, c0, n):
        return w[:, c0:c0 + n].rearrange("(kc p) n -> p kc n", p=128)

    def conv_win(nm, off, i):
        convert(kview(w_in, off + i * 512, 512), KC, 512, ["%s%d" % (nm, 2 * i), "%s%d" % (nm, 2 * i + 1)])

    def gen_convert_first():
        for hp in range(2):
            conv_win("K", OFF_K, hp)
            yield
        for h in range(3):
            convert(kview(w_in, OFF_V + h * 512, 512), KC, 512, ["V%d_0" % h, "V%d_1" % h])
            yield

    def gen_diag_units():
        for c in range(KC):
            i = nxt("stg", 2)
            sbap, sbb = stg_bap[i], stg_bb[i]
            d3 = sbap[:, 0:2048].rearrange("p (a b) -> p a b", a=16)
            for t in range(16):
                A("activation", [B_const], [sbb], out=d3[:, t, :], in_=ident_f[:], func=AF.Copy,
                  scale=conv_w05[:, c, 2 * t:2 * t + 1])
            DMA2([sbb], [B_wsc[units["DG%d" % c]]], out=wsc[units["DG%d" % c]], in_=sbap[:, 0:2048])
            yield

    def rest_specs():
        sp = [(kview(w_in, OFF_V + 3 * 512, 512), KC, 512, ["V3_0", "V3_1"])]
        for i in range(2):
            for nm, off in (("UG", OFF_UG), ("UV", OFF_UV)):
                sp.append((kview(w_in, off + i * 512, 512), KC, 512, ["%s%d" % (nm, 2 * i), "%s%d" % (nm, 2 * i + 1)]))
        for i in range(2):
            sp.append((kview(w_in, OFF_ZC + i * 512, 512), KC, 512, ["ZC%d" % (2 * i), "ZC%d" % (2 * i + 1)]))
        for hp in range(2):
            sp.append((kview(w_in, OFF_Q + hp * 512, 512), KC, 512, ["Q%d" % (2 * hp), "Q%d" % (2 * hp + 1)]))
        for h in range(H):
            sp.append((kview(w_in, OFF_Z + h * 512, 512), KC, 512, ["Z%d_0" % h, "Z%d_1" % h]))
        for nm, off in (("GA", OFF_GA), ("GB", OFF_GB)):
            for i in range(2):
                sp.append((kview(w_in, off + i * 512, 512), KC, 512, ["%s%d" % (nm, 2 * i), "%s%d" % (nm, 2 * i + 1)]))
        for i in range(2):
            sp.append((kview(w_co, i * 512, 512), KC, 512, ["CO%d" % (2 * i), "CO%d" % (2 * i + 1)]))
        for i in range(4):
            sp.append((kview(w_ro, i * 256, 256), 16, 256, ["RO%d" % (2 * i), "RO%d" % (2 * i + 1)]))
        for i in range(2):
            sp.append((kview(w_o, i * 512, 512), KC, 512, ["WO%d" % (2 * i), "WO%d" % (2 * i + 1)]))
        return sp

    def gen_convert_rest():
        sp = rest_specs()
        slots = [stage_load_issue(sp[0][0], sp[0][1])]
        for i in range(len(sp)):
            if i + 1 < len(sp):
                slots.append(stage_load_issue(sp[i + 1][0], sp[i + 1][1]))
            convert_finish(slots[i], sp[i][1], sp[i][2], sp[i][3])
            yield

    pmod, pmodb = pbank[4], pbuf[4]
    for blk in range(4):
        sb3, sbb = stage_load(kview(w_ada, blk * 512, 512), KC)
        for jj in range(4):
            j = blk * 4 + jj
            for kc in range(KC):
                PE("matmul", [sbb, B_c2], [pmodb], out=pmod[:, j:j + 1], lhsT=sb3[:, kc, jj * 128:(jj + 1) * 128],
                   rhs=cT_b[:, kc:kc + 1], start=(kc == 0), stop=(kc == KC - 1))
    V("tensor_tensor", [pmodb, B_const], [B_c2], out=shiftT[:], in0=pmod[:, 0:8], in1=b_adaT[:, 0:8], op=ALU.add)
    V("tensor_tensor", [pmodb, B_const], [B_c2], out=gs[:], in0=pmod[:, 8:16], in1=b_adaT[:, 8:16], op=ALU.add)
    V("scalar_tensor_tensor", [B_c2, B_const], [B_c2], out=gs[:], in0=gs[:], scalar=1.0, in1=pre_gT[:],
      op0=ALU.add, op1=ALU.mult)
    def gen_gate_row():
        for blk in range(2):
            sb3, sbb = stage_load(kview(w_ada, 2048 + blk * 512, 512), KC)
            pg, pgb = pring()
            for kc in range(KC):
                PE("matmul", [sbb, B_c2, bT_b[0], bT_b[1]], [pgb], out=pg[:], lhsT=cbc[:, kc, :], rhs=sb3[:, kc, :],
                   start=(kc == 0), stop=(kc == KC - 1))
            sl = slice(blk * 512, (blk + 1) * 512)
            V("tensor_tensor", [pgb, bgate_b], [bgate_b], out=bgate[:, sl], in0=pg[:], in1=bgate[:, sl], op=ALU.add)
            V("tensor_tensor", [bgate_b, B_const], [B_pgg], out=pgg[:, sl], in0=pgg[:, sl], in1=bgate[:, sl], op=ALU.mult)
            yield

    B_pgg = Buf("pgg")

    ac = stg_f[0]
    ac_cb = [Buf("ac%d" % c) for c in range(KC)]
    lnt = stg_f[1]
    lnt_b = Buf("lnt")
    lns = [lnt[:, (6 + i) * T:(7 + i) * T] for i in range(2)]
    lns_b = [Buf("lns%d" % i) for i in range(2)]
    szT3 = [stg_b0[:, i * 2048:(i + 1) * 2048].rearrange("p (a b) -> p a b", a=4) for i in range(2)]
    szT_b = [Buf("szT0"), Buf("szT1")]
    vv = [vvbuf[:, i * 2048:(i + 1) * 2048].rearrange("p (n e) -> p n e", n=NT) for i in range(2)]
    vv_b = [[Buf("vv%d_%d" % (i, n)) for n in range(NT)] for i in range(2)]
    rT_b = [Buf("rT%d" % i) for i in range(H)]
    hT = sb("hT", [128, KC, T], BF16)
    hT_b = [Buf("hT%d" % c) for c in range(KC)]
    aT = sb("aT", [128, KC, HALO + T], BF16)
    aT_b = [Buf("aT%d" % c) for c in range(KC)]
    mT3 = aT[:, :, HALO:HALO + T]
    NXS = 2
    xs = [sb("xs%d" % i, [128, D], F32) for i in range(NXS)]
    xs_b = [Buf("xs%d" % i) for i in range(NXS)]
    xn = [sb("xn%d" % i, [128, D], BF16) for i in range(4)]
    xn_b = [Buf("xn%d" % i) for i in range(4)]
    st_f = sb("st_f", [128, H * 2, 512], F32)
    st_bf = sb("st_bf", [128, H * 2, 512], BF16)
    st_fb = [Buf("stf%d" % i) for i in range(H * 2)]
    st_bb = [Buf("stb%d" % i) for i in range(H * 2)]
    cos_t = sb("cos_t", [128, T], F32)
    sin_t = sb("sin_t", [128, T], F32)
    tab_b = Buf("tab")
    rtmp = [sb("rtmp%d" % i, [128, T], F32) for i in range(4)]
    rtmp_b = [Buf("rtmp%d" % i) for i in range(4)]
    qT = [sb("qT%d" % i, [128, 2, T], BF16) for i in range(2)]
    kT = [sb("kT%d" % i, [128, 2, T], BF16) for i in range(2)]
    kz = [sb("kz%d" % i, [128, NT, 256], BF16) for i in range(2)]
    qT_b = [[Buf("qT%d_%d" % (i, d)) for d in range(2)] for i in range(2)]
    kT_b = [[Buf("kT%d_%d" % (i, d)) for d in range(2)] for i in range(2)]
    kz_b = [[Buf("kz%d_%d" % (i, n)) for n in range(NT)] for i in range(2)]
    PT = [sb("PT%d" % i, [128, 128], BF16) for i in range(6)]
    PT_b = [Buf("PT%d" % i) for i in range(6)]
    on = [sb("on%d" % i, [128, 512], BF16) for i in range(2)]
    on_b = [Buf("on%d" % i) for i in range(2)]
    sm = [sb("sm%d" % i, [128, 16], F32) for i in range(4)]
    smi = [sb("smi%d" % i, [128, 2], I32) for i in range(4)]
    sm_b = [Buf("sm%d" % i) for i in range(4)]
    NDIAG = 30
    diag = [sb("diag%d" % i, [128, 128], BF16) for i in range(NDIAG)]
    diag_b = [Buf("diag%d" % i) for i in range(NDIAG)]
    acb16 = [sb("acb%d" % i, [128, T], BF16) for i in range(2)]
    sqb16 = [sb("sqb%d" % i, [128, T], BF16) for i in range(2)]
    acb16_b = [Buf("acb%d" % i) for i in range(2)]
    sqb16_b = [Buf("sqb%d" % i) for i in range(2)]
    szc = [sb("szc%d" % i, [128, T], BF16) for i in range(2)]
    szc_b = [Buf("szc%d" % i) for i in range(2)]
    print("sbuf bytes remaining/partition:", nc.sbuf_bytes_remaining)

    bar_t = sb("bar_t", [128, 1], F32)

    def staging_barrier():
        bar_w = ac_cb + [lnt_b] + szT_b + lns_b + rT_b + [bT_b[0], bT_b[1]]
        V("memset", [], stg_fb + stg_bb + bar_w, ap=bar_t[:], constant=0.0)

    def passA_units(last):
        s = []
        for h in range(H):
            s += ["K%d" % h, "V%d_0" % h, "V%d_1" % h]
        if last:
            for i in range(4):
                s += ["UG%d" % i, "UV%d" % i]
        return s

    def main_units():
        s = []
        for i in range(4):
            s += ["UG%d" % i, "UV%d" % i]
        s += ["ZC%d" % i for i in range(4)]
        for h in range(H):
            s += ["Q%d" % h, "K%d" % h, "V%d_0" % h, "V%d_1" % h, "Z%d_0" % h, "Z%d_1" % h]
        for i in range(4):
            s += ["GA%d" % i, "GB%d" % i, "CO%d" % i, "RO%d" % (2 * i), "RO%d" % (2 * i + 1)]
        s += ["WO%d" % i for i in range(4)] * 2
        return s

    useq = []
    wst = {"cons": 0, "loaded": 0}
    LOOK = NSLOT - 1
    slot_free = list(range(NSLOT))
    slot_of = {}
    held = {}

    def _load(j):
        sl = slot_free.pop(0)
        slot_of[j] = sl
        DMA([B_wsc[units[useq[j]]]], [wslot_b[sl]], out=wslot[sl][:], in_=wsc[units[useq[j]]])

    def _prefetch():
        while wst["loaded"] < len(useq) and slot_free and wst["loaded"] <= wst["cons"] + LOOK:
            _load(wst["loaded"])
            wst["loaded"] += 1

    def release(stream):
        if S.dry:
            return
        j = held.pop(stream, None)
        if j is not None:
            slot_free.append(slot_of.pop(j))
            _prefetch()

    def acquire(name, stream="M"):
        if S.dry:
            useq.append(name)
            return wslot[0], wslot_b[0]
        release(stream)
        i = wst["cons"]
        assert useq[i] == name, (i, useq[i], name)
        if i >= wst["loaded"]:
            assert wst["loaded"] == i and slot_free
            _load(i)
            wst["loaded"] += 1
        wst["cons"] += 1
        held[stream] = i
        _prefetch()
        sl = slot_of[i]
        return wslot[sl], wslot_b[sl]

    xseq = [("prev", t) for t in range(NBLK * NT)] + [("own", t) for t in range(NBLK * NT)]
    xst = {"cons": 0, "loaded": 0}

    def acquire_x():
        i = xst["cons"]
        xst["cons"] += 1
        while xst["loaded"] < len(xseq) and xst["loaded"] <= i + (NXS - 1):
            j = xst["loaded"]
            which, t = xseq[j]
            src = (x_prev if which == "prev" else x_own)[t * 128:(t + 1) * 128, :]
            DMA([], [xs_b[j % NXS]], out=xs[j % NXS][:], in_=src)
            xst["loaded"] += 1
        return xs[i % NXS], xs_b[i % NXS]

    def rsqrt(a_ap, y_ap, t_ap, i_ap, bufs, scalar=True):
        V("tensor_scalar", bufs, bufs, out=i_ap, in0=a_ap.bitcast(I32), scalar1=1, scalar2=None, op0=ALU.arith_shift_right)
        V("tensor_scalar", bufs, bufs, out=i_ap, in0=i_ap, scalar1=-1, scalar2=RSQRT_MAGIC, op0=ALU.mult, op1=ALU.add)
        cur = i_ap.bitcast(F32)
        for it in range(NEWTON_ITERS):
            if scalar:
                V("scalar_tensor_tensor", bufs, bufs, out=t_ap, in0=cur, scalar=a_ap, in1=cur, op0=ALU.mult, op1=ALU.mult)
            else:
                V("tensor_tensor", bufs, bufs, out=t_ap, in0=cur, in1=cur, op=ALU.mult)
                V("tensor_tensor", bufs, bufs, out=t_ap, in0=t_ap, in1=a_ap, op=ALU.mult)
            V("tensor_scalar", bufs, bufs, out=t_ap, in0=t_ap, scalar1=-0.5, scalar2=1.5, op0=ALU.mult, op1=ALU.add)
            V("tensor_tensor", bufs, bufs, out=y_ap, in0=cur, in1=t_ap, op=ALU.mult)
            cur = y_ap

    def pre_elem():
        for n in range(NT):
            xt, xtb = acquire_x()
            si = nxt("sm", 4)
            smt, smb = sm[si], sm_b[si]
            A("activation", [xtb], [xn_b[n], smb], out=xn[n][:], in_=xt[:], func=AF.Square, accum_out=smt[:, 0:1])
            V("tensor_scalar", [smb], [smb], out=smt[:, 1:2], in0=smt[:, 0:1], scalar1=1.0 / D, scalar2=RMS_EPS,
              op0=ALU.mult, op1=ALU.add)
            rsqrt(smt[:, 1:2], smt[:, 2:3], smt[:, 3:4], smi[si][:, 0:1], [smb])
            V("tensor_scalar", [xtb, smb], [xn_b[n]], out=xn[n][:], in0=xt[:], scalar1=smt[:, 2:3], scalar2=None,
              op0=ALU.mult)

    def pre_pe():
        for half in range(2):
            tl = [half * 2, half * 2 + 1]
            for kc2 in range(KC // 2):
                pt, ptb = ptring()
                for q in range(2):
                    kc = kc2 * 2 + q
                    for n2 in range(2):
                        PE("transpose", [xn_b[tl[n2]], B_c2], [ptb],
                           out=pt[:, (q * 2 + n2) * 128:(q * 2 + n2 + 1) * 128],
                           in_=xn[tl[n2]][:, kc * 128:(kc + 1) * 128], identity=ident_b[:])
                for q in range(2):
                    kc = kc2 * 2 + q
                    A("activation", [ptb, B_c2], [hT_b[kc]], out=hT[:, kc, half * 256:(half + 1) * 256],
                      in_=pt[:, q * 256:(q + 1) * 256], func=AF.Identity, bias=shiftT[:, kc:kc + 1], scale=gs[:, kc:kc + 1])

    def stage_tables(pos_ap, t0):
        tA, tA_b = rtmp[0], rtmp_b[0]
        tB, tB_b = rtmp[1], rtmp_b[1]
        posi, posi_b = rtmp[2][:].bitcast(I32), rtmp_b[2]
        DMA([], [posi_b], out=posi, in_=pos_ap[0:1, t0:t0 + T].to_broadcast([128, T]))
        V("tensor_copy", [posi_b], [tA_b], out=tA[:], in_=posi)
        V("tensor_scalar", [tA_b, B_const], [tA_b], out=tA[:], in0=tA[:], scalar1=invf[:, 0:1], scalar2=None, op0=ALU.mult)
        V("tensor_scalar", [tA_b], [posi_b], out=posi, in0=tA[:], scalar1=1.0 / TWO_PI, scalar2=None, op0=ALU.mult)
        V("tensor_copy", [posi_b], [tB_b], out=tB[:], in_=posi)
        V("scalar_tensor_tensor", [tA_b, tB_b], [tA_b], out=tA[:], in0=tB[:], scalar=-C1, in1=tA[:], op0=ALU.mult, op1=ALU.add)
        V("scalar_tensor_tensor", [tA_b, tB_b], [tA_b], out=tA[:], in0=tB[:], scalar=-C2, in1=tA[:], op0=ALU.mult, op1=ALU.add)
        V("tensor_scalar", [tA_b], [tA_b], out=tA[:], in0=tA[:], scalar1=PI_LO, scalar2=-PI_LO, op0=ALU.min, op1=ALU.max)
        A("activation", [tA_b], [tab_b], out=sin_t[:], in_=tA[:], func=AF.Sin)
        A("activation", [tA_b], [tB_b], out=tB[:], in_=tA[:], func=AF.Sin, scale=0.5)
        V("tensor_tensor", [tB_b], [tB_b], out=tB[:], in0=tB[:], in1=tB[:], op=ALU.mult)
        V("tensor_scalar", [tB_b], [tab_b], out=cos_t[:], in0=tB[:], scalar1=-2.0, scalar2=1.0, op0=ALU.mult, op1=ALU.add)

    def proj_fm(unit_t, col, nkc, rhs_fn, rhs_bufs, unit_buf):
        pb, pbb = pring()
        u3 = unit_t[:].rearrange("p (a b) -> p a b", a=nkc)
        for kc in range(nkc):
            PE("matmul", [unit_buf, rhs_bufs[kc]], [pbb], out=pb[:], lhsT=u3[:, kc, col:col + 128], rhs=rhs_fn(kc),
               start=(kc == 0), stop=(kc == nkc - 1))
        return pb, pbb

    def hT_rhs(kc):
        return hT[:, kc, :]

    def rotary(pa, pab, pb_, pbb_, dst, dstb, si):
        i0, i1 = nxt("rtmp", 4), nxt("rtmp", 4)
        V("tensor_tensor", [pab, tab_b], [rtmp_b[i0]], out=rtmp[i0][:], in0=pa[:], in1=cos_t[:], op=ALU.mult)
        V("tensor_tensor", [pbb_, tab_b], [rtmp_b[i1]], out=rtmp[i1][:], in0=pb_[:], in1=sin_t[:], op=ALU.mult)
        V("tensor_tensor", [rtmp_b[i0], rtmp_b[i1]], [dstb[si][0]], out=dst[si][:, 0, :], in0=rtmp[i0][:], in1=rtmp[i1][:],
          op=ALU.subtract)
        i2, i3 = nxt("rtmp", 4), nxt("rtmp", 4)
        V("tensor_tensor", [pab, tab_b], [rtmp_b[i2]], out=rtmp[i2][:], in0=pa[:], in1=sin_t[:], op=ALU.mult)
        V("tensor_tensor", [pbb_, tab_b], [rtmp_b[i3]], out=rtmp[i3][:], in0=pb_[:], in1=cos_t[:], op=ALU.mult)
        V("tensor_tensor", [rtmp_b[i2], rtmp_b[i3]], [dstb[si][1]], out=dst[si][:, 1, :], in0=rtmp[i2][:], in1=rtmp[i3][:],
          op=ALU.add)

    def head_kv(h, si, with_q, stream="M", c256=False):
        if with_q:
            u, ub = acquire("Q%d" % h, stream)
            pa, pab = proj_fm(u, 0, KC, hT_rhs, hT_b, ub)
            pb_, pbb_ = proj_fm(u, 128, KC, hT_rhs, hT_b, ub)
            rotary(pa, pab, pb_, pbb_, qT, qT_b, si)
            yield
        u, ub = acquire("K%d" % h, stream)
        pa, pab = proj_fm(u, 0, KC, hT_rhs, hT_b, ub)
        pb_, pbb_ = proj_fm(u, 128, KC, hT_rhs, hT_b, ub)
        rotary(pa, pab, pb_, pbb_, kT, kT_b, si)
        for n2 in range(2):
            pt, ptb = ptring()
            for nn in range(2):
                n = n2 * 2 + nn
                for dc in range(2):
                    PE("transpose", [kT_b[si][dc], B_c2], [ptb], out=pt[:, (nn * 2 + dc) * 128:(nn * 2 + dc + 1) * 128],
                       in_=kT[si][:, dc, n * 128:(n + 1) * 128], identity=ident_b[:])
            for nn in range(2):
                n = n2 * 2 + nn
                zsc = ext[:, h:h + 1] if (c256 and n % 2 == 0) else zs[:, h:h + 1]
                A("activation", [ptb, B_const], [kz_b[si][n]], out=kz[si][:, n, :], in_=pt[:, nn * 256:(nn + 1) * 256],
                  func=AF.Copy, scale=zsc)
        yield
        for half in range(2):
            u, ub = acquire("V%d_%d" % (h, half), stream)
            u3 = u[:].rearrange("p (a b) -> p a b", a=KC)
            for n in range(NT):
                pb, pbb = pring()
                for kc in range(KC):
                    PE("matmul", [ub, hT_b[kc]], [pbb], out=pb[:, 0:256], lhsT=hT[:, kc, n * 128:(n + 1) * 128],
                       rhs=u3[:, kc, :], start=(kc == 0), stop=(kc == KC - 1))
                dst = vv[si][:, n, half * 256:(half + 1) * 256]
                if (n + half) % 2 == 0:
                    A("activation", [pbb], [vv_b[si][n]], out=dst, in_=pb[:, 0:256], func=AF.Copy)
                else:
                    V("tensor_copy", [pbb], [vv_b[si][n]], out=dst, in_=pb[:, 0:256])
        yield

    def state_update(h, si, n, need_bf):
        for dc in range(2):
            pb, pbb = pring()
            PE("matmul", [kz_b[si][n], vv_b[si][n]], [pbb], out=pb[:], lhsT=kz[si][:, n, dc * 128:(dc + 1) * 128],
               rhs=vv[si][:, n, :], start=True, stop=True)
            j = h * 2 + dc
            V("scalar_tensor_tensor", [pbb, st_fb[j]], [st_fb[j]], out=st_f[:, j, :], in0=st_f[:, j, :], scalar=gC[h],
              in1=pb[:], op0=ALU.mult, op1=ALU.add)
            if need_bf:
                A("activation", [st_fb[j]], [st_bb[j]], out=st_bf[:, j, :], in_=st_f[:, j, :], func=AF.Copy)

    def stage_glu(stream="M"):
        for i in range(4):
            ug, ugb = acquire("UG%d" % i, stream)
            tg = []
            for q in range(2):
                pg, pgb = proj_fm(ug, q * 128, KC, hT_rhs, hT_b, ugb)
                li = nxt("lns", 2)
                A("activation", [pgb], [lns_b[li]], out=lns[li], in_=pg[:], func=AF.Tanh, scale=0.5)
                tg.append(li)
                yield
            uv, uvb = acquire("UV%d" % i, stream)
            for q in range(2):
                c = i * 2 + q
                pv, pvb = proj_fm(uv, q * 128, KC, hT_rhs, hT_b, uvb)
                li = tg[q]
                V("scalar_tensor_tensor", [pvb, lns_b[li]], [aT_b[c]], out=aT[:, c, HALO:HALO + T], in0=lns[li], scalar=1.0,
                  in1=pv[:], op0=ALU.add, op1=ALU.mult)
                yield
        release(stream)

    def halo_shift(use_flag):
        for c in range(KC):
            if use_flag:
                V("tensor_scalar", [aT_b[c], B_const], [aT_b[c]], out=aT[:, c, 0:HALO], in0=aT[:, c, T:T + HALO],
                  scalar1=flag[:, 0:1], scalar2=None, op0=ALU.mult)
            else:
                V("tensor_copy", [aT_b[c]], [aT_b[c]], out=aT[:, c, 0:HALO], in_=aT[:, c, T:T + HALO])

    def stage_conv(stream="M"):
        L = lambda i: lnt[:, i * T:(i + 1) * T]
        def gen_odd(c):
            tl = {}
            for k in range(1, CONVW, 2):
                di = nxt("diag", NDIAG)
                A("activation", [B_const], [diag_b[di]], out=diag[di][:], in_=ident_f[:], func=AF.Copy,
                  scale=conv_w05[:, c, k:k + 1])
                tl[k] = di
            return tl

        odd_next = gen_odd(0)
        pend_stats = None
        for c in range(KC):
            odd = odd_next
            if c + 1 < KC:
                odd_next = gen_odd(c + 1)
            dg, dgb = acquire("DG%d" % c, stream)
            dg3 = dg[:].rearrange("p (a b) -> p a b", a=16)
            pb, pbb = pring()
            for k in range(CONVW):
                if k % 2 == 0:
                    lhs, lb_ = dg3[:, k // 2, :], dgb
                else:
                    lhs, lb_ = diag[odd[k]][:], diag_b[odd[k]]
                PE("matmul", [lb_, aT_b[c]], [pbb], out=pb[:], lhsT=lhs, rhs=aT[:, c, k:k + T],
                   start=(k == 0), stop=(k == CONVW - 1))
            acs = ac[:, c * T:(c + 1) * T]
            A("activation", [pbb, B_const], [ac_cb[c]], out=acs, in_=pb[:], func=AF.Identity, bias=conv_bT[:, c:c + 1])
            ai = nxt("acb", 2)
            A("activation", [ac_cb[c]], [acb16_b[ai]], out=acb16[ai][:], in_=acs, func=AF.Copy)
            A("activation", [ac_cb[c]], [sqb16_b[ai]], out=sqb16[ai][:], in_=acs, func=AF.Square)
            def stats(c=c, ai=ai):
                ps_, psb_ = pring()
                PE("matmul", [acb16_b[ai], B_c2], [psb_], out=ps_[:], lhsT=ones_b[:], rhs=acb16[ai][:], start=True, stop=True)
                pq_, pqb_ = pring()
                PE("matmul", [sqb16_b[ai], B_c2], [pqb_], out=pq_[:], lhsT=ones_b[:], rhs=sqb16[ai][:], start=True, stop=True)
                if c == 0:
                    V("tensor_copy", [psb_], [lnt_b], out=L(0), in_=ps_[:])
                    V("tensor_copy", [pqb_], [lnt_b], out=L(1), in_=pq_[:])
                else:
                    V("tensor_tensor", [psb_, lnt_b], [lnt_b], out=L(0), in0=L(0), in1=ps_[:], op=ALU.add)
                    V("tensor_tensor", [pqb_, lnt_b], [lnt_b], out=L(1), in0=L(1), in1=pq_[:], op=ALU.add)
            if pend_stats is not None:
                pend_stats()
            pend_stats = stats
            yield
        pend_stats()
        mean, var, rstd, tmp, mr = L(0), L(1), L(2), L(3), L(4)
        lni = L(5).bitcast(I32)
        lb = [lnt_b]
        V("tensor_scalar", lb, lb, out=mean, in0=mean, scalar1=1.0 / D, scalar2=None, op0=ALU.mult)
        V("tensor_scalar", lb, lb, out=var, in0=var, scalar1=1.0 / D, scalar2=LN_EPS, op0=ALU.mult, op1=ALU.add)
        V("tensor_tensor", lb, lb, out=tmp, in0=mean, in1=mean, op=ALU.mult)
        V("tensor_tensor", lb, lb, out=var, in0=var, in1=tmp, op=ALU.subtract)
        rsqrt(var, rstd, tmp, lni, lb, scalar=False)
        V("tensor_tensor", lb, lb, out=mr, in0=mean, in1=rstd, op=ALU.mult)
        for i in range(4):
            zc, zcb = acquire("ZC%d" % i, stream)
            for q in range(2):
                c = i * 2 + q
                acs = ac[:, c * T:(c + 1) * T]
                pz, pzb = proj_fm(zc, q * 128, KC, hT_rhs, hT_b, zcb)
                zi = nxt("szc", 2)
                A("activation", [pzb], [szc_b[zi]], out=szc[zi][:], in_=pz[:], func=AF.Silu)
                li = nxt("lns", 2)
                V("tensor_tensor", [ac_cb[c], lnt_b], [lns_b[li]], out=lns[li], in0=acs, in1=rstd, op=ALU.mult)
                V("tensor_tensor", [lns_b[li], lnt_b], [lns_b[li]], out=lns[li], in0=lns[li], in1=mr, op=ALU.subtract)
                A("activation", [lns_b[li], B_const], [lns_b[li]], out=lns[li], in_=lns[li], func=AF.Silu,
                  bias=ln_bT[:, c:c + 1], scale=ln_gT[:, c:c + 1])
                V("tensor_tensor", [lns_b[li], szc_b[zi]], [bT_b[c]], out=bT[:, c, :], in0=lns[li], in1=szc[zi][:], op=ALU.mult)
                yield
        release(stream)

    def head_proj(h, si, stream="M"):
        for _ in head_kv(h, si, True, stream, True):
            yield
        for half in range(2):
            u, ub = acquire("Z%d_%d" % (h, half), stream)
            for q in range(2):
                ech = half * 2 + q
                pz, pzb = proj_fm(u, q * 128, KC, hT_rhs, hT_b, ub)
                A("activation", [pzb], [szT_b[si]], out=szT3[si][:, ech, :], in_=pz[:], func=AF.Silu)
        yield

    def ret_scores(h, si):
        out = []
        for cp in range(NT // 2):
            ta = slice((2 * cp) * 128, (2 * cp + 1) * 128)
            tb_ = slice((2 * cp + 1) * 128, (2 * cp + 2) * 128)
            trip = []
            for kind, (tj, ti) in enumerate(((ta, ta), (ta, tb_), (tb_, tb_))):
                pS, pSb = pring()
                for dc in range(2):
                    PE("matmul", [kT_b[si][dc], qT_b[si][dc]], [pSb], out=pS[:, 0:128], lhsT=kT[si][:, dc, tj],
                       rhs=qT[si][:, dc, ti], start=(dc == 0), stop=(dc == 1))
                pi = nxt("PT", 6)
                if kind == 0:
                    V("tensor_tensor", [pSb, B_const], [PT_b[pi]], out=PT[pi][:], in0=pS[:, 0:128], in1=maskT[:, h, :],
                      op=ALU.mult)
                elif kind == 1:
                    V("tensor_scalar", [pSb, B_const], [PT_b[pi]], out=PT[pi][:], in0=pS[:, 0:128],
                      scalar1=ext[:, 4 + h:5 + h], scalar2=None, op0=ALU.mult)
                else:
                    V("scalar_tensor_tensor", [pSb, B_const], [PT_b[pi]], out=PT[pi][:], in0=pS[:, 0:128], scalar=gm128[h],
                      in1=maskT[:, h, :], op0=ALU.mult, op1=ALU.mult)
                trip.append(pi)
            out.append(trip)
        return out

    def ret_chunk(h, si, n, trip):
        tok = slice(n * 128, (n + 1) * 128)
        second = (n % 2 == 1)
        oi_ = nxt("pO", 2)
        pO, pOb = pbank[4 + oi_], pbuf[4 + oi_]
        for dc in range(2):
            j = h * 2 + dc
            PE("matmul", [qT_b[si][dc], st_bb[j]], [pOb], out=pO[:], lhsT=qT[si][:, dc, tok], rhs=st_bf[:, j, :],
               start=(dc == 0), stop=False)
        if not second:
            PE("matmul", [PT_b[trip[0]], vv_b[si][n]], [pOb], out=pO[:], lhsT=PT[trip[0]][:], rhs=vv[si][:, n, :],
               start=False, stop=True)
        else:
            PE("matmul", [PT_b[trip[1]], vv_b[si][n - 1]], [pOb], out=pO[:], lhsT=PT[trip[1]][:], rhs=vv[si][:, n - 1, :],
               start=False, stop=False)
            PE("matmul", [PT_b[trip[2]], vv_b[si][n]], [pOb], out=pO[:], lhsT=PT[trip[2]][:], rhs=vv[si][:, n, :],
               start=False, stop=True)
            for dc in range(2):
                pb, pbb = pring()
                for q_, nn in enumerate((n - 1, n)):
                    PE("matmul", [kz_b[si][nn], vv_b[si][nn]], [pbb], out=pb[:], lhsT=kz[si][:, nn, dc * 128:(dc + 1) * 128],
                       rhs=vv[si][:, nn, :], start=(q_ == 0), stop=(q_ == 1))
                j = h * 2 + dc
                V("scalar_tensor_tensor", [pbb, st_fb[j]], [st_fb[j]], out=st_f[:, j, :], in0=st_f[:, j, :], scalar=gC2[h],
                  in1=pb[:], op0=ALU.mult, op1=ALU.add)
                A("activation", [st_fb[j]], [st_bb[j]], out=st_bf[:, j, :], in_=st_f[:, j, :], func=AF.Copy)
        mi = nxt("sm", 4)
        smt, smb = sm[mi], sm_b[mi]
        V("bn_stats", [pOb], [smb], out=smt[:, 0:6], in_=pO[:])
        V("bn_aggr", [smb], [smb], out=smt[:, 6:8], in_=smt[:, 0:6])
        eps_ap = ext[:, 8 + h:9 + h] if second else epsp[:, h:h + 1]
        V("tensor_tensor", [smb, B_const], [smb], out=smt[:, 8:9], in0=smt[:, 7:8], in1=eps_ap, op=ALU.add)
        rsqrt(smt[:, 8:9], smt[:, 9:10], smt[:, 10:11], smi[mi][:, 0:1], [smb])
        V("scalar_tensor_tensor", [smb], [smb], out=smt[:, 11:12], in0=smt[:, 6:7], scalar=-1.0, in1=smt[:, 9:10],
          op0=ALU.mult, op1=ALU.mult)
        oi = nxt("on", 2)
        A("activation", [pOb, smb], [on_b[oi]], out=on[oi][:], in_=pO[:], func=AF.Identity, bias=smt[:, 11:12],
          scale=smt[:, 9:10])

        def tail():
            pt, ptb = ptring()
            for ech in range(4):
                PE("transpose", [on_b[oi], B_c2], [ptb], out=pt[:, ech * 128:(ech + 1) * 128],
                   in_=on[oi][:, ech * 128:(ech + 1) * 128], identity=ident_b[:])
            V("tensor_tensor", [ptb, szT_b[si]], [rT_b[h]], out=rT[:, h * 4:(h + 1) * 4, tok],
              in0=pt.rearrange("p (a b) -> p a b", a=4), in1=szT3[si][:, :, tok], op=ALU.mult)
        return tail

    def stage_ret(stream="M"):
        g = head_proj(0, 0, stream)
        for _ in g:
            yield
        pending = None
        for h in range(H):
            gn = head_proj(h + 1, (h + 1) % 2, stream) if h + 1 < H else None
            pis = ret_scores(h, h % 2)
            for n in range(NT):
                if gn is not None:
                    next(gn, None)
                    yield
                t = ret_chunk(h, h % 2, n, pis[n // 2])
                if pending is not None:
                    pending()
                pending = t
                yield
            if gn is not None:
                for _ in gn:
                    yield
        pending()
        release(stream)

    def run(g):
        for _ in g:
            pass

    def interleave(ga, gb_):
        a_live, b_live = True, True
        while a_live or b_live:
            if a_live:
                try:
                    next(ga)
                except StopIteration:
                    a_live = False
            if b_live:
                try:
                    next(gb_)
                except StopIteration:
                    b_live = False

    def conv_stream():
        for _ in stage_glu("X"):
            yield
        for _ in stage_conv("X"):
            yield
        halo_shift(False)

    def stage_merge():
        for i in range(4):
            ga, gab = acquire("GA%d" % i)
            ta = []
            for q in range(2):
                pg, pgb = proj_fm(ga, q * 128, KC, hT_rhs, hT_b, gab)
                li = nxt("rtmp", 4)
                A("activation", [pgb], [rtmp_b[li]], out=rtmp[li][:], in_=pg[:], func=AF.Tanh, scale=0.5)
                ta.append(li)
            gb_, gbb = acquire("GB%d" % i)
            tb = []
            for q in range(2):
                pg, pgb = proj_fm(gb_, q * 128, KC, hT_rhs, hT_b, gbb)
                li = nxt("rtmp", 4)
                A("activation", [pgb], [rtmp_b[li]], out=rtmp[li][:], in_=pg[:], func=AF.Tanh, scale=0.5)
                tb.append(li)
            co, cob = acquire("CO%d" % i)
            for q in range(2):
                py, pyb = proj_fm(co, q * 128, KC, lambda kc: bT[:, kc, :], bT_b, cob)
                li = tb[q]
                V("scalar_tensor_tensor", [pyb, rtmp_b[li]], [rtmp_b[li]], out=rtmp[li][:], in0=rtmp[li][:], scalar=1.0,
                  in1=py[:], op0=ALU.add, op1=ALU.mult)
            for q in range(2):
                dch = i * 2 + q
                ro, rob = acquire("RO%d" % dch)
                py, pyb = proj_fm(ro, 0, 16, lambda ec: rT[:, ec, :], [rT_b[ec // 4] for ec in range(16)], rob)
                la, lb_ = ta[q], tb[q]
                V("scalar_tensor_tensor", [pyb, rtmp_b[la]], [rtmp_b[la]], out=rtmp[la][:], in0=rtmp[la][:], scalar=1.0,
                  in1=py[:], op0=ALU.add, op1=ALU.mult)
                V("tensor_tensor", [rtmp_b[la], rtmp_b[lb_]], [aT_b[dch]], out=mT3[:, dch, :], in0=rtmp[la][:],
                  in1=rtmp[lb_][:], op=ALU.add)

    B_out = Buf("outd")

    def stage_out(blk, between=None):
        for tp in range(2):
            if tp == 1 and between is not None:
                between()
            tiles = (tp * 2, tp * 2 + 1)
            banks = {n: (pring(), pring()) for n in tiles}
            for ui in range(4):
                u, ub = acquire("WO%d" % ui)
                u3 = u[:].rearrange("p (a b) -> p a b", a=KC)
                cs = (ui % 2) * 256
                for n in tiles:
                    pb, pbb = banks[n][ui // 2]
                    for kc in range(KC):
                        PE("matmul", [ub, aT_b[kc]], [pbb], out=pb[:, cs:cs + 256], lhsT=mT3[:, kc, n * 128:(n + 1) * 128],
                           rhs=u3[:, kc, :], start=(kc == 0), stop=(kc == KC - 1))
            for n in tiles:
                (pb0, pbb0), (pb1, pbb1) = banks[n]
                t = blk * NT + n
                xi_ = t % 2
                DMA2([], [xr_b[xi_]], out=xr[xi_][:], in_=x_own[t * 128:(t + 1) * 128, :])
                mi = nxt("sm", 4)
                smt, smb = sm[mi], sm_b[mi]
                j0, j1 = nxt("rtmp", 4), nxt("rtmp", 4)
                A("activation", [pbb0], [rtmp_b[j0], smb], out=rtmp[j0][:], in_=pb0[:], func=AF.Square, accum_out=smt[:, 0:1])
                A("activation", [pbb1], [rtmp_b[j1], smb], out=rtmp[j1][:], in_=pb1[:], func=AF.Square, accum_out=smt[:, 1:2])
                V("tensor_tensor", [smb], [smb], out=smt[:, 2:3], in0=smt[:, 0:1], in1=smt[:, 1:2], op=ALU.add)
                V("tensor_scalar", [smb], [smb], out=smt[:, 3:4], in0=smt[:, 2:3], scalar1=1.0 / D, scalar2=4.0 * RMS_EPS,
                  op0=ALU.mult, op1=ALU.add)
                rsqrt(smt[:, 3:4], smt[:, 4:5], smt[:, 5:6], smi[mi][:, 0:1], [smb])
                for hf, (pb, pbb) in enumerate(((pb0, pbb0), (pb1, pbb1))):
                    cs = slice(hf * 512, (hf + 1) * 512)
                    li = nxt("lns", 2)
                    V("scalar_tensor_tensor", [pbb, smb, B_pgg], [lns_b[li]], out=lns[li], in0=pb[:], scalar=smt[:, 4:5],
                      in1=pgg[:, cs], op0=ALU.mult, op1=ALU.mult)
                    V("tensor_tensor", [lns_b[li], xr_b[xi_]], [xr_b[xi_]], out=xr[xi_][:, cs], in0=xr[xi_][:, cs],
                      in1=lns[li], op=ALU.add)
                DMA2([xr_b[xi_]], [B_out], out=out_d[t * 128:(t + 1) * 128, :], in_=xr[xi_][:])

    def body():
        for j in range(H * 2):
            V("memset", [], [st_fb[j]], ap=st_f[:, j, :], constant=0.0)
        for c in range(KC):
            V("memset", [], [aT_b[c]], ap=aT[:, c, :], constant=0.0)
        cast_mode["pool_only"] = False
        run(gen_convert_first())
        gc = gen_convert_rest()
        cast_mode["pool_only"] = True
        gd = gen_diag_units()
        if NBLK == 1:
            run(gc)
            run(gd)
            run(gen_gate_row())
            staging_barrier()
        pre_elem()
        pre_pe()
        stage_tables(pos_prev, 0)
        for blk in range(NBLK):
            run(head_kv(0, 0, False))
            for h in range(H):
                si = h % 2
                if NBLK > 1 and blk < NBLK - 1:
                    next(gc, None)
                    next(gc, None)
                if h + 1 < H:
                    run(head_kv(h + 1, (h + 1) % 2, False))
                if h == 1:
                    pre_elem()
                if h == 2 and blk + 1 < NBLK:
                    pre_pe()
                    stage_tables(pos_prev, (blk + 1) * T)
                for n in range(NT):
                    state_update(h, si, n, False)
            if NBLK > 1 and blk == NBLK - 2:
                run(gc)
                run(gd)
                run(gen_gate_row())
                staging_barrier()
            if blk == NBLK - 1:
                run(stage_glu())
                halo_shift(True)
        for j in range(H * 2):
            V("tensor_scalar", [st_fb[j], B_const], [st_fb[j]], out=st_f[:, j, :], in0=st_f[:, j, :], scalar1=flag[:, 0:1],
              scalar2=None, op0=ALU.mult)
            A("activation", [st_fb[j]], [st_bb[j]], out=st_bf[:, j, :], in_=st_f[:, j, :], func=AF.Copy)
        pre_pe()
        stage_tables(pos_own, 0)
        for blk in range(NBLK):
            release("M")
            interleave(conv_stream(), stage_ret("Y"))
            if blk + 1 < NBLK:
                pre_elem()
            stage_merge()
            if blk + 1 < NBLK:
                def nxt_blk(b=blk + 1):
                    pre_pe()
                    stage_tables(pos_own, b * T)
                stage_out(blk, nxt_blk)
            else:
                stage_out(blk)

    rr_save = dict(rr)
    S.dry = True
    body()
    S.dry = False
    rr.clear()
    rr.update(rr_save)
    xst["cons"] = 0
    xst["loaded"] = 0
    body()
    assert wst["cons"] == len(useq), (wst["cons"], len(useq))

    with ExitStack() as stack:
        nops, nwait = S.emit(stack)
    print("ops", nops, "waits", nwait)
    return nc, nops, nwait


_PROG_CACHE = {}


def kernel(x, c, positions, w_ada, b_ada, pre_norm_g, w_in, conv_w, conv_b, conv_ln_g, conv_ln_b,
           w_ret_out, w_conv_out, w_out, post_norm_g):
    x = np.asarray(x, dtype=np.float32)
    B, S_, _ = x.shape
    half = S_ // 2
    NBLK = half // T
    assert half % T == 0
    ncores = 2 * B
    if NBLK not in _PROG_CACHE:
        _PROG_CACHE[NBLK] = build_program(NBLK)[0]
    nc = _PROG_CACHE[NBLK]
    maskT, zs, epsp, gC, inv_freq, ext = _host_consts()
    c = np.asarray(c, np.float32)
    positions = np.asarray(positions, np.int32)
    f = lambda a: np.ascontiguousarray(np.asarray(a, np.float32))
    lay = lambda v, n: f(np.asarray(v, np.float32).reshape(n, 128).T)
    b_ada0 = np.asarray(b_ada, np.float32)[0]
    shared = {
        "w_ada": f(w_ada[0]), "b_adaT": lay(b_ada0[:2048], 16), "b_gate": f(b_ada0[2048:].reshape(1, D)),
        "pre_gT": lay(pre_norm_g[0], KC), "w_in": f(w_in[0]),
        "conv_wT": f(np.asarray(conv_w[0], np.float32).T.reshape(KC, 128, CONVW).transpose(1, 0, 2).reshape(128, KC * CONVW)),
        "conv_bT": lay(conv_b[0], KC), "ln_gT": lay(conv_ln_g[0], KC), "ln_bT": lay(conv_ln_b[0], KC),
        "w_ret_out": f(w_ret_out[0]), "w_conv_out": f(w_conv_out[0]), "w_out": f(w_out[0]),
        "post_g": f(np.asarray(post_norm_g[0], np.float32).reshape(1, D)),
        "ident": np.eye(128, dtype=np.float32), "maskT": f(maskT.reshape(128, H * 128)), "zs": f(zs), "epsp": f(epsp),
        "inv_freq": f(inv_freq), "ext": f(ext),
    }
    in_maps = []
    for b in range(B):
        for j in range(2):
            own = x[b, j * half:(j + 1) * half]
            prev = x[b, 0:half]
            m = dict(shared)
            m["x_own"] = np.ascontiguousarray(own)
            m["x_prev"] = np.ascontiguousarray(prev)
            m["pos_own"] = np.ascontiguousarray(positions[b, j * half:(j + 1) * half].reshape(1, half))
            m["pos_prev"] = np.ascontiguousarray(positions[b, 0:half].reshape(1, half))
            m["flag"] = np.full((128, 1), float(j), np.float32)
            m["cT"] = lay(c[b], KC)
            in_maps.append(m)
    res = run_bass_kernel_spmd(nc, in_maps, core_ids=list(range(ncores)))
    out = np.empty((B, S_, D), np.float32)
    for b in range(B):
        for j in range(2):
            out[b, j * half:(j + 1) * half] = res.results[b * 2 + j]["out"]
    return out
```

```python
import math
from contextlib import ExitStack

import numpy as np
import concourse.bass as bass
import concourse.mybir as mybir
from concourse.bass_utils import run_bass_kernel_spmd

F32 = mybir.dt.float32
BF16 = mybir.dt.bfloat16
I32 = mybir.dt.int32
AF = mybir.ActivationFunctionType
ALU = mybir.AluOpType

D = 1024
T = 512
NT = 4
KC = 8
H = 4
CONVW = 31
HALO = CONVW - 1
IN_W = 11264
RMS_EPS = 1e-6
GN_EPS = 1e-5
LN_EPS = 1e-5
TWO_PI = 2.0 * math.pi
C1 = 6.28125
C2 = TWO_PI - C1
PI_LO = 3.1415925
RSQRT_MAGIC = 0x5F3759DF
NEWTON_ITERS = 3

OFF_Q, OFF_K, OFF_V, OFF_Z, OFF_UV, OFF_UG, OFF_ZC, OFF_GA, OFF_GB = (
    0, 1024, 2048, 4096, 6144, 7168, 8192, 9216, 10240)


class Buf:
    __slots__ = ("name", "w", "r")

    def __init__(self, name=""):
        self.name = name
        self.w = None
        self.r = []


class Op:
    __slots__ = ("eng", "fn", "deps", "dma", "sig", "sem", "val", "idx")


class Sched:
    ENGS = ("pe", "act", "dve", "pool", "sp")

    def __init__(self, nc, n_dma_sems=32):
        self.nc = nc
        self.ops = []
        self.n_dma_sems = n_dma_sems

    dry = False

    def op(self, eng, fn, reads=(), writes=(), dma=False, **kw):
        if self.dry:
            return None
        o = Op()
        o.eng = eng
        o.fn = (fn, kw)
        o.dma = dma
        o.sig = False
        o.sem = None
        o.val = 0
        o.idx = len(self.ops)
        deps = set()
        for b in reads:
            if b.w is not None:
                deps.add(b.w)
        for b in writes:
            if b.w is not None:
                deps.add(b.w)
            for r in b.r:
                deps.add(r)
        deps.discard(o)
        o.deps = deps
        for b in reads:
            b.r.append(o)
        for b in writes:
            b.w = o
            b.r = []
        self.ops.append(o)
        return o

    def emit(self, stack):
        nc = self.nc
        engh = {"pe": nc.tensor, "act": nc.scalar, "dve": nc.vector, "pool": nc.gpsimd, "sp": nc.sync}
        esem = {e: stack.enter_context(nc.semaphore("s_" + e)) for e in self.ENGS}
        dsem = [stack.enter_context(nc.semaphore("d%d" % i)) for i in range(self.n_dma_sems)]
        for o in self.ops:
            for d in o.deps:
                if d.dma:
                    continue
                if d.eng == o.eng and o.eng in ("pe", "sp") and not o.dma:
                    continue
                d.sig = True
        cnt = {e: 0 for e in self.ENGS}
        NSW = 8
        dsem_sw = [stack.enter_context(nc.semaphore("w%d" % i)) for i in range(NSW)]
        pools = {"sp": (dsem, [0] * self.n_dma_sems, [None] * self.n_dma_sems, [0]),
                 "pool": (dsem_sw, [0] * NSW, [None] * NSW, [0])}
        for o in self.ops:
            if o.dma:
                sems, dcnt_, dlast_, nd_ = pools["pool" if o.eng == "pool" else "sp"]
                k = nd_[0] % len(sems)
                nd_[0] += 1
                if dlast_[k] is not None:
                    o.deps.add(dlast_[k])
                dlast_[k] = o
                dcnt_[k] += 16
                o.sem = sems[k]
                o.val = dcnt_[k]
                o.sig = True
            elif o.sig:
                cnt[o.eng] += 1
                o.sem = esem[o.eng]
                o.val = cnt[o.eng]
        waited = {}
        nwait = 0
        for o in self.ops:
            e = engh[o.eng]
            need = {}
            for d in o.deps:
                if not d.dma and d.eng == o.eng and o.eng in ("pe", "sp") and not o.dma:
                    continue
                if not d.dma and d.eng == o.eng and o.eng == "sp":
                    continue
                key = id(d.sem)
                if key not in need or need[key][1] < d.val:
                    need[key] = (d.sem, d.val)
            pend = []
            for key, (sem, val) in need.items():
                wk = (o.eng, key)
                if waited.get(wk, 0) >= val:
                    continue
                waited[wk] = val
                pend.append((sem, val))
            for sem, val in pend[:-1]:
                e.wait_ge(sem, val)
                nwait += 1
            ins = getattr(e, o.fn[0])(**o.fn[1])
            if pend:
                ins._wait_ge(pend[-1][0], pend[-1][1])
            if o.sig:
                ins.then_inc(o.sem, 16 if o.dma else 1)
        for sems, dcnt_, _, _ in pools.values():
            for k in range(len(sems)):
                if dcnt_[k] > 0:
                    nc.sync.wait_ge(sems[k], dcnt_[k])
        return len(self.ops), nwait


def _host_consts():
    hh = np.arange(H, dtype=np.float64)
    g = 1.0 - np.exp2(-5.0 - hh)
    j = np.arange(128, dtype=np.float64)
    ginv = g[None, :] ** (-(j[:, None] + 1.0))
    causal = (j[:, None] <= j[None, :]).astype(np.float64)
    maskT = (ginv[:, :, None] * causal[:, None, :] / 16.0).astype(np.float32)
    zs = (g[None, :] ** (127.0 - j[:, None]) / 16.0).astype(np.float32)
    xi = g[None, :] ** (j[:, None] + 1.0)
    epsp = (GN_EPS / (xi * xi)).astype(np.float32)
    gC = [float(np.float32(gg ** 128.0)) for gg in g]
    half = 128
    inv_freq = (np.float32(10000.0) ** (-(np.arange(half, dtype=np.float32) / np.float32(half)))).astype(np.float32)
    ext = np.concatenate([zs.astype(np.float64) * (g[None, :] ** 128.0), ginv / 16.0,
                          epsp.astype(np.float64) * (g[None, :] ** -256.0)], axis=1).astype(np.float32)
    return maskT, zs, epsp, gC, inv_freq.reshape(128, 1), ext


def build_program(NBLK):
    NTOK = NBLK * T
    nc = bass.Bass("TRN2", target_bir_lowering=False)
    maskT_np, zs_np, epsp_np, gC, _, _ext = _host_consts()
    g_np = [1.0 - 2.0 ** (-5.0 - hh) for hh in range(H)]
    gC2 = [float(np.float32(gg ** 256.0)) for gg in g_np]
    gm128 = [float(np.float32(gg ** -128.0)) for gg in g_np]

    def din(name, shape, dt=F32):
        return nc.dram_tensor(name, list(shape), dt, kind="ExternalInput").ap()

    x_own = din("x_own", [NTOK, D])
    x_prev = din("x_prev", [NTOK, D])
    pos_own = din("pos_own", [1, NTOK], I32)
    pos_prev = din("pos_prev", [1, NTOK], I32)
    flag_d = din("flag", [128, 1])
    cT_d = din("cT", [128, KC])
    w_ada = din("w_ada", [D, 3 * D])
    b_adaT_d = din("b_adaT", [128, 16])
    b_gate_d = din("b_gate", [1, D])
    pre_gT_d = din("pre_gT", [128, KC])
    w_in = din("w_in", [D, IN_W])
    conv_wT_d = din("conv_wT", [128, KC * CONVW])
    conv_bT_d = din("conv_bT", [128, KC])
    ln_gT_d = din("ln_gT", [128, KC])
    ln_bT_d = din("ln_bT", [128, KC])
    w_ro = din("w_ret_out", [2 * D, D])
    w_co = din("w_conv_out", [D, D])
    w_o = din("w_out", [D, D])
    post_g_d = din("post_g", [1, D])
    ident_d = din("ident", [128, 128])
    maskT_d = din("maskT", [128, H * 128])
    zs_d = din("zs", [128, H])
    epsp_d = din("epsp", [128, H])
    invf_d = din("inv_freq", [128, 1])
    ext_d = din("ext", [128, 12])
    out_d = nc.dram_tensor("out", [NTOK, D], F32, kind="ExternalOutput").ap()

    units = {}
    ulist = []

    def add_unit(name):
        units[name] = len(ulist)
        ulist.append(name)

    for h in range(H):
        for nm in ("Q%d", "K%d", "V%d_0", "V%d_1", "Z%d_0", "Z%d_1"):
            add_unit(nm % h)
    for nm in ("UV", "UG", "ZC", "GA", "GB"):
        for i in range(4):
            add_unit("%s%d" % (nm, i))
    for i in range(8):
        add_unit("RO%d" % i)
    for i in range(4):
        add_unit("CO%d" % i)
    for i in range(4):
        add_unit("WO%d" % i)
    for c in range(KC):
        add_unit("DG%d" % c)
    NU = len(ulist)
    wsc = nc.dram_tensor("wsc", [NU, 128, 2048], BF16).ap()
    B_wsc = [Buf("wsc%d" % i) for i in range(NU)]

    S = Sched(nc)
    sb = nc.alloc_sbuf_tensor
    ps = nc.alloc_psum_tensor

    def V(m, reads, writes, **kw):
        S.op("dve", m, reads, writes, **kw)

    def A(m, reads, writes, **kw):
        S.op("act", m, reads, writes, **kw)

    def G(m, reads, writes, **kw):
        S.op("pool", m, reads, writes, **kw)

    def PE(m, reads, writes, **kw):
        S.op("pe", m, reads, writes, **kw)

    def DMA(reads, writes, **kw):
        S.op("sp", "dma_start", reads, writes, dma=True, **kw)

    def DMA2(reads, writes, **kw):
        S.op("pool", "dma_start", reads, writes, dma=True, **kw)

    ident_f = sb("ident_f", [128, 128], F32)
    ident_b = sb("ident_b", [128, 128], BF16)
    ones_b = sb("ones_b", [128, 128], BF16)
    maskT = sb("maskT_sb", [128, H, 128], F32)
    zs = sb("zs_sb", [128, H], F32)
    epsp = sb("epsp_sb", [128, H], F32)
    invf = sb("invf_sb", [128, 1], F32)
    ext = sb("ext_sb", [128, 12], F32)
    flag = sb("flagt", [128, 1], F32)
    cT = sb("cTt", [128, KC], F32)
    cT_b = sb("cT_b", [128, KC], BF16)
    bT = sb("bT", [128, KC, T], BF16)
    bT_b = [Buf("bT%d" % c) for c in range(KC)]
    cbc = bT[:, 0:2, :].rearrange("p a (b c) -> p (a b) c", c=128)
    b_adaT = sb("b_adaTt", [128, 16], F32)
    pre_gT = sb("pre_gTt", [128, KC], F32)
    gs = sb("gs", [128, KC], F32)
    shiftT = sb("shiftT", [128, KC], F32)
    conv_w05 = sb("conv_w05", [128, KC, CONVW], F32)
    conv_bT = sb("conv_bTt", [128, KC], F32)
    ln_gT = sb("ln_gTt", [128, KC], F32)
    ln_bT = sb("ln_bTt", [128, KC], F32)
    pgg = sb("pgg", [128, D], F32)
    xr = [sb("xr%d" % i, [128, D], F32) for i in range(2)]
    xr_b = [Buf("xr%d" % i) for i in range(2)]
    bgate, bgate_b = xr[1], xr_b[1]
    B_const = Buf("const")
    B_c2 = Buf("const2")

    def ld_const(dst_ap, src_ap, buf=B_const):
        DMA([], [buf], out=dst_ap, in_=src_ap)

    ld_const(ident_f[:], ident_d[:, :])
    ld_const(maskT[:].rearrange("p h i -> p (h i)"), maskT_d[:, :])
    ld_const(zs[:], zs_d[:, :])
    ld_const(epsp[:], epsp_d[:, :])
    ld_const(invf[:], invf_d[:, :])
    ld_const(ext[:], ext_d[:, :])
    ld_const(flag[:], flag_d[:, :])
    ld_const(cT[:], cT_d[:, :])
    ld_const(b_adaT[:], b_adaT_d[:, :])
    ld_const(pre_gT[:], pre_gT_d[:, :])
    ld_const(conv_w05[:].rearrange("p c k -> p (c k)"), conv_wT_d[:, :])
    ld_const(conv_bT[:], conv_bT_d[:, :])
    ld_const(ln_gT[:], ln_gT_d[:, :])
    ld_const(ln_bT[:], ln_bT_d[:, :])
    ld_const(pgg[:], post_g_d[0:1, :].to_broadcast([128, D]))
    ld_const(bgate[:], b_gate_d[0:1, :].to_broadcast([128, D]), bgate_b)
    V("tensor_copy", [B_const], [B_c2], out=ident_b[:], in_=ident_f[:])
    V("memset", [], [B_c2], ap=ones_b[:], constant=1.0)
    V("tensor_copy", [B_const], [B_c2], out=cT_b[:], in_=cT[:])
    for kc in range(KC):
        V("tensor_copy", [B_const], [B_c2, bT_b[0], bT_b[1]], out=cbc[:, kc, :], in_=cT[:, kc:kc + 1].to_broadcast([128, 128]))
    V("tensor_scalar", [B_const], [B_const], out=conv_w05[:], in0=conv_w05[:], scalar1=0.5, scalar2=None, op0=ALU.mult)

    pbank = [ps("pb%d" % i, [128, 512], F32) for i in range(6)]
    pbuf = [Buf("pb%d" % i) for i in range(6)]
    ptr = [ps("ptr%d" % i, [128, 1024], BF16) for i in range(2)]
    ptrbuf = [Buf("ptr0"), Buf("ptr1")]
    rr = {}

    def nxt(key, n):
        i = rr.get(key, 0)
        rr[key] = i + 1
        return i % n

    def pring():
        i = nxt("pring", 4)
        return pbank[i], pbuf[i]

    def ptring():
        i = nxt("ptring", 2)
        return ptr[i][:, 0:512], ptrbuf[i]

    stg_f = [sb("stg_f%d" % i, [128, 4096], F32) for i in range(2)]
    stg_fb = [Buf("stg_f%d" % i) for i in range(2)]
    stg_b0 = sb("stg_b0", [128, 4096], BF16)
    rT = sb("rT", [128, 16, T], BF16)
    stg_bap = [stg_b0[:], rT[:, 0:8, :].rearrange("p a b -> p (a b)")]
    stg_bb = [Buf("stg_b%d" % i) for i in range(2)]
    vvbuf = sb("vvbuf", [128, 4096], BF16)

    NSLOT = 5
    wslot = [sb("wslot%d" % i, [128, 2048], BF16) for i in range(NSLOT)]
    wslot_b = [Buf("wslot%d" % i) for i in range(NSLOT)]

    cast_mode = {"pool_only": False}

    def cast_op(dst, src, reads, writes):
        k = 1 if cast_mode["pool_only"] else nxt("cast", 3)
        if k == 0:
            V("tensor_copy", reads, writes, out=dst, in_=src)
        elif k == 1:
            A("activation", reads, writes, out=dst, in_=src, func=AF.Copy)
        else:
            G("tensor_copy", reads, writes, out=dst, in_=src)

    def stage_load_issue(src_ap3, nA):
        i = nxt("stg", 2)
        sf, sfb = stg_f[i], stg_fb[i]
        sf3 = sf[:].rearrange("p (a b) -> p a b", a=nA)
        (DMA2 if cast_mode["pool_only"] else DMA)([], [sfb], out=sf3, in_=src_ap3)
        return i

    def stage_load_cast(i, nA):
        sf, sfb, sbap, sbb = stg_f[i], stg_fb[i], stg_bap[i], stg_bb[i]
        sb3 = sbap.rearrange("p (a b) -> p a b", a=nA)
        cast_op(sbap, sf[:], [sfb], [sbb])
        return sb3, sbb

    def stage_load(src_ap3, nA):
        i = stage_load_issue(src_ap3, nA)
        return stage_load_cast(i, nA)

    def convert_finish(i, nA, nB, unit_names):
        sb3, sbb = stage_load_cast(i, nA)
        hb = nB // 2
        for q, un in enumerate(unit_names):
            dst = wsc[units[un]].rearrange("p (a b) -> p a b", a=nA)
            (DMA2 if cast_mode["pool_only"] else DMA)([sbb], [B_wsc[units[un]]], out=dst, in_=sb3[:, :, q * hb:(q + 1) * hb])

    def convert(src_ap3, nA, nB, unit_names):
        i = stage_load_issue(src_ap3, nA)
        convert_finish(i, nA, nB, unit_names)

    def kview(w, c0, n):
        return w[:, c0:c0 + n].rearrange("(kc p) n -> p kc n", p=128)

    def conv_win(nm, off, i):
        convert(kview(w_in, off + i * 512, 512), KC, 512, ["%s%d" % (nm, 2 * i), "%s%d" % (nm, 2 * i + 1)])

    def gen_convert_first():
        for hp in range(2):
            conv_win("K", OFF_K, hp)
            yield
        for h in range(3):
            convert(kview(w_in, OFF_V + h * 512, 512), KC, 512, ["V%d_0" % h, "V%d_1" % h])
            yield

    def gen_diag_units():
        for c in range(KC):
            i = nxt("stg", 2)
            sbap, sbb = stg_bap[i], stg_bb[i]
            d3 = sbap[:, 0:2048].rearrange("p (a b) -> p a b", a=16)
            for t in range(16):
                A("activation", [B_const], [sbb], out=d3[:, t, :], in_=ident_f[:], func=AF.Copy,
                  scale=conv_w05[:, c, 2 * t:2 * t + 1])
            DMA2([sbb], [B_wsc[units["DG%d" % c]]], out=wsc[units["DG%d" % c]], in_=sbap[:, 0:2048])
            yield

    def rest_specs():
        sp = [(kview(w_in, OFF_V + 3 * 512, 512), KC, 512, ["V3_0", "V3_1"])]
        for i in range(2):
            for nm, off in (("UG", OFF_UG), ("UV", OFF_UV)):
                sp.append((kview(w_in, off + i * 512, 512), KC, 512, ["%s%d" % (nm, 2 * i), "%s%d" % (nm, 2 * i + 1)]))
        for i in range(2):
            sp.append((kview(w_in, OFF_ZC + i * 512, 512), KC, 512, ["ZC%d" % (2 * i), "ZC%d" % (2 * i + 1)]))
        for hp in range(2):
            sp.append((kview(w_in, OFF_Q + hp * 512, 512), KC, 512, ["Q%d" % (2 * hp), "Q%d" % (2 * hp + 1)]))
        for h in range(H):
            sp.append((kview(w_in, OFF_Z + h * 512, 512), KC, 512, ["Z%d_0" % h, "Z%d_1" % h]))
        for nm, off in (("GA", OFF_GA), ("GB", OFF_GB)):
            for i in range(2):
                sp.append((kview(w_in, off + i * 512, 512), KC, 512, ["%s%d" % (nm, 2 * i), "%s%d" % (nm, 2 * i + 1)]))
        for i in range(2):
            sp.append((kview(w_co, i * 512, 512), KC, 512, ["CO%d" % (2 * i), "CO%d" % (2 * i + 1)]))
        for i in range(4):
            sp.append((kview(w_ro, i * 256, 256), 16, 256, ["RO%d" % (2 * i), "RO%d" % (2 * i + 1)]))
        for i in range(2):
            sp.append((kview(w_o, i * 512, 512), KC, 512, ["WO%d" % (2 * i), "WO%d" % (2 * i + 1)]))
        return sp

    def gen_convert_rest():
        sp = rest_specs()
        slots = [stage_load_issue(sp[0][0], sp[0][1])]
        for i in range(len(sp)):
            if i + 1 < len(sp):
                slots.append(stage_load_issue(sp[i + 1][0], sp[i + 1][1]))
            convert_finish(slots[i], sp[i][1], sp[i][2], sp[i][3])
            yield

    pmod, pmodb = pbank[4], pbuf[4]
    for blk in range(4):
        sb3, sbb = stage_load(kview(w_ada, blk * 512, 512), KC)
        for jj in range(4):
            j = blk * 4 + jj
            for kc in range(KC):
                PE("matmul", [sbb, B_c2], [pmodb], out=pmod[:, j:j + 1], lhsT=sb3[:, kc, jj * 128:(jj + 1) * 128],
                   rhs=cT_b[:, kc:kc + 1], start=(kc == 0), stop=(kc == KC - 1))
    V("tensor_tensor", [pmodb, B_const], [B_c2], out=shiftT[:], in0=pmod[:, 0:8], in1=b_adaT[:, 0:8], op=ALU.add)
    V("tensor_tensor", [pmodb, B_const], [B_c2], out=gs[:], in0=pmod[:, 8:16], in1=b_adaT[:, 8:16], op=ALU.add)
    V("scalar_tensor_tensor", [B_c2, B_const], [B_c2], out=gs[:], in0=gs[:], scalar=1.0, in1=pre_gT[:],
      op0=ALU.add, op1=ALU.mult)
    def gen_gate_row():
        for blk in range(2):
            sb3, sbb = stage_load(kview(w_ada, 2048 + blk * 512, 512), KC)
            pg, pgb = pring()
            for kc in range(KC):
                PE("matmul", [sbb, B_c2, bT_b[0], bT_b[1]], [pgb], out=pg[:], lhsT=cbc[:, kc, :], rhs=sb3[:, kc, :],
                   start=(kc == 0), stop=(kc == KC - 1))
            sl = slice(blk * 512, (blk + 1) * 512)
            V("tensor_tensor", [pgb, bgate_b], [bgate_b], out=bgate[:, sl], in0=pg[:], in1=bgate[:, sl], op=ALU.add)
            V("tensor_tensor", [bgate_b, B_const], [B_pgg], out=pgg[:, sl], in0=pgg[:, sl], in1=bgate[:, sl], op=ALU.mult)
            yield

    B_pgg = Buf("pgg")

    ac = stg_f[0]
    ac_cb = [Buf("ac%d" % c) for c in range(KC)]
    lnt = stg_f[1]
    lnt_b = Buf("lnt")
    lns = [lnt[:, (6 + i) * T:(7 + i) * T] for i in range(2)]
    lns_b = [Buf("lns%d" % i) for i in range(2)]
    szT3 = [stg_b0[:, i * 2048:(i + 1) * 2048].rearrange("p (a b) -> p a b", a=4) for i in range(2)]
    szT_b = [Buf("szT0"), Buf("szT1")]
    vv = [vvbuf[:, i * 2048:(i + 1) * 2048].rearrange("p (n e) -> p n e", n=NT) for i in range(2)]
    vv_b = [[Buf("vv%d_%d" % (i, n)) for n in range(NT)] for i in range(2)]
    rT_b = [Buf("rT%d" % i) for i in range(H)]
    hT = sb("hT", [128, KC, T], BF16)
    hT_b = [Buf("hT%d" % c) for c in range(KC)]
    aT = sb("aT", [128, KC, HALO + T], BF16)
    aT_b = [Buf("aT%d" % c) for c in range(KC)]
    mT3 = aT[:, :, HALO:HALO + T]
    NXS = 2
    xs = [sb("xs%d" % i, [128, D], F32) for i in range(NXS)]
    xs_b = [Buf("xs%d" % i) for i in range(NXS)]
    xn = [sb("xn%d" % i, [128, D], BF16) for i in range(4)]
    xn_b = [Buf("xn%d" % i) for i in range(4)]
    st_f = sb("st_f", [128, H * 2, 512], F32)
    st_bf = sb("st_bf", [128, H * 2, 512], BF16)
    st_fb = [Buf("stf%d" % i) for i in range(H * 2)]
    st_bb = [Buf("stb%d" % i) for i in range(H * 2)]
    cos_t = sb("cos_t", [128, T], F32)
    sin_t = sb("sin_t", [128, T], F32)
    tab_b = Buf("tab")
    rtmp = [sb("rtmp%d" % i, [128, T], F32) for i in range(4)]
    rtmp_b = [Buf("rtmp%d" % i) for i in range(4)]
    qT = [sb("qT%d" % i, [128, 2, T], BF16) for i in range(2)]
    kT = [sb("kT%d" % i, [128, 2, T], BF16) for i in range(2)]
    kz = [sb("kz%d" % i, [128, NT, 256], BF16) for i in range(2)]
    qT_b = [[Buf("qT%d_%d" % (i, d)) for d in range(2)] for i in range(2)]
    kT_b = [[Buf("kT%d_%d" % (i, d)) for d in range(2)] for i in range(2)]
    kz_b = [[Buf("kz%d_%d" % (i, n)) for n in range(NT)] for i in range(2)]
    PT = [sb("PT%d" % i, [128, 128], BF16) for i in range(6)]
    PT_b = [Buf("PT%d" % i) for i in range(6)]
    on = [sb("on%d" % i, [128, 512], BF16) for i in range(2)]
    on_b = [Buf("on%d" % i) for i in range(2)]
    sm = [sb("sm%d" % i, [128, 16], F32) for i in range(4)]
    smi = [sb("smi%d" % i, [128, 2], I32) for i in range(4)]
    sm_b = [Buf("sm%d" % i) for i in range(4)]
    NDIAG = 30
    diag = [sb("diag%d" % i, [128, 128], BF16) for i in range(NDIAG)]
    diag_b = [Buf("diag%d" % i) for i in range(NDIAG)]
    acb16 = [sb("acb%d" % i, [128, T], BF16) for i in range(2)]
    sqb16 = [sb("sqb%d" % i, [128, T], BF16) for i in range(2)]
    acb16_b = [Buf("acb%d" % i) for i in range(2)]
    sqb16_b = [Buf("sqb%d" % i) for i in range(2)]
    szc = [sb("szc%d" % i, [128, T], BF16) for i in range(2)]
    szc_b = [Buf("szc%d" % i) for i in range(2)]
    print("sbuf bytes remaining/partition:", nc.sbuf_bytes_remaining)

    bar_t = sb("bar_t", [128, 1], F32)

    def staging_barrier():
        bar_w = ac_cb + [lnt_b] + szT_b + lns_b + rT_b + [bT_b[0], bT_b[1]]
        V("memset", [], stg_fb + stg_bb + bar_w, ap=bar_t[:], constant=0.0)

    def passA_units(last):
        s = []
        for h in range(H):
            s += ["K%d" % h, "V%d_0" % h, "V%d_1" % h]
        if last:
            for i in range(4):
                s += ["UG%d" % i, "UV%d" % i]
        return s

    def main_units():
        s = []
        for i in range(4):
            s += ["UG%d" % i, "UV%d" % i]
        s += ["ZC%d" % i for i in range(4)]
        for h in range(H):
            s += ["Q%d" % h, "K%d" % h, "V%d_0" % h, "V%d_1" % h, "Z%d_0" % h, "Z%d_1" % h]
        for i in range(4):
            s += ["GA%d" % i, "GB%d" % i, "CO%d" % i, "RO%d" % (2 * i), "RO%d" % (2 * i + 1)]
        s += ["WO%d" % i for i in range(4)] * 2
        return s

    useq = []
    wst = {"cons": 0, "loaded": 0}
    LOOK = NSLOT - 1
    slot_free = list(range(NSLOT))
    slot_of = {}
    held = {}

    def _load(j):
        sl = slot_free.pop(0)
        slot_of[j] = sl
        DMA([B_wsc[units[useq[j]]]], [wslot_b[sl]], out=wslot[sl][:], in_=wsc[units[useq[j]]])

    def _prefetch():
        while wst["loaded"] < len(useq) and slot_free and wst["loaded"] <= wst["cons"] + LOOK:
            _load(wst["loaded"])
            wst["loaded"] += 1

    def release(stream):
        if S.dry:
            return
        j = held.pop(stream, None)
        if j is not None:
            slot_free.append(slot_of.pop(j))
            _prefetch()

    def acquire(name, stream="M"):
        if S.dry:
            useq.append(name)
            return wslot[0], wslot_b[0]
        release(stream)
        i = wst["cons"]
        assert useq[i] == name, (i, useq[i], name)
        if i >= wst["loaded"]:
            assert wst["loaded"] == i and slot_free
            _load(i)
            wst["loaded"] += 1
        wst["cons"] += 1
        held[stream] = i
        _prefetch()
        sl = slot_of[i]
        return wslot[sl], wslot_b[sl]

    xseq = [("prev", t) for t in range(NBLK * NT)] + [("own", t) for t in range(NBLK * NT)]
    xst = {"cons": 0, "loaded": 0}

    def acquire_x():
        i = xst["cons"]
        xst["cons"] += 1
        while xst["loaded"] < len(xseq) and xst["loaded"] <= i + (NXS - 1):
            j = xst["loaded"]
            which, t = xseq[j]
            src = (x_prev if which == "prev" else x_own)[t * 128:(t + 1) * 128, :]
            DMA([], [xs_b[j % NXS]], out=xs[j % NXS][:], in_=src)
            xst["loaded"] += 1
        return xs[i % NXS], xs_b[i % NXS]

    def rsqrt(a_ap, y_ap, t_ap, i_ap, bufs, scalar=True):
        V("tensor_scalar", bufs, bufs, out=i_ap, in0=a_ap.bitcast(I32), scalar1=1, scalar2=None, op0=ALU.arith_shift_right)
        V("tensor_scalar", bufs, bufs, out=i_ap, in0=i_ap, scalar1=-1, scalar2=RSQRT_MAGIC, op0=ALU.mult, op1=ALU.add)
        cur = i_ap.bitcast(F32)
        for it in range(NEWTON_ITERS):
            if scalar:
                V("scalar_tensor_tensor", bufs, bufs, out=t_ap, in0=cur, scalar=a_ap, in1=cur, op0=ALU.mult, op1=ALU.mult)
            else:
                V("tensor_tensor", bufs, bufs, out=t_ap, in0=cur, in1=cur, op=ALU.mult)
                V("tensor_tensor", bufs, bufs, out=t_ap, in0=t_ap, in1=a_ap, op=ALU.mult)
            V("tensor_scalar", bufs, bufs, out=t_ap, in0=t_ap, scalar1=-0.5, scalar2=1.5, op0=ALU.mult, op1=ALU.add)
            V("tensor_tensor", bufs, bufs, out=y_ap, in0=cur, in1=t_ap, op=ALU.mult)
            cur = y_ap

    def pre_elem():
        for n in range(NT):
            xt, xtb = acquire_x()
            si = nxt("sm", 4)
            smt, smb = sm[si], sm_b[si]
            A("activation", [xtb], [xn_b[n], smb], out=xn[n][:], in_=xt[:], func=AF.Square, accum_out=smt[:, 0:1])
            V("tensor_scalar", [smb], [smb], out=smt[:, 1:2], in0=smt[:, 0:1], scalar1=1.0 / D, scalar2=RMS_EPS,
              op0=ALU.mult, op1=ALU.add)
            rsqrt(smt[:, 1:2], smt[:, 2:3], smt[:, 3:4], smi[si][:, 0:1], [smb])
            V("tensor_scalar", [xtb, smb], [xn_b[n]], out=xn[n][:], in0=xt[:], scalar1=smt[:, 2:3], scalar2=None,
              op0=ALU.mult)

    def pre_pe():
        for half in range(2):
            tl = [half * 2, half * 2 + 1]
            for kc2 in range(KC // 2):
                pt, ptb = ptring()
                for q in range(2):
                    kc = kc2 * 2 + q
                    for n2 in range(2):
                        PE("transpose", [xn_b[tl[n2]], B_c2], [ptb],
                           out=pt[:, (q * 2 + n2) * 128:(q * 2 + n2 + 1) * 128],
                           in_=xn[tl[n2]][:, kc * 128:(kc + 1) * 128], identity=ident_b[:])
                for q in range(2):
                    kc = kc2 * 2 + q
                    A("activation", [ptb, B_c2], [hT_b[kc]], out=hT[:, kc, half * 256:(half + 1) * 256],
                      in_=pt[:, q * 256:(q + 1) * 256], func=AF.Identity, bias=shiftT[:, kc:kc + 1], scale=gs[:, kc:kc + 1])

    def stage_tables(pos_ap, t0):
        tA, tA_b = rtmp[0], rtmp_b[0]
        tB, tB_b = rtmp[1], rtmp_b[1]
        posi, posi_b = rtmp[2][:].bitcast(I32), rtmp_b[2]
        DMA([], [posi_b], out=posi, in_=pos_ap[0:1, t0:t0 + T].to_broadcast([128, T]))
        V("tensor_copy", [posi_b], [tA_b], out=tA[:], in_=posi)
        V("tensor_scalar", [tA_b, B_const], [tA_b], out=tA[:], in0=tA[:], scalar1=invf[:, 0:1], scalar2=None, op0=ALU.mult)
        V("tensor_scalar", [tA_b], [posi_b], out=posi, in0=tA[:], scalar1=1.0 / TWO_PI, scalar2=None, op0=ALU.mult)
        V("tensor_copy", [posi_b], [tB_b], out=tB[:], in_=posi)
        V("scalar_tensor_tensor", [tA_b, tB_b], [tA_b], out=tA[:], in0=tB[:], scalar=-C1, in1=tA[:], op0=ALU.mult, op1=ALU.add)
        V("scalar_tensor_tensor", [tA_b, tB_b], [tA_b], out=tA[:], in0=tB[:], scalar=-C2, in1=tA[:], op0=ALU.mult, op1=ALU.add)
        V("tensor_scalar", [tA_b], [tA_b], out=tA[:], in0=tA[:], scalar1=PI_LO, scalar2=-PI_LO, op0=ALU.min, op1=ALU.max)
        A("activation", [tA_b], [tab_b], out=sin_t[:], in_=tA[:], func=AF.Sin)
        A("activation", [tA_b], [tB_b], out=tB[:], in_=tA[:], func=AF.Sin, scale=0.5)
        V("tensor_tensor", [tB_b], [tB_b], out=tB[:], in0=tB[:], in1=tB[:], op=ALU.mult)
        V("tensor_scalar", [tB_b], [tab_b], out=cos_t[:], in0=tB[:], scalar1=-2.0, scalar2=1.0, op0=ALU.mult, op1=ALU.add)

    def proj_fm(unit_t, col, nkc, rhs_fn, rhs_bufs, unit_buf):
        pb, pbb = pring()
        u3 = unit_t[:].rearrange("p (a b) -> p a b", a=nkc)
        for kc in range(nkc):
            PE("matmul", [unit_buf, rhs_bufs[kc]], [pbb], out=pb[:], lhsT=u3[:, kc, col:col + 128], rhs=rhs_fn(kc),
               start=(kc == 0), stop=(kc == nkc - 1))
        return pb, pbb

    def hT_rhs(kc):
        return hT[:, kc, :]

    def rotary(pa, pab, pb_, pbb_, dst, dstb, si):
        i0, i1 = nxt("rtmp", 4), nxt("rtmp", 4)
        V("tensor_tensor", [pab, tab_b], [rtmp_b[i0]], out=rtmp[i0][:], in0=pa[:], in1=cos_t[:], op=ALU.mult)
        V("tensor_tensor", [pbb_, tab_b], [rtmp_b[i1]], out=rtmp[i1][:], in0=pb_[:], in1=sin_t[:], op=ALU.mult)
        V("tensor_tensor", [rtmp_b[i0], rtmp_b[i1]], [dstb[si][0]], out=dst[si][:, 0, :], in0=rtmp[i0][:], in1=rtmp[i1][:],
          op=ALU.subtract)
        i2, i3 = nxt("rtmp", 4), nxt("rtmp", 4)
        V("tensor_tensor", [pab, tab_b], [rtmp_b[i2]], out=rtmp[i2][:], in0=pa[:], in1=sin_t[:], op=ALU.mult)
        V("tensor_tensor", [pbb_, tab_b], [rtmp_b[i3]], out=rtmp[i3][:], in0=pb_[:], in1=cos_t[:], op=ALU.mult)
        V("tensor_tensor", [rtmp_b[i2], rtmp_b[i3]], [dstb[si][1]], out=dst[si][:, 1, :], in0=rtmp[i2][:], in1=rtmp[i3][:],
          op=ALU.add)

    def head_kv(h, si, with_q, stream="M", c256=False):
        if with_q:
            u, ub = acquire("Q%d" % h, stream)
            pa, pab = proj_fm(u, 0, KC, hT_rhs, hT_b, ub)
            pb_, pbb_ = proj_fm(u, 128, KC, hT_rhs, hT_b, ub)
            rotary(pa, pab, pb_, pbb_, qT, qT_b, si)
            yield
        u, ub = acquire("K%d" % h, stream)
        pa, pab = proj_fm(u, 0, KC, hT_rhs, hT_b, ub)
        pb_, pbb_ = proj_fm(u, 128, KC, hT_rhs, hT_b, ub)
        rotary(pa, pab, pb_, pbb_, kT, kT_b, si)
        for n2 in range(2):
            pt, ptb = ptring()
            for nn in range(2):
                n = n2 * 2 + nn
                for dc in range(2):
                    PE("transpose", [kT_b[si][dc], B_c2], [ptb], out=pt[:, (nn * 2 + dc) * 128:(nn * 2 + dc + 1) * 128],
                       in_=kT[si][:, dc, n * 128:(n + 1) * 128], identity=ident_b[:])
            for nn in range(2):
                n = n2 * 2 + nn
                zsc = ext[:, h:h + 1] if (c256 and n % 2 == 0) else zs[:, h:h + 1]
                A("activation", [ptb, B_const], [kz_b[si][n]], out=kz[si][:, n, :], in_=pt[:, nn * 256:(nn + 1) * 256],
                  func=AF.Copy, scale=zsc)
        yield
        for half in range(2):
            u, ub = acquire("V%d_%d" % (h, half), stream)
            u3 = u[:].rearrange("p (a b) -> p a b", a=KC)
            for n in range(NT):
                pb, pbb = pring()
                for kc in range(KC):
                    PE("matmul", [ub, hT_b[kc]], [pbb], out=pb[:, 0:256], lhsT=hT[:, kc, n * 128:(n + 1) * 128],
                       rhs=u3[:, kc, :], start=(kc == 0), stop=(kc == KC - 1))
                dst = vv[si][:, n, half * 256:(half + 1) * 256]
                if (n + half) % 2 == 0:
                    A("activation", [pbb], [vv_b[si][n]], out=dst, in_=pb[:, 0:256], func=AF.Copy)
                else:
                    V("tensor_copy", [pbb], [vv_b[si][n]], out=dst, in_=pb[:, 0:256])
        yield

    def state_update256(h, si, cp):
        for dc in range(2):
            pb, pbb = pring()
            for q_, nn in enumerate((2 * cp, 2 * cp + 1)):
                PE("matmul", [kz_b[si][nn], vv_b[si][nn]], [pbb], out=pb[:], lhsT=kz[si][:, nn, dc * 128:(dc + 1) * 128],
                   rhs=vv[si][:, nn, :], start=(q_ == 0), stop=(q_ == 1))
            j = h * 2 + dc
            V("scalar_tensor_tensor", [pbb, st_fb[j]], [st_fb[j]], out=st_f[:, j, :], in0=st_f[:, j, :], scalar=gC2[h],
              in1=pb[:], op0=ALU.mult, op1=ALU.add)

    def state_update(h, si, n, need_bf):
        for dc in range(2):
            pb, pbb = pring()
            PE("matmul", [kz_b[si][n], vv_b[si][n]], [pbb], out=pb[:], lhsT=kz[si][:, n, dc * 128:(dc + 1) * 128],
               rhs=vv[si][:, n, :], start=True, stop=True)
            j = h * 2 + dc
            V("scalar_tensor_tensor", [pbb, st_fb[j]], [st_fb[j]], out=st_f[:, j, :], in0=st_f[:, j, :], scalar=gC[h],
              in1=pb[:], op0=ALU.mult, op1=ALU.add)
            if need_bf:
                A("activation", [st_fb[j]], [st_bb[j]], out=st_bf[:, j, :], in_=st_f[:, j, :], func=AF.Copy)

    def stage_glu(stream="M"):
        for i in range(4):
            ug, ugb = acquire("UG%d" % i, stream)
            tg = []
            for q in range(2):
                pg, pgb = proj_fm(ug, q * 128, KC, hT_rhs, hT_b, ugb)
                li = nxt("lns", 2)
                A("activation", [pgb], [lns_b[li]], out=lns[li], in_=pg[:], func=AF.Tanh, scale=0.5)
                tg.append(li)
                yield
            uv, uvb = acquire("UV%d" % i, stream)
            for q in range(2):
                c = i * 2 + q
                pv, pvb = proj_fm(uv, q * 128, KC, hT_rhs, hT_b, uvb)
                li = tg[q]
                V("scalar_tensor_tensor", [pvb, lns_b[li]], [aT_b[c]], out=aT[:, c, HALO:HALO + T], in0=lns[li], scalar=1.0,
                  in1=pv[:], op0=ALU.add, op1=ALU.mult)
                yield
        release(stream)

    def halo_shift(use_flag):
        for c in range(KC):
            if use_flag:
                V("tensor_scalar", [aT_b[c], B_const], [aT_b[c]], out=aT[:, c, 0:HALO], in0=aT[:, c, T:T + HALO],
                  scalar1=flag[:, 0:1], scalar2=None, op0=ALU.mult)
            else:
                V("tensor_copy", [aT_b[c]], [aT_b[c]], out=aT[:, c, 0:HALO], in_=aT[:, c, T:T + HALO])

    def stage_conv(stream="M"):
        L = lambda i: lnt[:, i * T:(i + 1) * T]
        def gen_odd(c):
            tl = {}
            for k in range(1, CONVW, 2):
                di = nxt("diag", NDIAG)
                A("activation", [B_const], [diag_b[di]], out=diag[di][:], in_=ident_f[:], func=AF.Copy,
                  scale=conv_w05[:, c, k:k + 1])
                tl[k] = di
            return tl

        odd_next = gen_odd(0)
        pend_stats = None
        for c in range(KC):
            odd = odd_next
            if c + 1 < KC:
                odd_next = gen_odd(c + 1)
            dg, dgb = acquire("DG%d" % c, stream)
            dg3 = dg[:].rearrange("p (a b) -> p a b", a=16)
            pb, pbb = pring()
            for k in range(CONVW):
                if k % 2 == 0:
                    lhs, lb_ = dg3[:, k // 2, :], dgb
                else:
                    lhs, lb_ = diag[odd[k]][:], diag_b[odd[k]]
                PE("matmul", [lb_, aT_b[c]], [pbb], out=pb[:], lhsT=lhs, rhs=aT[:, c, k:k + T],
                   start=(k == 0), stop=(k == CONVW - 1))
            acs = ac[:, c * T:(c + 1) * T]
            A("activation", [pbb, B_const], [ac_cb[c]], out=acs, in_=pb[:], func=AF.Identity, bias=conv_bT[:, c:c + 1])
            ai = nxt("acb", 2)
            A("activation", [ac_cb[c]], [acb16_b[ai]], out=acb16[ai][:], in_=acs, func=AF.Copy)
            A("activation", [ac_cb[c]], [sqb16_b[ai]], out=sqb16[ai][:], in_=acs, func=AF.Square)
            def stats(c=c, ai=ai):
                ps_, psb_ = pring()
                PE("matmul", [acb16_b[ai], B_c2], [psb_], out=ps_[:], lhsT=ones_b[:], rhs=acb16[ai][:], start=True, stop=True)
                pq_, pqb_ = pring()
                PE("matmul", [sqb16_b[ai], B_c2], [pqb_], out=pq_[:], lhsT=ones_b[:], rhs=sqb16[ai][:], start=True, stop=True)
                if c == 0:
                    V("tensor_copy", [psb_], [lnt_b], out=L(0), in_=ps_[:])
                    V("tensor_copy", [pqb_], [lnt_b], out=L(1), in_=pq_[:])
                else:
                    V("tensor_tensor", [psb_, lnt_b], [lnt_b], out=L(0), in0=L(0), in1=ps_[:], op=ALU.add)
                    V("tensor_tensor", [pqb_, lnt_b], [lnt_b], out=L(1), in0=L(1), in1=pq_[:], op=ALU.add)
            if pend_stats is not None:
                pend_stats()
            pend_stats = stats
            yield
        pend_stats()
        mean, var, rstd, tmp, mr = L(0), L(1), L(2), L(3), L(4)
        lni = L(5).bitcast(I32)
        lb = [lnt_b]
        V("tensor_scalar", lb, lb, out=mean, in0=mean, scalar1=1.0 / D, scalar2=None, op0=ALU.mult)
        V("tensor_scalar", lb, lb, out=var, in0=var, scalar1=1.0 / D, scalar2=LN_EPS, op0=ALU.mult, op1=ALU.add)
        V("tensor_tensor", lb, lb, out=tmp, in0=mean, in1=mean, op=ALU.mult)
        V("tensor_tensor", lb, lb, out=var, in0=var, in1=tmp, op=ALU.subtract)
        rsqrt(var, rstd, tmp, lni, lb, scalar=False)
        V("tensor_tensor", lb, lb, out=mr, in0=mean, in1=rstd, op=ALU.mult)
        for i in range(4):
            zc, zcb = acquire("ZC%d" % i, stream)
            for q in range(2):
                c = i * 2 + q
                acs = ac[:, c * T:(c + 1) * T]
                pz, pzb = proj_fm(zc, q * 128, KC, hT_rhs, hT_b, zcb)
                zi = nxt("szc", 2)
                A("activation", [pzb], [szc_b[zi]], out=szc[zi][:], in_=pz[:], func=AF.Silu)
                li = nxt("lns", 2)
                V("tensor_tensor", [ac_cb[c], lnt_b], [lns_b[li]], out=lns[li], in0=acs, in1=rstd, op=ALU.mult)
                V("tensor_tensor", [lns_b[li], lnt_b], [lns_b[li]], out=lns[li], in0=lns[li], in1=mr, op=ALU.subtract)
                A("activation", [lns_b[li], B_const], [lns_b[li]], out=lns[li], in_=lns[li], func=AF.Silu,
                  bias=ln_bT[:, c:c + 1], scale=ln_gT[:, c:c + 1])
                V("tensor_tensor", [lns_b[li], szc_b[zi]], [bT_b[c]], out=bT[:, c, :], in0=lns[li], in1=szc[zi][:], op=ALU.mult)
                yield
        release(stream)

    def head_proj(h, si, stream="M"):
        for _ in head_kv(h, si, True, stream, True):
            yield
        for half in range(2):
            u, ub = acquire("Z%d_%d" % (h, half), stream)
            for q in range(2):
                ech = half * 2 + q
                pz, pzb = proj_fm(u, q * 128, KC, hT_rhs, hT_b, ub)
                A("activation", [pzb], [szT_b[si]], out=szT3[si][:, ech, :], in_=pz[:], func=AF.Silu)
        yield

    def ret_scores(h, si):
        out = []
        for cp in range(NT // 2):
            ta = slice((2 * cp) * 128, (2 * cp + 1) * 128)
            tb_ = slice((2 * cp + 1) * 128, (2 * cp + 2) * 128)
            trip = []
            for kind, (tj, ti) in enumerate(((ta, ta), (ta, tb_), (tb_, tb_))):
                pS, pSb = pring()
                for dc in range(2):
                    PE("matmul", [kT_b[si][dc], qT_b[si][dc]], [pSb], out=pS[:, 0:128], lhsT=kT[si][:, dc, tj],
                       rhs=qT[si][:, dc, ti], start=(dc == 0), stop=(dc == 1))
                pi = nxt("PT", 6)
                if kind == 0:
                    V("tensor_tensor", [pSb, B_const], [PT_b[pi]], out=PT[pi][:], in0=pS[:, 0:128], in1=maskT[:, h, :],
                      op=ALU.mult)
                elif kind == 1:
                    V("tensor_scalar", [pSb, B_const], [PT_b[pi]], out=PT[pi][:], in0=pS[:, 0:128],
                      scalar1=ext[:, 4 + h:5 + h], scalar2=None, op0=ALU.mult)
                else:
                    V("scalar_tensor_tensor", [pSb, B_const], [PT_b[pi]], out=PT[pi][:], in0=pS[:, 0:128], scalar=gm128[h],
                      in1=maskT[:, h, :], op0=ALU.mult, op1=ALU.mult)
                trip.append(pi)
            out.append(trip)
        return out

    def ret_chunk(h, si, n, trip):
        tok = slice(n * 128, (n + 1) * 128)
        second = (n % 2 == 1)
        oi_ = nxt("pO", 2)
        pO, pOb = pbank[4 + oi_], pbuf[4 + oi_]
        for dc in range(2):
            j = h * 2 + dc
            PE("matmul", [qT_b[si][dc], st_bb[j]], [pOb], out=pO[:], lhsT=qT[si][:, dc, tok], rhs=st_bf[:, j, :],
               start=(dc == 0), stop=False)
        if not second:
            PE("matmul", [PT_b[trip[0]], vv_b[si][n]], [pOb], out=pO[:], lhsT=PT[trip[0]][:], rhs=vv[si][:, n, :],
               start=False, stop=True)
        else:
            PE("matmul", [PT_b[trip[1]], vv_b[si][n - 1]], [pOb], out=pO[:], lhsT=PT[trip[1]][:], rhs=vv[si][:, n - 1, :],
               start=False, stop=False)
            PE("matmul", [PT_b[trip[2]], vv_b[si][n]], [pOb], out=pO[:], lhsT=PT[trip[2]][:], rhs=vv[si][:, n, :],
               start=False, stop=True)
            for dc in range(2):
                pb, pbb = pring()
                for q_, nn in enumerate((n - 1, n)):
                    PE("matmul", [kz_b[si][nn], vv_b[si][nn]], [pbb], out=pb[:], lhsT=kz[si][:, nn, dc * 128:(dc + 1) * 128],
                       rhs=vv[si][:, nn, :], start=(q_ == 0), stop=(q_ == 1))
                j = h * 2 + dc
                V("scalar_tensor_tensor", [pbb, st_fb[j]], [st_fb[j]], out=st_f[:, j, :], in0=st_f[:, j, :], scalar=gC2[h],
                  in1=pb[:], op0=ALU.mult, op1=ALU.add)
                A("activation", [st_fb[j]], [st_bb[j]], out=st_bf[:, j, :], in_=st_f[:, j, :], func=AF.Copy)
        mi = nxt("sm", 4)
        smt, smb = sm[mi], sm_b[mi]
        V("bn_stats", [pOb], [smb], out=smt[:, 0:6], in_=pO[:])
        V("bn_aggr", [smb], [smb], out=smt[:, 6:8], in_=smt[:, 0:6])
        eps_ap = ext[:, 8 + h:9 + h] if second else epsp[:, h:h + 1]
        V("tensor_tensor", [smb, B_const], [smb], out=smt[:, 8:9], in0=smt[:, 7:8], in1=eps_ap, op=ALU.add)
        rsqrt(smt[:, 8:9], smt[:, 9:10], smt[:, 10:11], smi[mi][:, 0:1], [smb])
        V("scalar_tensor_tensor", [smb], [smb], out=smt[:, 11:12], in0=smt[:, 6:7], scalar=-1.0, in1=smt[:, 9:10],
          op0=ALU.mult, op1=ALU.mult)
        oi = nxt("on", 2)
        A("activation", [pOb, smb], [on_b[oi]], out=on[oi][:], in_=pO[:], func=AF.Identity, bias=smt[:, 11:12],
          scale=smt[:, 9:10])

        def tail():
            pt, ptb = ptring()
            for ech in range(4):
                PE("transpose", [on_b[oi], B_c2], [ptb], out=pt[:, ech * 128:(ech + 1) * 128],
                   in_=on[oi][:, ech * 128:(ech + 1) * 128], identity=ident_b[:])
            V("tensor_tensor", [ptb, szT_b[si]], [rT_b[h]], out=rT[:, h * 4:(h + 1) * 4, tok],
              in0=pt.rearrange("p (a b) -> p a b", a=4), in1=szT3[si][:, :, tok], op=ALU.mult)
        return tail

    def stage_ret(stream="M"):
        g = head_proj(0, 0, stream)
        for _ in g:
            yield
        pending = None
        for h in range(H):
            gn = head_proj(h + 1, (h + 1) % 2, stream) if h + 1 < H else None
            pis = ret_scores(h, h % 2)
            for n in range(NT):
                if gn is not None:
                    next(gn, None)
                    yield
                t = ret_chunk(h, h % 2, n, pis[n // 2])
                if pending is not None:
                    pending()
                pending = t
                yield
            if gn is not None:
                for _ in gn:
                    yield
        pending()
        release(stream)

    def run(g):
        for _ in g:
            pass

    def interleave(ga, gb_):
        a_live, b_live = True, True
        while a_live or b_live:
            if a_live:
                try:
                    next(ga)
                except StopIteration:
                    a_live = False
            if b_live:
                try:
                    next(gb_)
                except StopIteration:
                    b_live = False

    def conv_stream():
        for _ in stage_glu("X"):
            yield
        for _ in stage_conv("X"):
            yield
        halo_shift(False)

    def stage_merge():
        for i in range(4):
            ga, gab = acquire("GA%d" % i)
            ta = []
            for q in range(2):
                pg, pgb = proj_fm(ga, q * 128, KC, hT_rhs, hT_b, gab)
                li = nxt("rtmp", 4)
                A("activation", [pgb], [rtmp_b[li]], out=rtmp[li][:], in_=pg[:], func=AF.Tanh, scale=0.5)
                ta.append(li)
            gb_, gbb = acquire("GB%d" % i)
            tb = []
            for q in range(2):
                pg, pgb = proj_fm(gb_, q * 128, KC, hT_rhs, hT_b, gbb)
                li = nxt("rtmp", 4)
                A("activation", [pgb], [rtmp_b[li]], out=rtmp[li][:], in_=pg[:], func=AF.Tanh, scale=0.5)
                tb.append(li)
            co, cob = acquire("CO%d" % i)
            for q in range(2):
                py, pyb = proj_fm(co, q * 128, KC, lambda kc: bT[:, kc, :], bT_b, cob)
                li = tb[q]
                V("scalar_tensor_tensor", [pyb, rtmp_b[li]], [rtmp_b[li]], out=rtmp[li][:], in0=rtmp[li][:], scalar=1.0,
                  in1=py[:], op0=ALU.add, op1=ALU.mult)
            for q in range(2):
                dch = i * 2 + q
                ro, rob = acquire("RO%d" % dch)
                py, pyb = proj_fm(ro, 0, 16, lambda ec: rT[:, ec, :], [rT_b[ec // 4] for ec in range(16)], rob)
                la, lb_ = ta[q], tb[q]
                V("scalar_tensor_tensor", [pyb, rtmp_b[la]], [rtmp_b[la]], out=rtmp[la][:], in0=rtmp[la][:], scalar=1.0,
                  in1=py[:], op0=ALU.add, op1=ALU.mult)
                V("tensor_tensor", [rtmp_b[la], rtmp_b[lb_]], [aT_b[dch]], out=mT3[:, dch, :], in0=rtmp[la][:],
                  in1=rtmp[lb_][:], op=ALU.add)

    B_out = Buf("outd")

    def stage_out(blk, between=None):
        for tp in range(2):
            if tp == 1 and between is not None:
                between()
            tiles = (tp * 2, tp * 2 + 1)
            banks = {n: (pring(), pring()) for n in tiles}
            for ui in range(4):
                u, ub = acquire("WO%d" % ui)
                u3 = u[:].rearrange("p (a b) -> p a b", a=KC)
                cs = (ui % 2) * 256
                for n in tiles:
                    pb, pbb = banks[n][ui // 2]
                    for kc in range(KC):
                        PE("matmul", [ub, aT_b[kc]], [pbb], out=pb[:, cs:cs + 256], lhsT=mT3[:, kc, n * 128:(n + 1) * 128],
                           rhs=u3[:, kc, :], start=(kc == 0), stop=(kc == KC - 1))
            for n in tiles:
                (pb0, pbb0), (pb1, pbb1) = banks[n]
                t = blk * NT + n
                xi_ = t % 2
                DMA2([], [xr_b[xi_]], out=xr[xi_][:], in_=x_own[t * 128:(t + 1) * 128, :])
                mi = nxt("sm", 4)
                smt, smb = sm[mi], sm_b[mi]
                j0, j1 = nxt("rtmp", 4), nxt("rtmp", 4)
                A("activation", [pbb0], [rtmp_b[j0], smb], out=rtmp[j0][:], in_=pb0[:], func=AF.Square, accum_out=smt[:, 0:1])
                A("activation", [pbb1], [rtmp_b[j1], smb], out=rtmp[j1][:], in_=pb1[:], func=AF.Square, accum_out=smt[:, 1:2])
                V("tensor_tensor", [smb], [smb], out=smt[:, 2:3], in0=smt[:, 0:1], in1=smt[:, 1:2], op=ALU.add)
                V("tensor_scalar", [smb], [smb], out=smt[:, 3:4], in0=smt[:, 2:3], scalar1=1.0 / D, scalar2=4.0 * RMS_EPS,
                  op0=ALU.mult, op1=ALU.add)
                rsqrt(smt[:, 3:4], smt[:, 4:5], smt[:, 5:6], smi[mi][:, 0:1], [smb])
                for hf, (pb, pbb) in enumerate(((pb0, pbb0), (pb1, pbb1))):
                    cs = slice(hf * 512, (hf + 1) * 512)
                    li = nxt("lns", 2)
                    V("scalar_tensor_tensor", [pbb, smb, B_pgg], [lns_b[li]], out=lns[li], in0=pb[:], scalar=smt[:, 4:5],
                      in1=pgg[:, cs], op0=ALU.mult, op1=ALU.mult)
                    V("tensor_tensor", [lns_b[li], xr_b[xi_]], [xr_b[xi_]], out=xr[xi_][:, cs], in0=xr[xi_][:, cs],
                      in1=lns[li], op=ALU.add)
                DMA2([xr_b[xi_]], [B_out], out=out_d[t * 128:(t + 1) * 128, :], in_=xr[xi_][:])

    def body():
        for j in range(H * 2):
            V("memset", [], [st_fb[j]], ap=st_f[:, j, :], constant=0.0)
        for c in range(KC):
            V("memset", [], [aT_b[c]], ap=aT[:, c, :], constant=0.0)
        cast_mode["pool_only"] = False
        run(gen_convert_first())
        gc = gen_convert_rest()
        cast_mode["pool_only"] = True
        gd = gen_diag_units()
        if NBLK == 1:
            run(gc)
            run(gd)
            run(gen_gate_row())
            staging_barrier()
        pre_elem()
        pre_pe()
        stage_tables(pos_prev, 0)
        for blk in range(NBLK):
            run(head_kv(0, 0, False, "M", True))
            for h in range(H):
                si = h % 2
                if NBLK > 1 and blk < NBLK - 1:
                    next(gc, None)
                    next(gc, None)
                if h + 1 < H:
                    run(head_kv(h + 1, (h + 1) % 2, False, "M", True))
                if h == 1:
                    pre_elem()
                if h == 2 and blk + 1 < NBLK:
                    pre_pe()
                    stage_tables(pos_prev, (blk + 1) * T)
                for cp in range(NT // 2):
                    state_update256(h, si, cp)
            if NBLK > 1 and blk == NBLK - 2:
                run(gc)
                run(gd)
                run(gen_gate_row())
                staging_barrier()
            if blk == NBLK - 1:
                run(stage_glu())
                halo_shift(True)
        for j in range(H * 2):
            V("tensor_scalar", [st_fb[j], B_const], [st_fb[j]], out=st_f[:, j, :], in0=st_f[:, j, :], scalar1=flag[:, 0:1],
              scalar2=None, op0=ALU.mult)
            A("activation", [st_fb[j]], [st_bb[j]], out=st_bf[:, j, :], in_=st_f[:, j, :], func=AF.Copy)
        pre_pe()
        stage_tables(pos_own, 0)
        for blk in range(NBLK):
            release("M")
            interleave(conv_stream(), stage_ret("Y"))
            if blk + 1 < NBLK:
                pre_elem()
            stage_merge()
            if blk + 1 < NBLK:
                def nxt_blk(b=blk + 1):
                    pre_pe()
                    stage_tables(pos_own, b * T)
                stage_out(blk, nxt_blk)
            else:
                stage_out(blk)

    rr_save = dict(rr)
    S.dry = True
    body()
    S.dry = False
    rr.clear()
    rr.update(rr_save)
    xst["cons"] = 0
    xst["loaded"] = 0
    body()
    assert wst["cons"] == len(useq), (wst["cons"], len(useq))

    with ExitStack() as stack:
        nops, nwait = S.emit(stack)
    print("ops", nops, "waits", nwait)
    return nc, nops, nwait


_PROG_CACHE = {}


def kernel(x, c, positions, w_ada, b_ada, pre_norm_g, w_in, conv_w, conv_b, conv_ln_g, conv_ln_b,
           w_ret_out, w_conv_out, w_out, post_norm_g):
    x = np.asarray(x, dtype=np.float32)
    B, S_, _ = x.shape
    half = S_ // 2
    NBLK = half // T
    assert half % T == 0
    ncores = 2 * B
    if NBLK not in _PROG_CACHE:
        _PROG_CACHE[NBLK] = build_program(NBLK)[0]
    nc = _PROG_CACHE[NBLK]
    maskT, zs, epsp, gC, inv_freq, ext = _host_consts()
    c = np.asarray(c, np.float32)
    positions = np.asarray(positions, np.int32)
    f = lambda a: np.ascontiguousarray(np.asarray(a, np.float32))
    lay = lambda v, n: f(np.asarray(v, np.float32).reshape(n, 128).T)
    b_ada0 = np.asarray(b_ada, np.float32)[0]
    shared = {
        "w_ada": f(w_ada[0]), "b_adaT": lay(b_ada0[:2048], 16), "b_gate": f(b_ada0[2048:].reshape(1, D)),
        "pre_gT": lay(pre_norm_g[0], KC), "w_in": f(w_in[0]),
        "conv_wT": f(np.asarray(conv_w[0], np.float32).T.reshape(KC, 128, CONVW).transpose(1, 0, 2).reshape(128, KC * CONVW)),
        "conv_bT": lay(conv_b[0], KC), "ln_gT": lay(conv_ln_g[0], KC), "ln_bT": lay(conv_ln_b[0], KC),
        "w_ret_out": f(w_ret_out[0]), "w_conv_out": f(w_conv_out[0]), "w_out": f(w_out[0]),
        "post_g": f(np.asarray(post_norm_g[0], np.float32).reshape(1, D)),
        "ident": np.eye(128, dtype=np.float32), "maskT": f(maskT.reshape(128, H * 128)), "zs": f(zs), "epsp": f(epsp),
        "inv_freq": f(inv_freq), "ext": f(ext),
    }
    in_maps = []
    for b in range(B):
        for j in range(2):
            own = x[b, j * half:(j + 1) * half]
            prev = x[b, 0:half]
            m = dict(shared)
            m["x_own"] = np.ascontiguousarray(own)
            m["x_prev"] = np.ascontiguousarray(prev)
            m["pos_own"] = np.ascontiguousarray(positions[b, j * half:(j + 1) * half].reshape(1, half))
            m["pos_prev"] = np.ascontiguousarray(positions[b, 0:half].reshape(1, half))
            m["flag"] = np.full((128, 1), float(j), np.float32)
            m["cT"] = lay(c[b], KC)
            in_maps.append(m)
    res = run_bass_kernel_spmd(nc, in_maps, core_ids=list(range(ncores)))
    out = np.empty((B, S_, D), np.float32)
    for b in range(B):
        for j in range(2):
            out[b, j * half:(j + 1) * half] = res.results[b * 2 + j]["out"]
    return out
```

```python
import math
from contextlib import ExitStack

import numpy as np
import concourse.bass as bass
import concourse.mybir as mybir
from concourse.bass_utils import run_bass_kernel_spmd

F32 = mybir.dt.float32
BF16 = mybir.dt.bfloat16
I32 = mybir.dt.int32
AF = mybir.ActivationFunctionType
ALU = mybir.AluOpType

D = 1024
T = 512
NT = 4
KC = 8
H = 4
CONVW = 31
HALO = CONVW - 1
IN_W = 11264
RMS_EPS = 1e-6
GN_EPS = 1e-5
LN_EPS = 1e-5
TWO_PI = 2.0 * math.pi
C1 = 6.28125
C2 = TWO_PI - C1
PI_LO = 3.1415925
RSQRT_MAGIC = 0x5F3759DF
NEWTON_ITERS = 3

OFF_Q, OFF_K, OFF_V, OFF_Z, OFF_UV, OFF_UG, OFF_ZC, OFF_GA, OFF_GB = (
    0, 1024, 2048, 4096, 6144, 7168, 8192, 9216, 10240)


class Buf:
    __slots__ = ("name", "w", "r")

    def __init__(self, name=""):
        self.name = name
        self.w = None
        self.r = []


class Op:
    __slots__ = ("eng", "fn", "deps", "dma", "sig", "sem", "val", "idx")


class Sched:
    ENGS = ("pe", "act", "dve", "pool", "sp")

    def __init__(self, nc, n_dma_sems=32):
        self.nc = nc
        self.ops = []
        self.n_dma_sems = n_dma_sems

    dry = False

    def op(self, eng, fn, reads=(), writes=(), dma=False, **kw):
        if self.dry:
            return None
        o = Op()
        o.eng = eng
        o.fn = (fn, kw)
        o.dma = dma
        o.sig = False
        o.sem = None
        o.val = 0
        o.idx = len(self.ops)
        deps = set()
        for b in reads:
            if b.w is not None:
                deps.add(b.w)
        for b in writes:
            if b.w is not None:
                deps.add(b.w)
            for r in b.r:
                deps.add(r)
        deps.discard(o)
        o.deps = deps
        for b in reads:
            b.r.append(o)
        for b in writes:
            b.w = o
            b.r = []
        self.ops.append(o)
        return o

    def emit(self, stack):
        nc = self.nc
        engh = {"pe": nc.tensor, "act": nc.scalar, "dve": nc.vector, "pool": nc.gpsimd, "sp": nc.sync}
        esem = {e: stack.enter_context(nc.semaphore("s_" + e)) for e in self.ENGS}
        dsem = [stack.enter_context(nc.semaphore("d%d" % i)) for i in range(self.n_dma_sems)]
        for o in self.ops:
            for d in o.deps:
                if d.dma:
                    continue
                if d.eng == o.eng and o.eng in ("pe", "sp") and not o.dma:
                    continue
                d.sig = True
        cnt = {e: 0 for e in self.ENGS}
        NSW = 8
        dsem_sw = [stack.enter_context(nc.semaphore("w%d" % i)) for i in range(NSW)]
        pools = {"sp": (dsem, [0] * self.n_dma_sems, [None] * self.n_dma_sems, [0]),
                 "pool": (dsem_sw, [0] * NSW, [None] * NSW, [0])}
        for o in self.ops:
            if o.dma:
                sems, dcnt_, dlast_, nd_ = pools["pool" if o.eng == "pool" else "sp"]
                k = nd_[0] % len(sems)
                nd_[0] += 1
                if dlast_[k] is not None:
                    o.deps.add(dlast_[k])
                dlast_[k] = o
                dcnt_[k] += 16
                o.sem = sems[k]
                o.val = dcnt_[k]
                o.sig = True
            elif o.sig:
                cnt[o.eng] += 1
                o.sem = esem[o.eng]
                o.val = cnt[o.eng]
        waited = {}
        nwait = 0
        for o in self.ops:
            e = engh[o.eng]
            need = {}
            for d in o.deps:
                if not d.dma and d.eng == o.eng and o.eng in ("pe", "sp") and not o.dma:
                    continue
                if not d.dma and d.eng == o.eng and o.eng == "sp":
                    continue
                key = id(d.sem)
                if key not in need or need[key][1] < d.val:
                    need[key] = (d.sem, d.val)
            pend = []
            for key, (sem, val) in need.items():
                wk = (o.eng, key)
                if waited.get(wk, 0) >= val:
                    continue
                waited[wk] = val
                pend.append((sem, val))
            for sem, val in pend[:-1]:
                e.wait_ge(sem, val)
                nwait += 1
            ins = getattr(e, o.fn[0])(**o.fn[1])
            if pend:
                ins._wait_ge(pend[-1][0], pend[-1][1])
            if o.sig:
                ins.then_inc(o.sem, 16 if o.dma else 1)
        for sems, dcnt_, _, _ in pools.values():
            for k in range(len(sems)):
                if dcnt_[k] > 0:
                    nc.sync.wait_ge(sems[k], dcnt_[k])
        return len(self.ops), nwait


def _host_consts():
    hh = np.arange(H, dtype=np.float64)
    g = 1.0 - np.exp2(-5.0 - hh)
    j = np.arange(128, dtype=np.float64)
    ginv = g[None, :] ** (-(j[:, None] + 1.0))
    causal = (j[:, None] <= j[None, :]).astype(np.float64)
    maskT = (ginv[:, :, None] * causal[:, None, :] / 16.0).astype(np.float32)
    zs = (g[None, :] ** (127.0 - j[:, None]) / 16.0).astype(np.float32)
    xi = g[None, :] ** (j[:, None] + 1.0)
    epsp = (GN_EPS / (xi * xi)).astype(np.float32)
    gC = [float(np.float32(gg ** 128.0)) for gg in g]
    half = 128
    inv_freq = (np.float32(10000.0) ** (-(np.arange(half, dtype=np.float32) / np.float32(half)))).astype(np.float32)
    ext = np.concatenate([zs.astype(np.float64) * (g[None, :] ** 128.0), ginv / 16.0,
                          epsp.astype(np.float64) * (g[None, :] ** -256.0)], axis=1).astype(np.float32)
    return maskT, zs, epsp, gC, inv_freq.reshape(128, 1), ext


def build_program(NBLK):
    NTOK = NBLK * T
    nc = bass.Bass("TRN2", target_bir_lowering=False)
    maskT_np, zs_np, epsp_np, gC, _, _ext = _host_consts()
    g_np = [1.0 - 2.0 ** (-5.0 - hh) for hh in range(H)]
    gC2 = [float(np.float32(gg ** 256.0)) for gg in g_np]
    gm128 = [float(np.float32(gg ** -128.0)) for gg in g_np]

    def din(name, shape, dt=F32):
        return nc.dram_tensor(name, list(shape), dt, kind="ExternalInput").ap()

    x_own = din("x_own", [NTOK, D])
    x_prev = din("x_prev", [NTOK, D])
    pos_own = din("pos_own", [1, NTOK], I32)
    pos_prev = din("pos_prev", [1, NTOK], I32)
    flag_d = din("flag", [128, 1])
    cT_d = din("cT", [128, KC])
    w_ada = din("w_ada", [D, 3 * D])
    b_adaT_d = din("b_adaT", [128, 16])
    b_gate_d = din("b_gate", [1, D])
    pre_gT_d = din("pre_gT", [128, KC])
    w_in = din("w_in", [D, IN_W])
    conv_wT_d = din("conv_wT", [128, KC * CONVW])
    conv_bT_d = din("conv_bT", [128, KC])
    ln_gT_d = din("ln_gT", [128, KC])
    ln_bT_d = din("ln_bT", [128, KC])
    w_ro = din("w_ret_out", [2 * D, D])
    w_co = din("w_conv_out", [D, D])
    w_o = din("w_out", [D, D])
    post_g_d = din("post_g", [1, D])
    ident_d = din("ident", [128, 128])
    maskT_d = din("maskT", [128, H * 128])
    zs_d = din("zs", [128, H])
    epsp_d = din("epsp", [128, H])
    invf_d = din("inv_freq", [128, 1])
    ext_d = din("ext", [128, 12])
    out_d = nc.dram_tensor("out", [NTOK, D], F32, kind="ExternalOutput").ap()

    units = {}
    ulist = []

    def add_unit(name):
        units[name] = len(ulist)
        ulist.append(name)

    for h in range(H):
        for nm in ("Q%d", "K%d", "V%d_0", "V%d_1", "Z%d_0", "Z%d_1"):
            add_unit(nm % h)
    for nm in ("UV", "UG", "ZC", "GA", "GB"):
        for i in range(4):
            add_unit("%s%d" % (nm, i))
    for i in range(8):
        add_unit("RO%d" % i)
    for i in range(4):
        add_unit("CO%d" % i)
    for i in range(4):
        add_unit("WO%d" % i)
    for c in range(KC):
        add_unit("DG%d" % c)
    NU = len(ulist)
    wsc = nc.dram_tensor("wsc", [NU, 128, 2048], BF16).ap()
    B_wsc = [Buf("wsc%d" % i) for i in range(NU)]

    S = Sched(nc)
    sb = nc.alloc_sbuf_tensor
    ps = nc.alloc_psum_tensor

    def V(m, reads, writes, **kw):
        S.op("dve", m, reads, writes, **kw)

    def A(m, reads, writes, **kw):
        S.op("act", m, reads, writes, **kw)

    def G(m, reads, writes, **kw):
        S.op("pool", m, reads, writes, **kw)

    def PE(m, reads, writes, **kw):
        S.op("pe", m, reads, writes, **kw)

    def DMA(reads, writes, **kw):
        S.op("sp", "dma_start", reads, writes, dma=True, **kw)

    def DMA2(reads, writes, **kw):
        S.op("pool", "dma_start", reads, writes, dma=True, **kw)

    ident_f = sb("ident_f", [128, 128], F32)
    ident_b = sb("ident_b", [128, 128], BF16)
    ones_b = sb("ones_b", [128, 128], BF16)
    maskT = sb("maskT_sb", [128, H, 128], F32)
    zs = sb("zs_sb", [128, H], F32)
    epsp = sb("epsp_sb", [128, H], F32)
    invf = sb("invf_sb", [128, 1], F32)
    ext = sb("ext_sb", [128, 12], F32)
    flag = sb("flagt", [128, 1], F32)
    cT = sb("cTt", [128, KC], F32)
    cT_b = sb("cT_b", [128, KC], BF16)
    bT = sb("bT", [128, KC, T], BF16)
    bT_b = [Buf("bT%d" % c) for c in range(KC)]
    cbc = bT[:, 0:2, :].rearrange("p a (b c) -> p (a b) c", c=128)
    b_adaT = sb("b_adaTt", [128, 16], F32)
    pre_gT = sb("pre_gTt", [128, KC], F32)
    gs = sb("gs", [128, KC], F32)
    shiftT = sb("shiftT", [128, KC], F32)
    conv_w05 = sb("conv_w05", [128, KC, CONVW], F32)
    conv_bT = sb("conv_bTt", [128, KC], F32)
    ln_gT = sb("ln_gTt", [128, KC], F32)
    ln_bT = sb("ln_bTt", [128, KC], F32)
    pgg = sb("pgg", [128, D], F32)
    xr = [sb("xr%d" % i, [128, D], F32) for i in range(2)]
    xr_b = [Buf("xr%d" % i) for i in range(2)]
    bgate, bgate_b = xr[1], xr_b[1]
    B_const = Buf("const")
    B_c2 = Buf("const2")

    def ld_const(dst_ap, src_ap, buf=B_const):
        DMA([], [buf], out=dst_ap, in_=src_ap)

    ld_const(ident_f[:], ident_d[:, :])
    ld_const(maskT[:].rearrange("p h i -> p (h i)"), maskT_d[:, :])
    ld_const(zs[:], zs_d[:, :])
    ld_const(epsp[:], epsp_d[:, :])
    ld_const(invf[:], invf_d[:, :])
    ld_const(ext[:], ext_d[:, :])
    ld_const(flag[:], flag_d[:, :])
    ld_const(cT[:], cT_d[:, :])
    ld_const(b_adaT[:], b_adaT_d[:, :])
    ld_const(pre_gT[:], pre_gT_d[:, :])
    ld_const(conv_w05[:].rearrange("p c k -> p (c k)"), conv_wT_d[:, :])
    ld_const(conv_bT[:], conv_bT_d[:, :])
    ld_const(ln_gT[:], ln_gT_d[:, :])
    ld_const(ln_bT[:], ln_bT_d[:, :])
    ld_const(pgg[:], post_g_d[0:1, :].to_broadcast([128, D]))
    ld_const(bgate[:], b_gate_d[0:1, :].to_broadcast([128, D]), bgate_b)
    V("tensor_copy", [B_const], [B_c2], out=ident_b[:], in_=ident_f[:])
    V("memset", [], [B_c2], ap=ones_b[:], constant=1.0)
    V("tensor_copy", [B_const], [B_c2], out=cT_b[:], in_=cT[:])
    for kc in range(KC):
        V("tensor_copy", [B_const], [B_c2, bT_b[0], bT_b[1]], out=cbc[:, kc, :], in_=cT[:, kc:kc + 1].to_broadcast([128, 128]))
    V("tensor_scalar", [B_const], [B_const], out=conv_w05[:], in0=conv_w05[:], scalar1=0.5, scalar2=None, op0=ALU.mult)

    pbank = [ps("pb%d" % i, [128, 512], F32) for i in range(6)]
    pbuf = [Buf("pb%d" % i) for i in range(6)]
    ptr = [ps("ptr%d" % i, [128, 1024], BF16) for i in range(2)]
    ptrbuf = [Buf("ptr0"), Buf("ptr1")]
    rr = {}

    def nxt(key, n):
        i = rr.get(key, 0)
        rr[key] = i + 1
        return i % n

    def pring():
        i = nxt("pring", 4)
        return pbank[i], pbuf[i]

    def ptring():
        i = nxt("ptring", 2)
        return ptr[i][:, 0:512], ptrbuf[i]

    stg_f = [sb("stg_f%d" % i, [128, 4096], F32) for i in range(2)]
    stg_fb = [Buf("stg_f%d" % i) for i in range(2)]
    stg_b0 = sb("stg_b0", [128, 4096], BF16)
    rT = sb("rT", [128, 16, T], BF16)
    stg_bap = [stg_b0[:], rT[:, 0:8, :].rearrange("p a b -> p (a b)")]
    stg_bb = [Buf("stg_b%d" % i) for i in range(2)]
    vvbuf = sb("vvbuf", [128, 4096], BF16)

    NSLOT = 5
    wslot = [sb("wslot%d" % i, [128, 2048], BF16) for i in range(NSLOT)]
    wslot_b = [Buf("wslot%d" % i) for i in range(NSLOT)]

    cast_mode = {"pool_only": False}

    def cast_op(dst, src, reads, writes):
        k = 1 if cast_mode["pool_only"] else nxt("cast", 3)
        if k == 0:
            V("tensor_copy", reads, writes, out=dst, in_=src)
        elif k == 1:
            A("activation", reads, writes, out=dst, in_=src, func=AF.Copy)
        else:
            G("tensor_copy", reads, writes, out=dst, in_=src)

    def stage_load_issue(src_ap3, nA):
        i = nxt("stg", 2)
        sf, sfb = stg_f[i], stg_fb[i]
        sf3 = sf[:].rearrange("p (a b) -> p a b", a=nA)
        (DMA2 if cast_mode["pool_only"] else DMA)([], [sfb], out=sf3, in_=src_ap3)
        return i

    def stage_load_cast(i, nA):
        sf, sfb, sbap, sbb = stg_f[i], stg_fb[i], stg_bap[i], stg_bb[i]
        sb3 = sbap.rearrange("p (a b) -> p a b", a=nA)
        cast_op(sbap, sf[:], [sfb], [sbb])
        return sb3, sbb

    def stage_load(src_ap3, nA):
        i = stage_load_issue(src_ap3, nA)
        return stage_load_cast(i, nA)

    converted = set()

    def convert_finish(i, nA, nB, unit_names):
        converted.update(unit_names)
        sb3, sbb = stage_load_cast(i, nA)
        hb = nB // 2
        for q, un in enumerate(unit_names):
            dst = wsc[units[un]].rearrange("p (a b) -> p a b", a=nA)
            (DMA2 if cast_mode["pool_only"] else DMA)([sbb], [B_wsc[units[un]]], out=dst, in_=sb3[:, :, q * hb:(q + 1) * hb])

    def convert(src_ap3, nA, nB, unit_names):
        i = stage_load_issue(src_ap3, nA)
        convert_finish(i, nA, nB, unit_names)

    def kview(w, c0, n):
        return w[:, c0:c0 + n].rearrange("(kc p) n -> p kc n", p=128)

    def conv_win(nm, off, i):
        convert(kview(w_in, off + i * 512, 512), KC, 512, ["%s%d" % (nm, 2 * i), "%s%d" % (nm, 2 * i + 1)])

    def gen_convert_first():
        conv_win("K", OFF_K, 0)
        yield
        convert(kview(w_in, OFF_V, 512), KC, 512, ["V0_0", "V0_1"])
        yield

    def gen_diag_units():
        for c in range(KC):
            i = nxt("stg", 2)
            sbap, sbb = stg_bap[i], stg_bb[i]
            d3 = sbap[:, 0:2048].rearrange("p (a b) -> p a b", a=16)
            for t in range(16):
                A("activation", [B_const], [sbb], out=d3[:, t, :], in_=ident_f[:], func=AF.Copy,
                  scale=conv_w05[:, c, 2 * t:2 * t + 1])
            DMA2([sbb], [B_wsc[units["DG%d" % c]]], out=wsc[units["DG%d" % c]], in_=sbap[:, 0:2048])
            yield

    def rest_specs():
        sp = [(kview(w_in, OFF_K + 512, 512), KC, 512, ["K2", "K3"])]
        for h in range(1, H):
            sp.append((kview(w_in, OFF_V + h * 512, 512), KC, 512, ["V%d_0" % h, "V%d_1" % h]))
        for i in range(2):
            for nm, off in (("UG", OFF_UG), ("UV", OFF_UV)):
                sp.append((kview(w_in, off + i * 512, 512), KC, 512, ["%s%d" % (nm, 2 * i), "%s%d" % (nm, 2 * i + 1)]))
        for i in range(2):
            sp.append((kview(w_in, OFF_ZC + i * 512, 512), KC, 512, ["ZC%d" % (2 * i), "ZC%d" % (2 * i + 1)]))
        for hp in range(2):
            sp.append((kview(w_in, OFF_Q + hp * 512, 512), KC, 512, ["Q%d" % (2 * hp), "Q%d" % (2 * hp + 1)]))
        for h in range(H):
            sp.append((kview(w_in, OFF_Z + h * 512, 512), KC, 512, ["Z%d_0" % h, "Z%d_1" % h]))
        for nm, off in (("GA", OFF_GA), ("GB", OFF_GB)):
            for i in range(2):
                sp.append((kview(w_in, off + i * 512, 512), KC, 512, ["%s%d" % (nm, 2 * i), "%s%d" % (nm, 2 * i + 1)]))
        for i in range(2):
            sp.append((kview(w_co, i * 512, 512), KC, 512, ["CO%d" % (2 * i), "CO%d" % (2 * i + 1)]))
        for i in range(4):
            sp.append((kview(w_ro, i * 256, 256), 16, 256, ["RO%d" % (2 * i), "RO%d" % (2 * i + 1)]))
        for i in range(2):
            sp.append((kview(w_o, i * 512, 512), KC, 512, ["WO%d" % (2 * i), "WO%d" % (2 * i + 1)]))
        return sp

    def gen_convert_rest():
        sp = rest_specs()
        slots = [stage_load_issue(sp[0][0], sp[0][1])]
        for i in range(len(sp)):
            if i + 1 < len(sp):
                slots.append(stage_load_issue(sp[i + 1][0], sp[i + 1][1]))
            convert_finish(slots[i], sp[i][1], sp[i][2], sp[i][3])
            yield

    pmod, pmodb = pbank[4], pbuf[4]
    for blk in range(4):
        sb3, sbb = stage_load(kview(w_ada, blk * 512, 512), KC)
        for jj in range(4):
            j = blk * 4 + jj
            for kc in range(KC):
                PE("matmul", [sbb, B_c2], [pmodb], out=pmod[:, j:j + 1], lhsT=sb3[:, kc, jj * 128:(jj + 1) * 128],
                   rhs=cT_b[:, kc:kc + 1], start=(kc == 0), stop=(kc == KC - 1))
    V("tensor_tensor", [pmodb, B_const], [B_c2], out=shiftT[:], in0=pmod[:, 0:8], in1=b_adaT[:, 0:8], op=ALU.add)
    V("tensor_tensor", [pmodb, B_const], [B_c2], out=gs[:], in0=pmod[:, 8:16], in1=b_adaT[:, 8:16], op=ALU.add)
    V("scalar_tensor_tensor", [B_c2, B_const], [B_c2], out=gs[:], in0=gs[:], scalar=1.0, in1=pre_gT[:],
      op0=ALU.add, op1=ALU.mult)
    def gen_gate_row():
        for blk in range(2):
            sb3, sbb = stage_load(kview(w_ada, 2048 + blk * 512, 512), KC)
            pg, pgb = pring()
            for kc in range(KC):
                PE("matmul", [sbb, B_c2, bT_b[0], bT_b[1]], [pgb], out=pg[:], lhsT=cbc[:, kc, :], rhs=sb3[:, kc, :],
                   start=(kc == 0), stop=(kc == KC - 1))
            sl = slice(blk * 512, (blk + 1) * 512)
            V("tensor_tensor", [pgb, bgate_b], [bgate_b], out=bgate[:, sl], in0=pg[:], in1=bgate[:, sl], op=ALU.add)
            V("tensor_tensor", [bgate_b, B_const], [B_pgg], out=pgg[:, sl], in0=pgg[:, sl], in1=bgate[:, sl], op=ALU.mult)
            yield

    B_pgg = Buf("pgg")

    ac = stg_f[0]
    ac_cb = [Buf("ac%d" % c) for c in range(KC)]
    lnt = stg_f[1]
    lnt_b = Buf("lnt")
    lns = [lnt[:, (6 + i) * T:(7 + i) * T] for i in range(2)]
    lns_b = [Buf("lns%d" % i) for i in range(2)]
    szT3 = [stg_b0[:, i * 2048:(i + 1) * 2048].rearrange("p (a b) -> p a b", a=4) for i in range(2)]
    szT_b = [Buf("szT0"), Buf("szT1")]
    vv = [vvbuf[:, i * 2048:(i + 1) * 2048].rearrange("p (n e) -> p n e", n=NT) for i in range(2)]
    vv_b = [[Buf("vv%d_%d" % (i, n)) for n in range(NT)] for i in range(2)]
    rT_b = [Buf("rT%d" % i) for i in range(H)]
    hT = sb("hT", [128, KC, T], BF16)
    hT_b = [Buf("hT%d" % c) for c in range(KC)]
    aT = sb("aT", [128, KC, HALO + T], BF16)
    aT_b = [Buf("aT%d" % c) for c in range(KC)]
    mT3 = aT[:, :, HALO:HALO + T]
    NXS = 2
    xs = [sb("xs%d" % i, [128, D], F32) for i in range(NXS)]
    xs_b = [Buf("xs%d" % i) for i in range(NXS)]
    xn = [sb("xn%d" % i, [128, D], BF16) for i in range(4)]
    xn_b = [Buf("xn%d" % i) for i in range(4)]
    st_f = sb("st_f", [128, H * 2, 512], F32)
    st_bf = sb("st_bf", [128, H * 2, 512], BF16)
    st_fb = [Buf("stf%d" % i) for i in range(H * 2)]
    st_bb = [Buf("stb%d" % i) for i in range(H * 2)]
    cos_t = sb("cos_t", [128, T], F32)
    sin_t = sb("sin_t", [128, T], F32)
    tab_b = Buf("tab")
    rtmp = [sb("rtmp%d" % i, [128, T], F32) for i in range(4)]
    rtmp_b = [Buf("rtmp%d" % i) for i in range(4)]
    qT = [sb("qT%d" % i, [128, 2, T], BF16) for i in range(2)]
    kT = [sb("kT%d" % i, [128, 2, T], BF16) for i in range(2)]
    kz = [sb("kz%d" % i, [128, NT, 256], BF16) for i in range(2)]
    qT_b = [[Buf("qT%d_%d" % (i, d)) for d in range(2)] for i in range(2)]
    kT_b = [[Buf("kT%d_%d" % (i, d)) for d in range(2)] for i in range(2)]
    kz_b = [[Buf("kz%d_%d" % (i, n)) for n in range(NT)] for i in range(2)]
    PT = [sb("PT%d" % i, [128, 128], BF16) for i in range(6)]
    PT_b = [Buf("PT%d" % i) for i in range(6)]
    on = [sb("on%d" % i, [128, 512], BF16) for i in range(2)]
    on_b = [Buf("on%d" % i) for i in range(2)]
    sm = [sb("sm%d" % i, [128, 16], F32) for i in range(4)]
    smi = [sb("smi%d" % i, [128, 2], I32) for i in range(4)]
    sm_b = [Buf("sm%d" % i) for i in range(4)]
    NDIAG = 30
    diag = [sb("diag%d" % i, [128, 128], BF16) for i in range(NDIAG)]
    diag_b = [Buf("diag%d" % i) for i in range(NDIAG)]
    acb16 = [sb("acb%d" % i, [128, T], BF16) for i in range(2)]
    sqb16 = [sb("sqb%d" % i, [128, T], BF16) for i in range(2)]
    acb16_b = [Buf("acb%d" % i) for i in range(2)]
    sqb16_b = [Buf("sqb%d" % i) for i in range(2)]
    szc = [sb("szc%d" % i, [128, T], BF16) for i in range(2)]
    szc_b = [Buf("szc%d" % i) for i in range(2)]
    print("sbuf bytes remaining/partition:", nc.sbuf_bytes_remaining)

    bar_t = sb("bar_t", [128, 1], F32)

    def staging_barrier():
        bar_w = ac_cb + [lnt_b] + szT_b + lns_b + rT_b + [bT_b[0], bT_b[1]]
        V("memset", [], stg_fb + stg_bb + bar_w, ap=bar_t[:], constant=0.0)

    def passA_units(last):
        s = []
        for h in range(H):
            s += ["K%d" % h, "V%d_0" % h, "V%d_1" % h]
        if last:
            for i in range(4):
                s += ["UG%d" % i, "UV%d" % i]
        return s

    def main_units():
        s = []
        for i in range(4):
            s += ["UG%d" % i, "UV%d" % i]
        s += ["ZC%d" % i for i in range(4)]
        for h in range(H):
            s += ["Q%d" % h, "K%d" % h, "V%d_0" % h, "V%d_1" % h, "Z%d_0" % h, "Z%d_1" % h]
        for i in range(4):
            s += ["GA%d" % i, "GB%d" % i, "CO%d" % i, "RO%d" % (2 * i), "RO%d" % (2 * i + 1)]
        s += ["WO%d" % i for i in range(4)] * 2
        return s

    useq = []
    wst = {"cons": 0, "loaded": 0}
    LOOK = NSLOT - 1
    slot_free = list(range(NSLOT))
    slot_of = {}
    held = {}

    def _load(j):
        sl = slot_free.pop(0)
        slot_of[j] = sl
        DMA([B_wsc[units[useq[j]]]], [wslot_b[sl]], out=wslot[sl][:], in_=wsc[units[useq[j]]])

    def _prefetch():
        while (wst["loaded"] < len(useq) and slot_free and wst["loaded"] <= wst["cons"] + LOOK
               and (useq[wst["loaded"]] in converted or useq[wst["loaded"]].startswith("DG"))):
            _load(wst["loaded"])
            wst["loaded"] += 1

    def release(stream):
        if S.dry:
            return
        j = held.pop(stream, None)
        if j is not None:
            slot_free.append(slot_of.pop(j))
            _prefetch()

    def acquire(name, stream="M"):
        if S.dry:
            useq.append(name)
            return wslot[0], wslot_b[0]
        release(stream)
        i = wst["cons"]
        assert useq[i] == name, (i, useq[i], name)
        if i >= wst["loaded"]:
            assert wst["loaded"] == i and slot_free and (name in converted or name.startswith("DG")), name
            _load(i)
            wst["loaded"] += 1
        wst["cons"] += 1
        held[stream] = i
        _prefetch()
        sl = slot_of[i]
        return wslot[sl], wslot_b[sl]

    xseq = [("prev", t) for t in range(NBLK * NT)] + [("own", t) for t in range(NBLK * NT)]
    xst = {"cons": 0, "loaded": 0}

    def acquire_x():
        i = xst["cons"]
        xst["cons"] += 1
        while xst["loaded"] < len(xseq) and xst["loaded"] <= i + (NXS - 1):
            j = xst["loaded"]
            which, t = xseq[j]
            src = (x_prev if which == "prev" else x_own)[t * 128:(t + 1) * 128, :]
            DMA([], [xs_b[j % NXS]], out=xs[j % NXS][:], in_=src)
            xst["loaded"] += 1
        return xs[i % NXS], xs_b[i % NXS]

    def rsqrt(a_ap, y_ap, t_ap, i_ap, bufs, scalar=True):
        V("tensor_scalar", bufs, bufs, out=i_ap, in0=a_ap.bitcast(I32), scalar1=1, scalar2=None, op0=ALU.arith_shift_right)
        V("tensor_scalar", bufs, bufs, out=i_ap, in0=i_ap, scalar1=-1, scalar2=RSQRT_MAGIC, op0=ALU.mult, op1=ALU.add)
        cur = i_ap.bitcast(F32)
        for it in range(NEWTON_ITERS):
            if scalar:
                V("scalar_tensor_tensor", bufs, bufs, out=t_ap, in0=cur, scalar=a_ap, in1=cur, op0=ALU.mult, op1=ALU.mult)
            else:
                V("tensor_tensor", bufs, bufs, out=t_ap, in0=cur, in1=cur, op=ALU.mult)
                V("tensor_tensor", bufs, bufs, out=t_ap, in0=t_ap, in1=a_ap, op=ALU.mult)
            V("tensor_scalar", bufs, bufs, out=t_ap, in0=t_ap, scalar1=-0.5, scalar2=1.5, op0=ALU.mult, op1=ALU.add)
            V("tensor_tensor", bufs, bufs, out=y_ap, in0=cur, in1=t_ap, op=ALU.mult)
            cur = y_ap

    def pre_elem():
        for n in range(NT):
            xt, xtb = acquire_x()
            si = nxt("sm", 4)
            smt, smb = sm[si], sm_b[si]
            A("activation", [xtb], [xn_b[n], smb], out=xn[n][:], in_=xt[:], func=AF.Square, accum_out=smt[:, 0:1])
            V("tensor_scalar", [smb], [smb], out=smt[:, 1:2], in0=smt[:, 0:1], scalar1=1.0 / D, scalar2=RMS_EPS,
              op0=ALU.mult, op1=ALU.add)
            rsqrt(smt[:, 1:2], smt[:, 2:3], smt[:, 3:4], smi[si][:, 0:1], [smb])
            V("tensor_scalar", [xtb, smb], [xn_b[n]], out=xn[n][:], in0=xt[:], scalar1=smt[:, 2:3], scalar2=None,
              op0=ALU.mult)

    def pre_pe():
        for half in range(2):
            tl = [half * 2, half * 2 + 1]
            for kc2 in range(KC // 2):
                pt, ptb = ptring()
                for q in range(2):
                    kc = kc2 * 2 + q
                    for n2 in range(2):
                        PE("transpose", [xn_b[tl[n2]], B_c2], [ptb],
                           out=pt[:, (q * 2 + n2) * 128:(q * 2 + n2 + 1) * 128],
                           in_=xn[tl[n2]][:, kc * 128:(kc + 1) * 128], identity=ident_b[:])
                for q in range(2):
                    kc = kc2 * 2 + q
                    A("activation", [ptb, B_c2], [hT_b[kc]], out=hT[:, kc, half * 256:(half + 1) * 256],
                      in_=pt[:, q * 256:(q + 1) * 256], func=AF.Identity, bias=shiftT[:, kc:kc + 1], scale=gs[:, kc:kc + 1])

    def stage_tables(pos_ap, t0):
        tA, tA_b = rtmp[0], rtmp_b[0]
        tB, tB_b = rtmp[1], rtmp_b[1]
        posi, posi_b = rtmp[2][:].bitcast(I32), rtmp_b[2]
        DMA([], [posi_b], out=posi, in_=pos_ap[0:1, t0:t0 + T].to_broadcast([128, T]))
        V("tensor_copy", [posi_b], [tA_b], out=tA[:], in_=posi)
        V("tensor_scalar", [tA_b, B_const], [tA_b], out=tA[:], in0=tA[:], scalar1=invf[:, 0:1], scalar2=None, op0=ALU.mult)
        V("tensor_scalar", [tA_b], [posi_b], out=posi, in0=tA[:], scalar1=1.0 / TWO_PI, scalar2=None, op0=ALU.mult)
        V("tensor_copy", [posi_b], [tB_b], out=tB[:], in_=posi)
        V("scalar_tensor_tensor", [tA_b, tB_b], [tA_b], out=tA[:], in0=tB[:], scalar=-C1, in1=tA[:], op0=ALU.mult, op1=ALU.add)
        V("scalar_tensor_tensor", [tA_b, tB_b], [tA_b], out=tA[:], in0=tB[:], scalar=-C2, in1=tA[:], op0=ALU.mult, op1=ALU.add)
        V("tensor_scalar", [tA_b], [tA_b], out=tA[:], in0=tA[:], scalar1=PI_LO, scalar2=-PI_LO, op0=ALU.min, op1=ALU.max)
        A("activation", [tA_b], [tab_b], out=sin_t[:], in_=tA[:], func=AF.Sin)
        A("activation", [tA_b], [tB_b], out=tB[:], in_=tA[:], func=AF.Sin, scale=0.5)
        V("tensor_tensor", [tB_b], [tB_b], out=tB[:], in0=tB[:], in1=tB[:], op=ALU.mult)
        V("tensor_scalar", [tB_b], [tab_b], out=cos_t[:], in0=tB[:], scalar1=-2.0, scalar2=1.0, op0=ALU.mult, op1=ALU.add)

    def proj_fm(unit_t, col, nkc, rhs_fn, rhs_bufs, unit_buf):
        pb, pbb = pring()
        u3 = unit_t[:].rearrange("p (a b) -> p a b", a=nkc)
        for kc in range(nkc):
            PE("matmul", [unit_buf, rhs_bufs[kc]], [pbb], out=pb[:], lhsT=u3[:, kc, col:col + 128], rhs=rhs_fn(kc),
               start=(kc == 0), stop=(kc == nkc - 1))
        return pb, pbb

    def hT_rhs(kc):
        return hT[:, kc, :]

    def rotary(pa, pab, pb_, pbb_, dst, dstb, si):
        i0, i1 = nxt("rtmp", 4), nxt("rtmp", 4)
        V("tensor_tensor", [pab, tab_b], [rtmp_b[i0]], out=rtmp[i0][:], in0=pa[:], in1=cos_t[:], op=ALU.mult)
        V("tensor_tensor", [pbb_, tab_b], [rtmp_b[i1]], out=rtmp[i1][:], in0=pb_[:], in1=sin_t[:], op=ALU.mult)
        V("tensor_tensor", [rtmp_b[i0], rtmp_b[i1]], [dstb[si][0]], out=dst[si][:, 0, :], in0=rtmp[i0][:], in1=rtmp[i1][:],
          op=ALU.subtract)
        i2, i3 = nxt("rtmp", 4), nxt("rtmp", 4)
        V("tensor_tensor", [pab, tab_b], [rtmp_b[i2]], out=rtmp[i2][:], in0=pa[:], in1=sin_t[:], op=ALU.mult)
        V("tensor_tensor", [pbb_, tab_b], [rtmp_b[i3]], out=rtmp[i3][:], in0=pb_[:], in1=cos_t[:], op=ALU.mult)
        V("tensor_tensor", [rtmp_b[i2], rtmp_b[i3]], [dstb[si][1]], out=dst[si][:, 1, :], in0=rtmp[i2][:], in1=rtmp[i3][:],
          op=ALU.add)

    def head_kv(h, si, with_q, stream="M", c256=False):
        if with_q:
            u, ub = acquire("Q%d" % h, stream)
            pa, pab = proj_fm(u, 0, KC, hT_rhs, hT_b, ub)
            pb_, pbb_ = proj_fm(u, 128, KC, hT_rhs, hT_b, ub)
            rotary(pa, pab, pb_, pbb_, qT, qT_b, si)
            yield
        u, ub = acquire("K%d" % h, stream)
        pa, pab = proj_fm(u, 0, KC, hT_rhs, hT_b, ub)
        pb_, pbb_ = proj_fm(u, 128, KC, hT_rhs, hT_b, ub)
        rotary(pa, pab, pb_, pbb_, kT, kT_b, si)
        for n2 in range(2):
            pt, ptb = ptring()
            for nn in range(2):
                n = n2 * 2 + nn
                for dc in range(2):
                    PE("transpose", [kT_b[si][dc], B_c2], [ptb], out=pt[:, (nn * 2 + dc) * 128:(nn * 2 + dc + 1) * 128],
                       in_=kT[si][:, dc, n * 128:(n + 1) * 128], identity=ident_b[:])
            for nn in range(2):
                n = n2 * 2 + nn
                zsc = ext[:, h:h + 1] if (c256 and n % 2 == 0) else zs[:, h:h + 1]
                A("activation", [ptb, B_const], [kz_b[si][n]], out=kz[si][:, n, :], in_=pt[:, nn * 256:(nn + 1) * 256],
                  func=AF.Copy, scale=zsc)
        yield
        for half in range(2):
            u, ub = acquire("V%d_%d" % (h, half), stream)
            u3 = u[:].rearrange("p (a b) -> p a b", a=KC)
            for n in range(NT):
                pb, pbb = pring()
                for kc in range(KC):
                    PE("matmul", [ub, hT_b[kc]], [pbb], out=pb[:, 0:256], lhsT=hT[:, kc, n * 128:(n + 1) * 128],
                       rhs=u3[:, kc, :], start=(kc == 0), stop=(kc == KC - 1))
                dst = vv[si][:, n, half * 256:(half + 1) * 256]
                if (n + half) % 2 == 0:
                    A("activation", [pbb], [vv_b[si][n]], out=dst, in_=pb[:, 0:256], func=AF.Copy)
                else:
                    V("tensor_copy", [pbb], [vv_b[si][n]], out=dst, in_=pb[:, 0:256])
        yield

    def state_update(h, si, n, need_bf):
        for dc in range(2):
            pb, pbb = pring()
            PE("matmul", [kz_b[si][n], vv_b[si][n]], [pbb], out=pb[:], lhsT=kz[si][:, n, dc * 128:(dc + 1) * 128],
               rhs=vv[si][:, n, :], start=True, stop=True)
            j = h * 2 + dc
            V("scalar_tensor_tensor", [pbb, st_fb[j]], [st_fb[j]], out=st_f[:, j, :], in0=st_f[:, j, :], scalar=gC[h],
              in1=pb[:], op0=ALU.mult, op1=ALU.add)
            if need_bf:
                A("activation", [st_fb[j]], [st_bb[j]], out=st_bf[:, j, :], in_=st_f[:, j, :], func=AF.Copy)

    def stage_glu(stream="M"):
        for i in range(4):
            ug, ugb = acquire("UG%d" % i, stream)
            tg = []
            for q in range(2):
                pg, pgb = proj_fm(ug, q * 128, KC, hT_rhs, hT_b, ugb)
                li = nxt("lns", 2)
                A("activation", [pgb], [lns_b[li]], out=lns[li], in_=pg[:], func=AF.Tanh, scale=0.5)
                tg.append(li)
                yield
            uv, uvb = acquire("UV%d" % i, stream)
            for q in range(2):
                c = i * 2 + q
                pv, pvb = proj_fm(uv, q * 128, KC, hT_rhs, hT_b, uvb)
                li = tg[q]
                V("scalar_tensor_tensor", [pvb, lns_b[li]], [aT_b[c]], out=aT[:, c, HALO:HALO + T], in0=lns[li], scalar=1.0,
                  in1=pv[:], op0=ALU.add, op1=ALU.mult)
                yield
        release(stream)

    def halo_shift(use_flag):
        for c in range(KC):
            if use_flag:
                V("tensor_scalar", [aT_b[c], B_const], [aT_b[c]], out=aT[:, c, 0:HALO], in0=aT[:, c, T:T + HALO],
                  scalar1=flag[:, 0:1], scalar2=None, op0=ALU.mult)
            else:
                V("tensor_copy", [aT_b[c]], [aT_b[c]], out=aT[:, c, 0:HALO], in_=aT[:, c, T:T + HALO])

    def stage_conv(stream="M"):
        L = lambda i: lnt[:, i * T:(i + 1) * T]
        def gen_odd(c):
            tl = {}
            for k in range(1, CONVW, 2):
                di = nxt("diag", NDIAG)
                A("activation", [B_const], [diag_b[di]], out=diag[di][:], in_=ident_f[:], func=AF.Copy,
                  scale=conv_w05[:, c, k:k + 1])
                tl[k] = di
            return tl

        odd_next = gen_odd(0)
        pend_stats = None
        for c in range(KC):
            odd = odd_next
            if c + 1 < KC:
                odd_next = gen_odd(c + 1)
            dg, dgb = acquire("DG%d" % c, stream)
            dg3 = dg[:].rearrange("p (a b) -> p a b", a=16)
            pb, pbb = pring()
            for k in range(CONVW):
                if k % 2 == 0:
                    lhs, lb_ = dg3[:, k // 2, :], dgb
                else:
                    lhs, lb_ = diag[odd[k]][:], diag_b[odd[k]]
                PE("matmul", [lb_, aT_b[c]], [pbb], out=pb[:], lhsT=lhs, rhs=aT[:, c, k:k + T],
                   start=(k == 0), stop=(k == CONVW - 1))
            acs = ac[:, c * T:(c + 1) * T]
            A("activation", [pbb, B_const], [ac_cb[c]], out=acs, in_=pb[:], func=AF.Identity, bias=conv_bT[:, c:c + 1])
            ai = nxt("acb", 2)
            A("activation", [ac_cb[c]], [acb16_b[ai]], out=acb16[ai][:], in_=acs, func=AF.Copy)
            A("activation", [ac_cb[c]], [sqb16_b[ai]], out=sqb16[ai][:], in_=acs, func=AF.Square)
            def stats(c=c, ai=ai):
                ps_, psb_ = pring()
                PE("matmul", [acb16_b[ai], B_c2], [psb_], out=ps_[:], lhsT=ones_b[:], rhs=acb16[ai][:], start=True, stop=True)
                pq_, pqb_ = pring()
                PE("matmul", [sqb16_b[ai], B_c2], [pqb_], out=pq_[:], lhsT=ones_b[:], rhs=sqb16[ai][:], start=True, stop=True)
                if c == 0:
                    V("tensor_copy", [psb_], [lnt_b], out=L(0), in_=ps_[:])
                    V("tensor_copy", [pqb_], [lnt_b], out=L(1), in_=pq_[:])
                else:
                    V("tensor_tensor", [psb_, lnt_b], [lnt_b], out=L(0), in0=L(0), in1=ps_[:], op=ALU.add)
                    V("tensor_tensor", [pqb_, lnt_b], [lnt_b], out=L(1), in0=L(1), in1=pq_[:], op=ALU.add)
            if pend_stats is not None:
                pend_stats()
            pend_stats = stats
            yield
        pend_stats()
        mean, var, rstd, tmp, mr = L(0), L(1), L(2), L(3), L(4)
        lni = L(5).bitcast(I32)
        lb = [lnt_b]
        V("tensor_scalar", lb, lb, out=mean, in0=mean, scalar1=1.0 / D, scalar2=None, op0=ALU.mult)
        V("tensor_scalar", lb, lb, out=var, in0=var, scalar1=1.0 / D, scalar2=LN_EPS, op0=ALU.mult, op1=ALU.add)
        V("tensor_tensor", lb, lb, out=tmp, in0=mean, in1=mean, op=ALU.mult)
        V("tensor_tensor", lb, lb, out=var, in0=var, in1=tmp, op=ALU.subtract)
        rsqrt(var, rstd, tmp, lni, lb, scalar=False)
        V("tensor_tensor", lb, lb, out=mr, in0=mean, in1=rstd, op=ALU.mult)
        for i in range(4):
            zc, zcb = acquire("ZC%d" % i, stream)
            for q in range(2):
                c = i * 2 + q
                acs = ac[:, c * T:(c + 1) * T]
                pz, pzb = proj_fm(zc, q * 128, KC, hT_rhs, hT_b, zcb)
                zi = nxt("szc", 2)
                A("activation", [pzb], [szc_b[zi]], out=szc[zi][:], in_=pz[:], func=AF.Silu)
                li = nxt("lns", 2)
                V("tensor_tensor", [ac_cb[c], lnt_b], [lns_b[li]], out=lns[li], in0=acs, in1=rstd, op=ALU.mult)
                V("tensor_tensor", [lns_b[li], lnt_b], [lns_b[li]], out=lns[li], in0=lns[li], in1=mr, op=ALU.subtract)
                A("activation", [lns_b[li], B_const], [lns_b[li]], out=lns[li], in_=lns[li], func=AF.Silu,
                  bias=ln_bT[:, c:c + 1], scale=ln_gT[:, c:c + 1])
                V("tensor_tensor", [lns_b[li], szc_b[zi]], [bT_b[c]], out=bT[:, c, :], in0=lns[li], in1=szc[zi][:], op=ALU.mult)
                yield
        release(stream)

    def head_proj(h, si, stream="M"):
        for _ in head_kv(h, si, True, stream, True):
            yield
        for half in range(2):
            u, ub = acquire("Z%d_%d" % (h, half), stream)
            for q in range(2):
                ech = half * 2 + q
                pz, pzb = proj_fm(u, q * 128, KC, hT_rhs, hT_b, ub)
                A("activation", [pzb], [szT_b[si]], out=szT3[si][:, ech, :], in_=pz[:], func=AF.Silu)
        yield

    def ret_scores(h, si):
        out = []
        for cp in range(NT // 2):
            ta = slice((2 * cp) * 128, (2 * cp + 1) * 128)
            tb_ = slice((2 * cp + 1) * 128, (2 * cp + 2) * 128)
            trip = []
            for kind, (tj, ti) in enumerate(((ta, ta), (ta, tb_), (tb_, tb_))):
                pS, pSb = pring()
                for dc in range(2):
                    PE("matmul", [kT_b[si][dc], qT_b[si][dc]], [pSb], out=pS[:, 0:128], lhsT=kT[si][:, dc, tj],
                       rhs=qT[si][:, dc, ti], start=(dc == 0), stop=(dc == 1))
                pi = nxt("PT", 6)
                if kind == 0:
                    V("tensor_tensor", [pSb, B_const], [PT_b[pi]], out=PT[pi][:], in0=pS[:, 0:128], in1=maskT[:, h, :],
                      op=ALU.mult)
                elif kind == 1:
                    V("tensor_scalar", [pSb, B_const], [PT_b[pi]], out=PT[pi][:], in0=pS[:, 0:128],
                      scalar1=ext[:, 4 + h:5 + h], scalar2=None, op0=ALU.mult)
                else:
                    V("scalar_tensor_tensor", [pSb, B_const], [PT_b[pi]], out=PT[pi][:], in0=pS[:, 0:128], scalar=gm128[h],
                      in1=maskT[:, h, :], op0=ALU.mult, op1=ALU.mult)
                trip.append(pi)
            out.append(trip)
        return out

    def ret_chunk(h, si, n, trip):
        tok = slice(n * 128, (n + 1) * 128)
        second = (n % 2 == 1)
        oi_ = nxt("pO", 2)
        pO, pOb = pbank[4 + oi_], pbuf[4 + oi_]
        for dc in range(2):
            j = h * 2 + dc
            PE("matmul", [qT_b[si][dc], st_bb[j]], [pOb], out=pO[:], lhsT=qT[si][:, dc, tok], rhs=st_bf[:, j, :],
               start=(dc == 0), stop=False)
        if not second:
            PE("matmul", [PT_b[trip[0]], vv_b[si][n]], [pOb], out=pO[:], lhsT=PT[trip[0]][:], rhs=vv[si][:, n, :],
               start=False, stop=True)
        else:
            PE("matmul", [PT_b[trip[1]], vv_b[si][n - 1]], [pOb], out=pO[:], lhsT=PT[trip[1]][:], rhs=vv[si][:, n - 1, :],
               start=False, stop=False)
            PE("matmul", [PT_b[trip[2]], vv_b[si][n]], [pOb], out=pO[:], lhsT=PT[trip[2]][:], rhs=vv[si][:, n, :],
               start=False, stop=True)
            for dc in range(2):
                pb, pbb = pring()
                for q_, nn in enumerate((n - 1, n)):
                    PE("matmul", [kz_b[si][nn], vv_b[si][nn]], [pbb], out=pb[:], lhsT=kz[si][:, nn, dc * 128:(dc + 1) * 128],
                       rhs=vv[si][:, nn, :], start=(q_ == 0), stop=(q_ == 1))
                j = h * 2 + dc
                V("scalar_tensor_tensor", [pbb, st_fb[j]], [st_fb[j]], out=st_f[:, j, :], in0=st_f[:, j, :], scalar=gC2[h],
                  in1=pb[:], op0=ALU.mult, op1=ALU.add)
                A("activation", [st_fb[j]], [st_bb[j]], out=st_bf[:, j, :], in_=st_f[:, j, :], func=AF.Copy)
        mi = nxt("sm", 4)
        smt, smb = sm[mi], sm_b[mi]
        V("bn_stats", [pOb], [smb], out=smt[:, 0:6], in_=pO[:])
        V("bn_aggr", [smb], [smb], out=smt[:, 6:8], in_=smt[:, 0:6])
        eps_ap = ext[:, 8 + h:9 + h] if second else epsp[:, h:h + 1]
        V("tensor_tensor", [smb, B_const], [smb], out=smt[:, 8:9], in0=smt[:, 7:8], in1=eps_ap, op=ALU.add)
        rsqrt(smt[:, 8:9], smt[:, 9:10], smt[:, 10:11], smi[mi][:, 0:1], [smb])
        V("scalar_tensor_tensor", [smb], [smb], out=smt[:, 11:12], in0=smt[:, 6:7], scalar=-1.0, in1=smt[:, 9:10],
          op0=ALU.mult, op1=ALU.mult)
        oi = nxt("on", 2)
        A("activation", [pOb, smb], [on_b[oi]], out=on[oi][:], in_=pO[:], func=AF.Identity, bias=smt[:, 11:12],
          scale=smt[:, 9:10])

        def tail():
            pt, ptb = ptring()
            for ech in range(4):
                PE("transpose", [on_b[oi], B_c2], [ptb], out=pt[:, ech * 128:(ech + 1) * 128],
                   in_=on[oi][:, ech * 128:(ech + 1) * 128], identity=ident_b[:])
            V("tensor_tensor", [ptb, szT_b[si]], [rT_b[h]], out=rT[:, h * 4:(h + 1) * 4, tok],
              in0=pt.rearrange("p (a b) -> p a b", a=4), in1=szT3[si][:, :, tok], op=ALU.mult)
        return tail

    def stage_ret(stream="M"):
        g = head_proj(0, 0, stream)
        for _ in g:
            yield
        pending = None
        for h in range(H):
            gn = head_proj(h + 1, (h + 1) % 2, stream) if h + 1 < H else None
            pis = ret_scores(h, h % 2)
            for n in range(NT):
                if gn is not None:
                    next(gn, None)
                    yield
                t = ret_chunk(h, h % 2, n, pis[n // 2])
                if pending is not None:
                    pending()
                pending = t
                yield
            if gn is not None:
                for _ in gn:
                    yield
        pending()
        release(stream)

    def run(g):
        for _ in g:
            pass

    def interleave(ga, gb_):
        a_live, b_live = True, True
        while a_live or b_live:
            if a_live:
                try:
                    next(ga)
                except StopIteration:
                    a_live = False
            if b_live:
                try:
                    next(gb_)
                except StopIteration:
                    b_live = False

    def conv_stream():
        for _ in stage_glu("X"):
            yield
        for _ in stage_conv("X"):
            yield
        halo_shift(False)

    def stage_merge():
        for i in range(4):
            ga, gab = acquire("GA%d" % i)
            ta = []
            for q in range(2):
                pg, pgb = proj_fm(ga, q * 128, KC, hT_rhs, hT_b, gab)
                li = nxt("rtmp", 4)
                A("activation", [pgb], [rtmp_b[li]], out=rtmp[li][:], in_=pg[:], func=AF.Tanh, scale=0.5)
                ta.append(li)
            gb_, gbb = acquire("GB%d" % i)
            tb = []
            for q in range(2):
                pg, pgb = proj_fm(gb_, q * 128, KC, hT_rhs, hT_b, gbb)
                li = nxt("rtmp", 4)
                A("activation", [pgb], [rtmp_b[li]], out=rtmp[li][:], in_=pg[:], func=AF.Tanh, scale=0.5)
                tb.append(li)
            co, cob = acquire("CO%d" % i)
            for q in range(2):
                py, pyb = proj_fm(co, q * 128, KC, lambda kc: bT[:, kc, :], bT_b, cob)
                li = tb[q]
                V("scalar_tensor_tensor", [pyb, rtmp_b[li]], [rtmp_b[li]], out=rtmp[li][:], in0=rtmp[li][:], scalar=1.0,
                  in1=py[:], op0=ALU.add, op1=ALU.mult)
            for q in range(2):
                dch = i * 2 + q
                ro, rob = acquire("RO%d" % dch)
                py, pyb = proj_fm(ro, 0, 16, lambda ec: rT[:, ec, :], [rT_b[ec // 4] for ec in range(16)], rob)
                la, lb_ = ta[q], tb[q]
                V("scalar_tensor_tensor", [pyb, rtmp_b[la]], [rtmp_b[la]], out=rtmp[la][:], in0=rtmp[la][:], scalar=1.0,
                  in1=py[:], op0=ALU.add, op1=ALU.mult)
                V("tensor_tensor", [rtmp_b[la], rtmp_b[lb_]], [aT_b[dch]], out=mT3[:, dch, :], in0=rtmp[la][:],
                  in1=rtmp[lb_][:], op=ALU.add)

    B_out = Buf("outd")

    def stage_out(blk, between=None):
        for tp in range(2):
            if tp == 1 and between is not None:
                between()
            tiles = (tp * 2, tp * 2 + 1)
            banks = {n: (pring(), pring()) for n in tiles}
            for ui in range(4):
                u, ub = acquire("WO%d" % ui)
                u3 = u[:].rearrange("p (a b) -> p a b", a=KC)
                cs = (ui % 2) * 256
                for n in tiles:
                    pb, pbb = banks[n][ui // 2]
                    for kc in range(KC):
                        PE("matmul", [ub, aT_b[kc]], [pbb], out=pb[:, cs:cs + 256], lhsT=mT3[:, kc, n * 128:(n + 1) * 128],
                           rhs=u3[:, kc, :], start=(kc == 0), stop=(kc == KC - 1))
            for n in tiles:
                (pb0, pbb0), (pb1, pbb1) = banks[n]
                t = blk * NT + n
                xi_ = t % 2
                DMA2([], [xr_b[xi_]], out=xr[xi_][:], in_=x_own[t * 128:(t + 1) * 128, :])
                mi = nxt("sm", 4)
                smt, smb = sm[mi], sm_b[mi]
                j0, j1 = nxt("rtmp", 4), nxt("rtmp", 4)
                A("activation", [pbb0], [rtmp_b[j0], smb], out=rtmp[j0][:], in_=pb0[:], func=AF.Square, accum_out=smt[:, 0:1])
                A("activation", [pbb1], [rtmp_b[j1], smb], out=rtmp[j1][:], in_=pb1[:], func=AF.Square, accum_out=smt[:, 1:2])
                V("tensor_tensor", [smb], [smb], out=smt[:, 2:3], in0=smt[:, 0:1], in1=smt[:, 1:2], op=ALU.add)
                V("tensor_scalar", [smb], [smb], out=smt[:, 3:4], in0=smt[:, 2:3], scalar1=1.0 / D, scalar2=4.0 * RMS_EPS,
                  op0=ALU.mult, op1=ALU.add)
                rsqrt(smt[:, 3:4], smt[:, 4:5], smt[:, 5:6], smi[mi][:, 0:1], [smb])
                for hf, (pb, pbb) in enumerate(((pb0, pbb0), (pb1, pbb1))):
                    cs = slice(hf * 512, (hf + 1) * 512)
                    li = nxt("lns", 2)
                    V("scalar_tensor_tensor", [pbb, smb, B_pgg], [lns_b[li]], out=lns[li], in0=pb[:], scalar=smt[:, 4:5],
                      in1=pgg[:, cs], op0=ALU.mult, op1=ALU.mult)
                    V("tensor_tensor", [lns_b[li], xr_b[xi_]], [xr_b[xi_]], out=xr[xi_][:, cs], in0=xr[xi_][:, cs],
                      in1=lns[li], op=ALU.add)
                DMA2([xr_b[xi_]], [B_out], out=out_d[t * 128:(t + 1) * 128, :], in_=xr[xi_][:])

    def body():
        for j in range(H * 2):
            V("memset", [], [st_fb[j]], ap=st_f[:, j, :], constant=0.0)
        for c in range(KC):
            V("memset", [], [aT_b[c]], ap=aT[:, c, :], constant=0.0)
        cast_mode["pool_only"] = False
        run(gen_convert_first())
        gc = gen_convert_rest()
        cast_mode["pool_only"] = True
        gd = gen_diag_units()
        if NBLK == 1:
            run(gc)
            run(gd)
            run(gen_gate_row())
            staging_barrier()
        pre_elem()
        pre_pe()
        stage_tables(pos_prev, 0)
        for blk in range(NBLK):
            run(head_kv(0, 0, False))
            for h in range(H):
                si = h % 2
                if NBLK > 1 and blk < NBLK - 1:
                    next(gc, None)
                    next(gc, None)
                if h + 1 < H:
                    run(head_kv(h + 1, (h + 1) % 2, False))
                if h == 1:
                    pre_elem()
                if h == 2 and blk + 1 < NBLK:
                    pre_pe()
                    stage_tables(pos_prev, (blk + 1) * T)
                for n in range(NT):
                    state_update(h, si, n, False)
            if NBLK > 1 and blk == NBLK - 2:
                run(gc)
                run(gd)
                run(gen_gate_row())
                staging_barrier()
            if blk == NBLK - 1:
                run(stage_glu())
                halo_shift(True)
        for j in range(H * 2):
            V("tensor_scalar", [st_fb[j], B_const], [st_fb[j]], out=st_f[:, j, :], in0=st_f[:, j, :], scalar1=flag[:, 0:1],
              scalar2=None, op0=ALU.mult)
            A("activation", [st_fb[j]], [st_bb[j]], out=st_bf[:, j, :], in_=st_f[:, j, :], func=AF.Copy)
        pre_pe()
        stage_tables(pos_own, 0)
        for blk in range(NBLK):
            release("M")
            interleave(conv_stream(), stage_ret("Y"))
            if blk + 1 < NBLK:
                pre_elem()
            stage_merge()
            if blk + 1 < NBLK:
                def nxt_blk(b=blk + 1):
                    pre_pe()
                    stage_tables(pos_own, b * T)
                stage_out(blk, nxt_blk)
            else:
                stage_out(blk)

    rr_save = dict(rr)
    S.dry = True
    body()
    S.dry = False
    converted.clear()
    rr.clear()
    rr.update(rr_save)
    xst["cons"] = 0
    xst["loaded"] = 0
    body()
    assert wst["cons"] == len(useq), (wst["cons"], len(useq))

    with ExitStack() as stack:
        nops, nwait = S.emit(stack)
    print("ops", nops, "waits", nwait)
    return nc, nops, nwait


_PROG_CACHE = {}


def kernel(x, c, positions, w_ada, b_ada, pre_norm_g, w_in, conv_w, conv_b, conv_ln_g, conv_ln_b,
           w_ret_out, w_conv_out, w_out, post_norm_g):
    x = np.asarray(x, dtype=np.float32)
    B, S_, _ = x.shape
    half = S_ // 2
    NBLK = half // T
    assert half % T == 0
    ncores = 2 * B
    if NBLK not in _PROG_CACHE:
        _PROG_CACHE[NBLK] = build_program(NBLK)[0]
    nc = _PROG_CACHE[NBLK]
    maskT, zs, epsp, gC, inv_freq, ext = _host_consts()
    c = np.asarray(c, np.float32)
    positions = np.asarray(positions, np.int32)
    f = lambda a: np.ascontiguousarray(np.asarray(a, np.float32))
    lay = lambda v, n: f(np.asarray(v, np.float32).reshape(n, 128).T)
    b_ada0 = np.asarray(b_ada, np.float32)[0]
    shared = {
        "w_ada": f(w_ada[0]), "b_adaT": lay(b_ada0[:2048], 16), "b_gate": f(b_ada0[2048:].reshape(1, D)),
        "pre_gT": lay(pre_norm_g[0], KC), "w_in": f(w_in[0]),
        "conv_wT": f(np.asarray(conv_w[0], np.float32).T.reshape(KC, 128, CONVW).transpose(1, 0, 2).reshape(128, KC * CONVW)),
        "conv_bT": lay(conv_b[0], KC), "ln_gT": lay(conv_ln_g[0], KC), "ln_bT": lay(conv_ln_b[0], KC),
        "w_ret_out": f(w_ret_out[0]), "w_conv_out": f(w_conv_out[0]), "w_out": f(w_out[0]),
        "post_g": f(np.asarray(post_norm_g[0], np.float32).reshape(1, D)),
        "ident": np.eye(128, dtype=np.float32), "maskT": f(maskT.reshape(128, H * 128)), "zs": f(zs), "epsp": f(epsp),
        "inv_freq": f(inv_freq), "ext": f(ext),
    }
    in_maps = []
    for b in range(B):
        for j in range(2):
            own = x[b, j * half:(j + 1) * half]
            prev = x[b, 0:half]
            m = dict(shared)
            m["x_own"] = np.ascontiguousarray(own)
            m["x_prev"] = np.ascontiguousarray(prev)
            m["pos_own"] = np.ascontiguousarray(positions[b, j * half:(j + 1) * half].reshape(1, half))
            m["pos_prev"] = np.ascontiguousarray(positions[b, 0:half].reshape(1, half))
            m["flag"] = np.full((128, 1), float(j), np.float32)
            m["cT"] = lay(c[b], KC)
            in_maps.append(m)
    res = run_bass_kernel_spmd(nc, in_maps, core_ids=list(range(ncores)))
    out = np.empty((B, S_, D), np.float32)
    for b in range(B):
        for j in range(2):
            out[b, j * half:(j + 1) * half] = res.results[b * 2 + j]["out"]
    return out
```
